# Optimizing a Trainium2 kernel written in Bass

```python
import math
import jax, jax.numpy as jnp
from jax import lax
import numpy as np

D_MODEL = 1024
BATCH = 8
SEQ = 8192
DEPTH = 1
DEC_BATCH = 32
DEC_SEQ = 32
PAST_LEN = 2048

CHUNK = 64
C_CONV = D_MODEL // 2
CONV_K = 31
C_SGU = D_MODEL // 2
SGU_GROUPS = 4
SGU_GROUP_DIM = C_SGU // SGU_GROUPS
SGU_CHUNK = 128
D_FF = int(math.ceil((8 * D_MODEL / 3) / 256) * 256)
N_IN = 2 * C_CONV + 2 * C_SGU + 2 * D_MODEL
EPS = 1e-6

kernel_name = "gated_conv_sgu_streaming_encoder_step"


def rmsnorm(x, g):
    xf = x.astype(jnp.float32)
    r = xf * lax.rsqrt(jnp.mean(xf * xf, axis=-1, keepdims=True) + EPS)
    return (r * g.astype(jnp.float32)).astype(x.dtype)


def layernorm(x, g, b):
    xf = x.astype(jnp.float32)
    mu = jnp.mean(xf, axis=-1, keepdims=True)
    var = jnp.mean(jnp.square(xf - mu), axis=-1, keepdims=True)
    r = (xf - mu) * lax.rsqrt(var + EPS)
    return (r * g.astype(jnp.float32) + b.astype(jnp.float32)).astype(x.dtype)


def depthwise_causal(xc, w, b):
    y = lax.conv_general_dilated(xc, w[:, None, :].astype(xc.dtype), window_strides=(1,),
                                 padding='VALID', dimension_numbers=('NWC', 'WIO', 'NWC'),
                                 feature_group_count=xc.shape[-1])
    return y + b


def spatial_gating(vn, w_s, b_s):
    B, T, _ = vn.shape
    if T >= SGU_CHUNK:
        n_chunks, lc = T // SGU_CHUNK, SGU_CHUNK
    else:
        n_chunks, lc = 1, T
    mask = jnp.tril(jnp.ones((SGU_CHUNK, SGU_CHUNK), w_s.dtype))
    ws = (w_s * mask)[:, :lc, :lc]
    bs = b_s[:, :lc]
    vr = vn.reshape(B, n_chunks, lc, SGU_GROUPS, SGU_GROUP_DIM)
    out = jnp.einsum('gts,bnsgc->bntgc', ws, vr) + bs.T[None, None, :, :, None]
    return out.reshape(B, T, C_SGU)


def layer(x, conv_hist, g_mix, w_in, b_in, w_dw, b_dw, ln_a_g, ln_a_b, w_a_out, b_a_out,
          ln_v_g, ln_v_b, w_s, b_s, w_b_out, w_o, g_ffn, w_ffn_in, w_ffn_out):
    h = rmsnorm(x, g_mix)
    z = h @ w_in + b_in
    a_lin = z[..., :C_CONV]
    a_gate = z[..., C_CONV:2 * C_CONV]
    o = 2 * C_CONV
    u = z[..., o:o + C_SGU]
    v = z[..., o + C_SGU:o + 2 * C_SGU]
    o = o + 2 * C_SGU
    g_a = jax.nn.sigmoid(z[..., o:o + D_MODEL])
    g_b = jax.nn.sigmoid(z[..., o + D_MODEL:o + 2 * D_MODEL])

    glu = a_lin * jax.nn.sigmoid(a_gate)
    xc = jnp.concatenate([conv_hist.astype(glu.dtype), glu], axis=1)
    new_conv = xc[:, -(CONV_K - 1):]
    yc = depthwise_causal(xc, w_dw, b_dw)
    y_a = jax.nn.silu(layernorm(yc, ln_a_g, ln_a_b)) @ w_a_out + b_a_out

    vn = layernorm(v, ln_v_g, ln_v_b)
    y_b = (u * spatial_gating(vn, w_s, b_s)) @ w_b_out

    x = x + (g_a * y_a + g_b * y_b) @ w_o

    h2 = rmsnorm(x, g_ffn)
    gu = h2 @ w_ffn_in
    x = x + (jax.nn.silu(gu[..., :D_FF]) * gu[..., D_FF:]) @ w_ffn_out
    return x, new_conv, vn


def setup_inputs(seed: int = 0) -> dict:
    key = jax.random.key(seed)
    ks = jax.random.split(key, 24)
    f32 = jnp.float32
    nrm = lambda k, s, sc: jax.random.normal(k, s, f32) * sc
    return {
        "x_prompt": nrm(ks[0], (BATCH, SEQ, D_MODEL), 1.0),
        "x_sample": nrm(ks[1], (DEC_BATCH, DEC_SEQ, D_MODEL), 1.0),
        "cache_conv": nrm(ks[2], (DEPTH, DEC_BATCH, CONV_K - 1, C_CONV), 0.5),
        "g_mix": 1.0 + nrm(ks[3], (DEPTH, D_MODEL), 0.02),
        "w_in": nrm(ks[4], (DEPTH, D_MODEL, N_IN), D_MODEL ** -0.5),
        "b_in": nrm(ks[5], (DEPTH, N_IN), 0.02),
        "w_dw": nrm(ks[6], (DEPTH, CONV_K, C_CONV), CONV_K ** -0.5),
        "b_dw": nrm(ks[7], (DEPTH, C_CONV), 0.02),
        "ln_a_g": 1.0 + nrm(ks[8], (DEPTH, C_CONV), 0.02),
        "ln_a_b": nrm(ks[9], (DEPTH, C_CONV), 0.02),
        "w_a_out": nrm(ks[10], (DEPTH, C_CONV, D_MODEL), C_CONV ** -0.5),
        "b_a_out": nrm(ks[11], (DEPTH, D_MODEL), 0.02),
        "ln_v_g": 1.0 + nrm(ks[12], (DEPTH, C_SGU), 0.02),
        "ln_v_b": nrm(ks[13], (DEPTH, C_SGU), 0.02),
        "w_s": nrm(ks[14], (DEPTH, SGU_GROUPS, SGU_CHUNK, SGU_CHUNK), SGU_CHUNK ** -0.5),
        "b_s": 1.0 + nrm(ks[15], (DEPTH, SGU_GROUPS, SGU_CHUNK), 0.1),
        "w_b_out": nrm(ks[16], (DEPTH, C_SGU, D_MODEL), C_SGU ** -0.5),
        "w_o": nrm(ks[17], (DEPTH, D_MODEL, D_MODEL), 0.5 * D_MODEL ** -0.5),
        "g_ffn": 1.0 + nrm(ks[18], (DEPTH, D_MODEL), 0.02),
        "w_ffn_in": nrm(ks[19], (DEPTH, D_MODEL, 2 * D_FF), D_MODEL ** -0.5),
        "w_ffn_out": nrm(ks[20], (DEPTH, D_FF, D_MODEL), 0.5 * D_FF ** -0.5),
        "g_final": 1.0 + nrm(ks[21], (D_MODEL,), 0.02),
    }


def reference(x_prompt, x_sample, cache_conv, g_mix, w_in, b_in, w_dw, b_dw, ln_a_g, ln_a_b,
              w_a_out, b_a_out, ln_v_g, ln_v_b, w_s, b_s, w_b_out, w_o, g_ffn, w_ffn_in,
              w_ffn_out, g_final):
    xp, xs = x_prompt, x_sample
    conv_p, conv_s, sgu_s = [], [], []
    for i in range(DEPTH):
        p = (g_mix[i], w_in[i], b_in[i], w_dw[i], b_dw[i], ln_a_g[i], ln_a_b[i], w_a_out[i],
             b_a_out[i], ln_v_g[i], ln_v_b[i], w_s[i], b_s[i], w_b_out[i], w_o[i], g_ffn[i],
             w_ffn_in[i], w_ffn_out[i])
        hist0 = jnp.zeros((xp.shape[0], CONV_K - 1, C_CONV), xp.dtype)
        xp, cp, _ = layer(xp, hist0, *p)
        xs, cs, vs = layer(xs, cache_conv[i], *p)
        conv_p.append(cp)
        conv_s.append(cs)
        sgu_s.append(vs)
    y_prompt = rmsnorm(xp, g_final)
    y_sample = rmsnorm(xs, g_final)
    state_conv_prompt = jnp.stack(conv_p, axis=0)
    state_conv_sample = jnp.stack(conv_s, axis=0)
    state_sgu_v_sample = jnp.stack(sgu_s, axis=0)
    return (y_prompt, y_sample, state_conv_prompt, state_conv_sample, state_sgu_v_sample)
```

```python
import numpy as np
from contextlib import ExitStack

import concourse.bass as bass
import concourse.mybir as mybir
from concourse.bass_utils import run_bass_kernel_spmd

F32 = mybir.dt.float32
BF16 = mybir.dt.bfloat16
AF = mybir.ActivationFunctionType
ALU = mybir.AluOpType

D = 1024
SEQ = 8192
NCORE = 8
CC = 512
CS = 512
KW = 31
DFF = 2816
NJ = DFF // 128
EPS = 1e-6
TP = 512
NPASS_PROMPT = SEQ // TP


class Buf:
    __slots__ = ("name", "w", "r")

    def __init__(self, name):
        self.name = name
        self.w = None
        self.r = []


class Sched:
    ENG = ("pe", "act", "dve", "pool", "sp")

    def __init__(self):
        self.ops = {e: [] for e in self.ENG}
        self.count = {e: 0 for e in self.ENG}
        self.waited = {e: {} for e in self.ENG}
        self.dcount = {}
        self.sem = {}
        self.pending_pe = False

    def add_dma_sem(self, key):
        self.dcount[key] = 0

    def _resolve(self, tok):
        sk, v = tok
        if sk in self.dcount:
            return sk, self.dcount[sk]
        return sk, v

    def op(self, eng, fn, reads=(), writes=(), signal=True, dma=None, ndma=1):
        deps = {}
        wset = set(id(b) for b in writes)
        for b in reads:
            if b.w is not None:
                sk, v = self._resolve(b.w)
                deps[sk] = max(deps.get(sk, 0), v)
        for b in writes:
            if b.w is not None:
                sk, v = self._resolve(b.w)
                deps[sk] = max(deps.get(sk, 0), v)
            for t in b.r:
                sk, v = self._resolve(t)
                deps[sk] = max(deps.get(sk, 0), v)
        waits = []
        wd = self.waited[eng]
        for sk, v in deps.items():
            if eng == "pe" and sk == "pe":
                continue
            if wd.get(sk, 0) >= v:
                continue
            wd[sk] = v
            waits.append((sk, v))
        if dma is not None:
            self.dcount[dma] += 16 * ndma
            tok = (dma, self.dcount[dma])
            inc = (dma, 16)
        elif signal:
            self.count[eng] += 1
            tok = (eng, self.count[eng])
            inc = (eng, 1)
            if eng == "pe":
                self.pending_pe = False
        else:
            tok = (eng, self.count[eng] + 1)
            inc = None
            if eng == "pe":
                self.pending_pe = True
        self.ops[eng].append((waits, fn, inc, ndma if dma is not None else 1))
        for b in writes:
            b.w = tok
            b.r = []
        for b in reads:
            if id(b) not in wset:
                b.r.append(tok)
        return tok

    def wait_all(self, eng, keys):
        waits = []
        for sk in keys:
            v = self.dcount[sk] if sk in self.dcount else self.count[sk]
            if v > 0:
                waits.append((sk, v))
        self.ops[eng].append((waits, None, None, 1))

    def emit(self, block):
        assert not self.pending_pe
        names = {"pe": "tensor", "act": "scalar", "dve": "vector", "pool": "gpsimd", "sp": "sync"}
        for eng in self.ENG:
            ops = self.ops[eng]
            if not ops:
                continue

            def body(e, ops=ops):
                for waits, fn, inc, _n in ops:
                    for sk, v in waits:
                        e.wait_ge(self.sem[sk], v)
                    if fn is None:
                        continue
                    ins = fn(e)
                    if inc is not None:
                        if isinstance(ins, (list, tuple)):
                            for i_ in ins:
                                i_.then_inc(self.sem[inc[0]], inc[1])
                        else:
                            ins.then_inc(self.sem[inc[0]], inc[1])

            getattr(block, names[eng])(body)


def build_stage_table():
    st = []
    for name, c0 in (("a_gate", 512), ("a_lin", 0), ("u", 1024), ("v", 1536),
                     ("ga0", 2048), ("ga1", 2560), ("gb0", 3072), ("gb1", 3584)):
        st.append(dict(name=name, src="w_in", nk=8, k0=0, cols=[(c0, 512)], scale="g_mix"))
    st.append(dict(name="a_out", src="w_a_out", nk=4, k0=0, cols=[(0, 1024)], scale=None))
    st.append(dict(name="b_out", src="w_b_out", nk=4, k0=0, cols=[(0, 1024)], scale=None))
    for h in range(2):
        st.append(dict(name=f"wo{h}", src="w_o", nk=8, k0=0, cols=[(h * 512, 512)], scale=None))
    for s in range(11):
        st.append(dict(name=f"fi{s}", src="w_ffn_in", nk=8, k0=0,
                       cols=[(256 * s, 256), (DFF + 256 * s, 256)], scale="g_ffn"))
    for h in range(2):
        for q in range(3):
            nk = 8 if q < 2 else 6
            st.append(dict(name=f"fo{h}{q}", src="w_ffn_out", nk=nk, k0=8 * q,
                           cols=[(h * 512, 512)], scale=None))
    for i, s in enumerate(st):
        s["idx"] = i
        s["ncol"] = sum(c[1] for c in s["cols"])
        assert s["nk"] * s["ncol"] <= 4096
    return st


STAGES = build_stage_table()
NSTAGE = len(STAGES)
STAGE_BY_NAME = {s["name"]: s for s in STAGES}

VEC_ROWS = {}
_r = 0
for _name, _n in (("b_in", 32), ("b_dw", 4), ("ln_a_g", 4), ("ln_a_b", 4), ("b_a_out", 8),
                  ("g_mix", 8), ("g_ffn", 8), ("ln_v_b", 4)):
    VEC_ROWS[_name] = (_r, _n)
    _r += _n
NVROW = _r


_CACHE = {}


def build_program(n_prompt_pass=NPASS_PROMPT, do_sample=True, debug=False):
    nc = bass.Bass("TRN2", target_bir_lowering=False)
    S = Sched()

    def din(name, shape):
        return nc.dram_tensor(name, list(shape), F32, kind="ExternalInput").ap()

    def dout(name, shape):
        return nc.dram_tensor(name, list(shape), F32, kind="ExternalOutput").ap()

    xp = din("xp", [SEQ, D])
    xs = din("xs", [128, D])
    cch = din("cch", [120, CC])
    g_mix = din("g_mix", [D])
    w_in = din("w_in", [D, 4096])
    b_in = din("b_in", [4096])
    w_dw = din("w_dw", [KW, CC])
    b_dw = din("b_dw", [CC])
    ln_a_g = din("ln_a_g", [CC])
    ln_a_b = din("ln_a_b", [CC])
    w_a_out = din("w_a_out", [CC, D])
    b_a_out = din("b_a_out", [D])
    ln_v_g = din("ln_v_g", [CS])
    ln_v_b = din("ln_v_b", [CS])
    w_s = din("w_s", [4, 128, 128])
    b_s = din("b_s", [4, 128])
    w_b_out = din("w_b_out", [CS, D])
    w_o = din("w_o", [D, D])
    g_ffn = din("g_ffn", [D])
    w_ffn_in = din("w_ffn_in", [D, 2 * DFF])
    w_ffn_out = din("w_ffn_out", [DFF, D])
    g_final = din("g_final", [D])
    dram_in = dict(w_in=w_in, w_a_out=w_a_out, w_b_out=w_b_out, w_o=w_o, w_ffn_in=w_ffn_in,
                   w_ffn_out=w_ffn_out, b_in=b_in, b_dw=b_dw, ln_a_g=ln_a_g, ln_a_b=ln_a_b,
                   b_a_out=b_a_out, g_mix=g_mix, g_ffn=g_ffn, ln_v_b=ln_v_b)

    yp = dout("yp", [SEQ, D])
    ys = dout("ys", [128, D])
    scp = dout("scp", [30, CC])
    scs = dout("scs", [120, CC])
    svo = dout("svo", [128, CS])

    wscr = nc.dram_tensor("wscr", [NSTAGE, 128, 4096], BF16, kind="Internal").ap()
    if debug:
        dbg_sa = nc.dram_tensor("dbg_sa", [128, 4, TP], BF16, kind="ExternalOutput").ap()
        dbg_yb = nc.dram_tensor("dbg_yb", [128, 4, TP], BF16, kind="ExternalOutput").ap()
        dbg_h2 = nc.dram_tensor("dbg_h2", [128, 8, TP], BF16, kind="ExternalOutput").ap()
        dbg_hid = nc.dram_tensor("dbg_hid", [128, NJ, TP], BF16, kind="ExternalOutput").ap()
        dbg_m = nc.dram_tensor("dbg_m", [128, 8, TP], BF16, kind="ExternalOutput").ap()

    es = ExitStack()

    def sb(name, shape, dt):
        return es.enter_context(nc.sbuf_tensor(name, list(shape), dt))

    with es:
        xa = sb("xa", [128, 2, 4, D], F32)
        htm = sb("htm", [128, 4, D], BF16)
        junk = sb("junk", [128, D], BF16)
        hT = sb("hT", [128, 8, TP], BF16)
        RR = sb("RR", [128, NJ, TP], BF16)
        gl = sb("gl", [128, 4, 544], BF16)
        glf = sb("glf", [128, 4, 128], F32)
        usb = sb("usb", [128, 4, TP], BF16)
        vhat = sb("vhat", [128, 2, CS], F32)
        vnf = sb("vnf", [128, CS], F32)
        vg = sb("vg", [128, 4, CS], BF16)
        ycsb = sb("ycsb", [128, 4, TP], BF16)
        yctm = sb("yctm", [128, 4, CC], BF16)
        sa = sb("sa", [128, 4, TP], BF16)
        tsg = sb("tsg", [128, 2, CS], F32)
        ybin = sb("ybin", [128, 4, TP], BF16)
        t1 = sb("t1", [128, 2, TP], BF16)
        t2 = sb("t2", [128, 2, TP], BF16)
        sgt = sb("sgt", [128, 2, TP], BF16)
        ya = sb("ya", [128, 2, TP], BF16)
        stf = sb("stf", [128, CC], F32)
        hst = sb("hst", [128, CC], F32)
        ring = sb("ring", [128, 4, 4096], BF16)
        diag = sb("diag", [128, KW, 4, 128], BF16)
        ident_b = sb("ident_b", [128, 128], BF16)
        ident_f = sb("ident_f", [128, 128], F32)
        ones_f = sb("ones_f", [128, 128], F32)
        wsT = sb("wsT", [128, 4, 128], BF16)
        wsTb = sb("wsTb", [128, 4, 128], BF16)
        cst = sb("cst", [128, 4, 128], F32)
        csts = sb("csts", [128, 4, 128], F32)
        gvb = sb("gvb", [128, CS], F32)
        bvb = sb("bvb", [128, CS], F32)
        bsb = sb("bsb", [128, 4, 128], F32)
        gfb = sb("gfb", [128, D], F32)
        vrows = sb("vrows", [128, 128], F32)
        vcol = sb("vcol", [128, 128], F32)
        wcol = sb("wcol", [128, 4, 32], F32)
        bvrow = sb("bvrow", [1, CS], BF16)
        ones_row = sb("ones_row", [1, 128], BF16)
        epst = sb("epst", [128, 1], F32)
        ss = sb("ss", [128, 3, 4], F32)
        sd = sb("sd", [128, 3, 4], F32)
        rs_ = sb("rs_", [128, 3, 4], F32)
        st6 = sb("st6", [128, 2, 4, 6], F32)
        mv = sb("mv", [128, 2, 4, 2], F32)
        lsd = sb("lsd", [128, 2, 4], F32)
        lr = sb("lr", [128, 2, 4], F32)
        lnb = sb("lnb", [128, 2, 4], F32)

        ps = es.enter_context(nc.psum_tensor("ps", [128, 8, 512], F32))
        wsn = stf[:].rearrange("p (g s) -> p g s", s=128)
        wsTf = vnf[:].rearrange("p (g s) -> p g s", s=128)
        wdwr = hst
        bvrow_f = glf[0:1, :, :].rearrange("p a b -> p (a b)")

        for e in Sched.ENG:
            S.sem[e] = es.enter_context(nc.semaphore("s_" + e))
        dma_keys = (["ring%d" % i for i in range(4)] + ["xin0", "xin1", "yout0", "yout1",
                    "cst", "c1", "c2", "c3", "c4", "c5", "stg0", "stg1", "stg2", "stg3", "wout0", "wout1", "wout2", "wout3", "misc"])
        for k in dma_keys:
            S.sem[k] = es.enter_context(nc.semaphore("d_" + k))
            S.add_dma_sem(k)

        B = {}

        def bufs(name, *dims):
            if not dims:
                B[name] = Buf(name)
                return B[name]
            import itertools
            arr = {}
            for idx in itertools.product(*[range(d) for d in dims]):
                arr[idx if len(idx) > 1 else idx[0]] = Buf(f"{name}{idx}")
            B[name] = arr
            return arr

        X = bufs("X", 2, 4, 2)
        HTM = bufs("HTM", 4)
        HT = bufs("HT", 8, 4)
        R = bufs("R", NJ)
        GL = bufs("GL", 4)
        GLF = bufs("GLF")
        U = bufs("U", 4)
        VH = bufs("VH", 2)
        VNF = bufs("VNF")
        VG = bufs("VG", 4)
        YC = bufs("YC", 4)
        YT = bufs("YT", 4)
        SA = bufs("SA", 4)
        TSG = bufs("TSG", 2)
        YB = bufs("YB", 4)
        T1 = bufs("T1", 2)
        T2 = bufs("T2", 2)
        SGT = bufs("SGT", 2)
        YA = bufs("YA", 2)
        STF = bufs("STF")
        HST = bufs("HST")
        RING = bufs("RING", 4)
        PS = bufs("PS", 8)
        CONST = bufs("CONST")
        DIAGB = bufs("DIAGB")
        WSB = bufs("WSB")
        BC = bufs("BC")
        BC1 = bufs("BC1")
        BC2 = bufs("BC2")
        BC3 = bufs("BC3")
        VR = {r0: Buf("VR%d" % r0) for (r0, n_) in VEC_ROWS.values()}
        SSb = bufs("SS", 3, 4)
        SDb = bufs("SD", 3, 4)
        RSb = bufs("RS", 3, 4)
        ST6 = bufs("ST6", 2, 4)
        MV = bufs("MV", 2)
        LSD = bufs("LSD", 2)
        LR = bufs("LR", 2)
        LNB = bufs("LNB", 2)
        TMPB = bufs("TMPB", 8)
        WSCR = bufs("WSCR", NSTAGE)

        def xall(b, i):
            return [X[b, i, 0], X[b, i, 1]]

        psi = [0]

        def next_bank():
            b = psi[0] % 8
            psi[0] += 1
            return b

        def psf(b, n=512):
            return ps[:, b, 0:n]

        def psb(b):
            return ps[:, b, :].bitcast(BF16)

        def setup_early():
            C = [CONST]
            S.op("dve", lambda e: e.memset(epst[:], EPS), writes=C)
            S.op("dve", lambda e: e.memset(ones_f[:], 1.0), writes=C)
            S.op("dve", lambda e: e.memset(ones_row[:], 1.0), writes=C)
            S.op("dve", lambda e: e.memset(vrows[:], 0.0), writes=[TMPB[0]])
            S.op("pool", lambda e: e.affine_select(out=ident_f[:], in_=ones_f[:], pattern=[[-1, 128]],
                                                   compare_op=ALU.is_equal, fill=0.0, base=0,
                                                   channel_multiplier=1), reads=C, writes=C)
            S.op("dve", lambda e: e.tensor_copy(out=ident_b[:], in_=ident_f[:]), reads=C, writes=C)
            for name, (r0, n) in VEC_ROWS.items():
                src = dram_in[name].rearrange("(n p) -> n p", p=128)
                S.op("sp", lambda e, src=src, r0=r0, n=n: e.dma_start(out=vrows[r0:r0 + n, :], in_=src),
                     reads=[TMPB[0]], writes=[VR[r0]], dma="cst")
            b0 = next_bank()
            S.op("pe", lambda e: e.transpose(out=psf(b0, 128), in_=vrows[:], identity=ident_f[:]),
                 reads=[TMPB[0]] + list(VR.values()) + C, writes=[PS[b0]])
            S.op("act", lambda e: e.activation(out=vcol[:], in_=psf(b0, 128), func=AF.Copy),
                 reads=[PS[b0]], writes=C)

        def setup_rest_dma():
            C = [CONST]
            S.op("dve", lambda e: e.memset(wdwr[:], 0.0), writes=[HST])
            S.op("dve", lambda e: e.memset(wsn[:], 0.0), writes=[STF])
            S.op("pool", lambda e: e.dma_start(out=wdwr[0:KW, :], in_=w_dw), writes=[HST], dma="c1")
            S.op("pool", lambda e: e.dma_start(out=wsn[:], in_=w_s.rearrange("g t s -> t g s")),
                 writes=[STF], dma="c2")
            S.op("pool", lambda e: e.dma_start(
                out=bsb[:], in_=b_s.rearrange("g t -> (g t)").partition_broadcast(128)),
                writes=[TMPB[1]], dma="c3")
            S.op("pool", lambda e: e.dma_start(out=bvrow_f[:], in_=b_in[1536:2048].rearrange("(o n) -> o n", o=1)),
                 writes=[GLF], dma="c4")
            S.op("pool", lambda e: e.dma_start(out=gvb[:], in_=ln_v_g.partition_broadcast(128)),
                 writes=[BC1], dma="c5")
            S.op("pool", lambda e: e.dma_start(out=bvb[:], in_=ln_v_b.partition_broadcast(128)),
                 writes=[BC2], dma="c5")
            S.op("pool", lambda e: e.dma_start(out=gfb[:], in_=g_final.partition_broadcast(128)),
                 writes=[BC3], dma="c5")

        def setup_rest_compute():
            C = [CONST]
            S.op("dve", lambda e: e.tensor_copy(out=bvrow[:], in_=bvrow_f[:]), reads=[GLF], writes=[BC])
            b1 = next_bank()
            for cc in range(4):
                S.op("pe", lambda e, cc=cc: e.transpose(out=ps[:, b1, cc * 128:(cc + 1) * 128],
                                                       in_=wdwr[:, cc * 128:(cc + 1) * 128],
                                                       identity=ident_f[:]),
                     reads=[HST] + C, writes=[PS[b1]], signal=(cc == 3))
            S.op("act", lambda e: e.activation(
                out=wcol[:], in_=ps[:, b1, :].rearrange("p (c k) -> p c k", k=128)[:, :, 0:32],
                func=AF.Copy), reads=[PS[b1]], writes=[DIAGB])
            S.op("pool", lambda e: e.affine_select(out=wsn[:], in_=wsn[:], pattern=[[0, 4], [-1, 128]],
                                                   compare_op=ALU.is_ge, fill=0.0, base=0,
                                                   channel_multiplier=1), reads=[STF], writes=[STF])
            for cc in range(4):
                S.op("pool", lambda e, cc=cc: e.affine_select(
                    out=diag[:, :, cc, :], in_=wcol[:, cc, 0:KW].unsqueeze(2).to_broadcast([128, KW, 128]),
                    pattern=[[0, KW], [-1, 128]], compare_op=ALU.is_equal, fill=0.0, base=0,
                    channel_multiplier=1), reads=[DIAGB], writes=[DIAGB])
            b2 = next_bank()
            for g in range(4):
                S.op("pe", lambda e, g=g: e.transpose(out=ps[:, b2, g * 128:(g + 1) * 128],
                                                     in_=wsn[:, g, :], identity=ident_f[:]),
                     reads=[STF] + C, writes=[PS[b2]], signal=(g == 3))
            S.op("act", lambda e: e.activation(out=wsTf[:], in_=ps[:, b2, :].rearrange("p (g t) -> p g t", t=128),
                                               func=AF.Copy), reads=[PS[b2]], writes=[VNF])
            S.op("dve", lambda e: e.tensor_copy(out=wsT[:], in_=wsTf[:]), reads=[VNF], writes=[WSB])
            b3 = next_bank()
            for g in range(4):
                S.op("pe", lambda e, g=g: e.matmul(ps[:, b3, g * 128:(g + 1) * 128], lhsT=ones_f[:],
                                                  rhs=wsTf[:, g, :], start=True, stop=True,
                                                  skip_group_check=True),
                     reads=[VNF] + C, writes=[PS[b3]], signal=(g == 3))
            for g in range(4):
                S.op("dve", lambda e, g=g: e.scalar_tensor_tensor(
                    out=cst[:, g, :], in0=ps[:, b3, g * 128:(g + 1) * 128],
                    scalar=bvcol(g), in1=bsb[:, g, :], op0=ALU.mult, op1=ALU.add),
                    reads=[PS[b3], TMPB[1]] + C, writes=[WSB])

        def setup_sample_consts():
            C = [CONST]
            for stq in range(4):
                S.op("dve", lambda e, stq=stq: e.tensor_copy(out=csts[:, :, 32 * stq:32 * stq + 32],
                                                            in_=cst[:, :, 0:32]), reads=[WSB], writes=[WSB])
            S.op("dve", lambda e: e.memset(wsn[:], 0.0), writes=[STF])
            for stq in range(4):
                S.op("pool", lambda e, stq=stq: e.dma_start(
                    out=wsn[32 * stq:32 * stq + 32, :, 32 * stq:32 * stq + 32],
                    in_=w_s[:, 0:32, 0:32].rearrange("g t s -> t g s")), reads=[STF], writes=[TMPB[2 + stq]],
                    dma="c2")
            S.op("pool", lambda e: e.affine_select(out=wsn[:], in_=wsn[:], pattern=[[0, 4], [-1, 128]],
                                                   compare_op=ALU.is_ge, fill=0.0, base=0,
                                                   channel_multiplier=1), reads=[STF] + [TMPB[2 + q] for q in range(4)],
                 writes=[STF])
            b4 = next_bank()
            for g in range(4):
                S.op("pe", lambda e, g=g: e.transpose(out=ps[:, b4, g * 128:(g + 1) * 128],
                                                     in_=wsn[:, g, :], identity=ident_f[:]),
                     reads=[STF] + C, writes=[PS[b4]], signal=(g == 3))
            S.op("act", lambda e: e.activation(out=wsTb[:], in_=ps[:, b4, :].rearrange("p (g t) -> p g t", t=128),
                                               func=AF.Copy), reads=[PS[b4]], writes=[WSB])

        def bvcol(g):
            return col("ln_v_b", g)

        def col(name, j):
            r0, n = VEC_ROWS[name]
            assert j < n
            return vcol[:, r0 + j:r0 + j + 1]


        wstate = dict(next_load=0, nloads=0)
        load_plan = []

        stg_flat = xa[:, 1].rearrange("p a b -> p (a b)")
        stg_ctr = [0]

        def produce_stage(gidx):
            sidx = load_plan[gidx]
            s = STAGES[sidx]
            slot = gidx % 4
            nk, ncol = s["nk"], s["ncol"]
            n = nk * ncol
            src = dram_in[s["src"]]
            kph = 2048 // ncol
            for k0h in range(0, nk, kph):
                k1h = min(nk, k0h + kph)
                q = stg_ctr[0] % 2
                stg_ctr[0] += 1
                sbufs = [X[1, 2 * q + t, h] for t in range(2) for h in range(2)]
                sview = stg_flat[:, 2048 * q:2048 * q + (k1h - k0h) * ncol].rearrange("p (k n) -> p k n", n=ncol)
                pieces = []
                pos = 0
                for (c0, cn) in s["cols"]:
                    sap = src[(s["k0"] + k0h) * 128:(s["k0"] + k1h) * 128, c0:c0 + cn].rearrange(
                        "(k p) n -> p k n", p=128)
                    pieces.append((sap, pos, cn))
                    pos += cn
                S.op("sp", lambda e, pieces=pieces, sview=sview: [
                    e.dma_start(out=sview[:, :, pos:pos + cn], in_=sap) for (sap, pos, cn) in pieces],
                    writes=sbufs, dma="stg%d" % q, ndma=len(pieces))
                sc = s["scale"]
                for kc in range(k0h, k1h):
                    dst = ring[:, slot, kc * ncol:(kc + 1) * ncol]
                    srcv = sview[:, kc - k0h, :]
                    if False:
                        if sc is not None:
                            S.op("dve", lambda e, dst=dst, srcv=srcv, sc=sc, kc=kc: e.tensor_scalar(
                                out=dst, in0=srcv, scalar1=col(sc, kc), scalar2=None, op0=ALU.mult),
                                reads=sbufs + [CONST], writes=[RING[slot]])
                        else:
                            S.op("dve", lambda e, dst=dst, srcv=srcv: e.tensor_copy(out=dst, in_=srcv),
                                 reads=sbufs, writes=[RING[slot]])
                    else:
                        if sc is not None:
                            S.op("act", lambda e, dst=dst, srcv=srcv, sc=sc, kc=kc: e.activation(
                                out=dst, in_=srcv, func=AF.Copy, scale=col(sc, kc)),
                                reads=sbufs + [CONST], writes=[RING[slot]])
                        else:
                            S.op("act", lambda e, dst=dst, srcv=srcv: e.activation(out=dst, in_=srcv, func=AF.Copy),
                                 reads=sbufs, writes=[RING[slot]])
            S.op("pool", lambda e: e.dma_start(out=wscr[sidx, :, 0:n], in_=ring[:, slot, 0:n]),
                 reads=[RING[slot]], writes=[WSCR[sidx]], dma="wout%d" % slot)

        def issue_load(gidx):
            if gidx < NSTAGE:
                return produce_stage(gidx)
            sidx = load_plan[gidx]
            s = STAGES[sidx]
            slot = gidx % 4
            n = s["nk"] * s["ncol"]
            S.op("sp", lambda e: e.dma_start(out=ring[:, slot, 0:n], in_=wscr[sidx, :, 0:n]),
                 reads=[WSCR[sidx]], writes=[RING[slot]], dma="ring%d" % slot)

        def acquire_stage(name, hold=0):
            g = wstate["nloads"]
            assert STAGES[load_plan[g]]["name"] == name, (STAGES[load_plan[g]]["name"], name)
            while wstate["next_load"] <= min(g + 3 - hold, len(load_plan) - 1):
                issue_load(wstate["next_load"])
                wstate["next_load"] += 1
            wstate["nloads"] += 1
            s = STAGES[load_plan[g]]
            slot = g % 4
            view = ring[:, slot, 0:s["nk"] * s["ncol"]].rearrange("p (k n) -> p k n", n=s["ncol"])
            return slot, view

        def rms_pre(xb, i, which):
            S.op("act", lambda e: e.activation(out=junk[:], in_=xa[:, xb, i, :], func=AF.Square,
                                               accum_out=ss[:, which, i:i + 1]),
                 reads=xall(xb, i), writes=[SSb[which, i]])
            S.op("act", lambda e: e.activation(out=sd[:, which, i:i + 1], in_=ss[:, which, i:i + 1],
                                               func=AF.Sqrt, scale=1.0 / D, bias=epst[:]),
                 reads=[SSb[which, i], CONST], writes=[SDb[which, i]])
            S.op("dve", lambda e: e.reciprocal(out=rs_[:, which, i:i + 1], in_=sd[:, which, i:i + 1]),
                 reads=[SDb[which, i]], writes=[RSb[which, i]])
            S.op("dve", lambda e: e.tensor_scalar(
                out=htm[:, i, :], in0=xa[:, xb, i, :], scalar1=rs_[:, which, i:i + 1], scalar2=None,
                op0=ALU.mult), reads=xall(xb, i) + [RSb[which, i]], writes=[HTM[i]])

        def rms_T(i, eng="act"):
            bk = next_bank()
            for kc in range(8):
                S.op("pe", lambda e, kc=kc, bk=bk: e.transpose(
                    out=psb(bk)[:, kc * 128:(kc + 1) * 128], in_=htm[:, i, kc * 128:(kc + 1) * 128],
                    identity=ident_b[:]), reads=[HTM[i], CONST], writes=[PS[bk]], signal=(kc == 7))
            if eng == "both":
                S.op("act", lambda e, bk=bk: e.activation(
                    out=hT[:, 0:4, i * 128:(i + 1) * 128],
                    in_=psb(bk)[:, 0:512].rearrange("p (k t) -> p k t", t=128), func=AF.Copy),
                    reads=[PS[bk]], writes=[HT[kc, i] for kc in range(4)])
                S.op("dve", lambda e, bk=bk: e.tensor_copy(
                    out=hT[:, 4:8, i * 128:(i + 1) * 128],
                    in_=psb(bk)[:, 512:1024].rearrange("p (k t) -> p k t", t=128)),
                    reads=[PS[bk]], writes=[HT[kc, i] for kc in range(4, 8)])
            elif eng == "act":
                S.op("act", lambda e, bk=bk: e.activation(
                    out=hT[:, :, i * 128:(i + 1) * 128],
                    in_=psb(bk).rearrange("p (k t) -> p k t", t=128), func=AF.Copy),
                    reads=[PS[bk]], writes=[HT[kc, i] for kc in range(8)])
            else:
                S.op("dve", lambda e, bk=bk: e.tensor_copy(
                    out=hT[:, :, i * 128:(i + 1) * 128],
                    in_=psb(bk).rearrange("p (k t) -> p k t", t=128)),
                    reads=[PS[bk]], writes=[HT[kc, i] for kc in range(8)])

        def run_pass(kind, p, xb, hook_pre=None, hook_T=None, hook_end=None):
            sample = kind == "sample"
            NT = 1 if sample else 4
            T = 128 * NT
            first = (not sample) and p == 0
            last = (not sample) and p == NPASS_PROMPT - 1
            want_state = sample or last
            HTall = lambda kc: [HT[kc, i] for i in range(NT)]

            if first:
                S.op("dve", lambda e: e.memset(gl[:, :, 0:30], 0.0), writes=[GL[c] for c in range(4)])
            if sample:
                S.op("pool", lambda e: e.dma_start(out=hst[0:120, :], in_=cch), writes=[HST], dma="misc")
                bk = next_bank()
                for cc in range(4):
                    S.op("pe", lambda e, cc=cc, bk=bk: e.transpose(
                        out=ps[:, bk, cc * 120:(cc + 1) * 120], in_=hst[0:120, cc * 128:(cc + 1) * 128],
                        identity=ident_f[0:120, 0:120]), reads=[HST, CONST], writes=[PS[bk]], signal=(cc == 3))
                for cc in range(4):
                    S.op("act", lambda e, cc=cc, bk=bk: e.activation(
                        out=gl[:, cc, 0:248].rearrange("p (s r) -> p s r", r=62)[:, :, 0:30],
                        in_=ps[:, bk, cc * 120:(cc + 1) * 120].rearrange("p (s r) -> p s r", r=30),
                        func=AF.Copy), reads=[PS[bk]], writes=[GL[cc]])

            def gl_data(cc):
                if sample:
                    return gl[:, cc, 0:248].rearrange("p (s r) -> p s r", r=62)[:, :, 30:62]
                return gl[:, cc, 30:30 + T]

            def v3(ap):
                return ap.rearrange("p (s r) -> p s r", r=32) if sample else ap

            NDVE = 11
            accb = [t1[:, 0, :], t1[:, 1, :], t2[:, 0, :], t2[:, 1, :]]
            ACCB = [T1[0], T1[1], T2[0], T2[1]]

            def fm_stage(name, evac):
                slot, W = acquire_stage(name)

                def chunk(n):
                    bk = next_bank()
                    for kc in range(8):
                        S.op("pe", lambda e, kc=kc, n=n, bk=bk, W=W: e.matmul(
                            psf(bk, T), lhsT=W[:, kc, n * 128:(n + 1) * 128], rhs=hT[:, kc, 0:T],
                            start=(kc == 0), stop=(kc == 7)),
                            reads=[RING[slot]] + HTall(kc), writes=[PS[bk]], signal=(kc == 7))
                    evac(n, bk)
                return chunk

            def ev_gate(n, bk):
                S.op("act", lambda e: e.activation(
                    out=RR[:, 16 + n, 0:T], in_=psf(bk, T), func=AF.Sigmoid, bias=col("b_in", 4 + n)),
                    reads=[PS[bk], CONST], writes=[R[16 + n]])
            ch = fm_stage("a_gate", ev_gate)
            for n in range(4):
                ch(n)
            if not wstate.get("setup_done"):
                wstate["setup_done"] = True
                setup_rest_compute()
            elif not wstate.get("setup2_done"):
                wstate["setup2_done"] = True
                setup_sample_consts()

            def ev_lin(n, bk):
                S.op("dve", lambda e: e.scalar_tensor_tensor(
                    out=gl_data(n), in0=v3(psf(bk, T)), scalar=col("b_in", n), in1=v3(RR[:, 16 + n, 0:T]),
                    op0=ALU.add, op1=ALU.mult), reads=[PS[bk], R[16 + n], CONST], writes=[GL[n]])
                if want_state:
                    S.op("dve", lambda e: e.scalar_tensor_tensor(
                        out=glf[:, n, :], in0=ps[:, bk, T - 128:T], scalar=col("b_in", n),
                        in1=RR[:, 16 + n, T - 128:T], op0=ALU.add, op1=ALU.mult),
                        reads=[PS[bk], R[16 + n], CONST], writes=[GLF])
            ch = fm_stage("a_lin", ev_lin)
            for n in range(4):
                ch(n)

            def gl_tap(cc, k):
                if sample:
                    return gl[:, cc, 0:248].rearrange("p (s r) -> p s r", r=62)[:, :, k:k + 32]
                return gl[:, cc, k:k + T]

            def dve_conv(cc):
                if NDVE == 0:
                    return
                tb = cc % 2
                acc = v3(tsg[:, tb, 0:T])
                S.op("dve", lambda e: e.tensor_scalar(out=acc, in0=gl_tap(cc, 0), scalar1=wcol[:, cc, 0:1],
                                                      scalar2=None, op0=ALU.mult),
                     reads=[GL[cc], DIAGB], writes=[TSG[tb]])
                for k in range(1, NDVE):
                    lastk = k == NDVE - 1
                    out = v3(accb[cc][:, 0:T]) if lastk else acc
                    S.op("dve", lambda e, k=k, out=out: e.scalar_tensor_tensor(
                        out=out, in0=gl_tap(cc, k), scalar=wcol[:, cc, k:k + 1], in1=acc,
                        op0=ALU.mult, op1=ALU.add),
                        reads=[GL[cc], DIAGB, TSG[tb]], writes=[ACCB[cc]] if lastk else [TSG[tb]])

            dve_conv(0)

            def ev_u(n, bk):
                S.op("act", lambda e: e.activation(
                    out=usb[:, n, 0:T], in_=psf(bk, T), func=AF.Identity, bias=col("b_in", 8 + n)),
                    reads=[PS[bk], CONST], writes=[U[n]])
            ch = fm_stage("u", ev_u)
            for n in range(4):
                ch(n)

            slot, W = acquire_stage("v")
            vbanks = []
            for i in range(NT):
                bk = next_bank()
                vbanks.append(bk)
                for kc in range(8):
                    S.op("pe", lambda e, kc=kc, i=i, bk=bk, W=W: e.matmul(
                        psf(bk), lhsT=hT[:, kc, i * 128:(i + 1) * 128], rhs=W[:, kc, :],
                        start=(kc == 0), stop=False),
                        reads=[RING[slot], HT[kc, i]], writes=[PS[bk]], signal=False)
                S.op("pe", lambda e, bk=bk: e.matmul(psf(bk), lhsT=ones_row[0:1, :], rhs=bvrow[0:1, :],
                                                    start=False, stop=True),
                     reads=[CONST, BC], writes=[PS[bk]], signal=True)
                S.op("dve", lambda e, i=i, bk=bk: e.bn_stats(out=st6[:, 0, i, :], in_=psf(bk)),
                     reads=[PS[bk]], writes=[ST6[0, i]])
                S.op("dve", lambda e, i=i: e.bn_aggr(out=mv[:, 0, i, :], in_=st6[:, 0, i, :]),
                     reads=[ST6[0, i]], writes=[MV[0]])
            ln_finish(0, NT)
            for i in range(NT):
                bk = vbanks[i]
                vb = i % 2
                S.op("act", lambda e, i=i, bk=bk, vb=vb: e.activation(
                    out=vhat[:, vb, :], in_=psf(bk), func=AF.Identity, scale=lr[:, 0, i:i + 1],
                    bias=lnb[:, 0, i:i + 1]), reads=[PS[bk], LR[0], LNB[0]], writes=[VH[vb]])
                if sample:
                    S.op("dve", lambda e, vb=vb: e.tensor_tensor(out=vnf[:], in0=vhat[:, vb, :], in1=gvb[:],
                                                                op=ALU.mult),
                         reads=[VH[vb], BC1], writes=[VNF])
                    S.op("dve", lambda e, i=i: e.tensor_copy(out=vg[:, i, :], in_=vnf[:]),
                         reads=[VNF], writes=[VG[i]])
                    S.op("dve", lambda e: e.tensor_tensor(out=vnf[:], in0=vnf[:], in1=bvb[:], op=ALU.add),
                         reads=[VNF, BC2], writes=[VNF])
                    S.op("pool", lambda e: e.dma_start(out=svo[:, :], in_=vnf[:]), reads=[VNF], dma="misc")
                else:
                    S.op("dve", lambda e, i=i, vb=vb: e.tensor_tensor(
                        out=vg[:, i, :], in0=vhat[:, vb, :], in1=gvb[:], op=ALU.mult),
                        reads=[VH[vb], BC1], writes=[VG[i]])

            dve_conv(1)
            dve_conv(2)
            dve_conv(3)

            for hh in range(2):
                def ev_ga(n, bk, hh=hh):
                    c = hh * 4 + n
                    S.op("act", lambda e: e.activation(
                        out=RR[:, c, 0:T], in_=psf(bk, T), func=AF.Sigmoid, bias=col("b_in", 16 + c)),
                        reads=[PS[bk], CONST], writes=[R[c]])
                ch = fm_stage("ga%d" % hh, ev_ga)
                for n in range(4):
                    ch(n)

            wsx = wsTb if sample else wsT
            cstx = csts if sample else cst
            for i in range(NT):
                bk = next_bank()
                for g in range(4):
                    S.op("pe", lambda e, g=g, i=i, bk=bk: e.matmul(
                        ps[:, bk, g * 128:(g + 1) * 128], lhsT=vg[:, i, g * 128:(g + 1) * 128],
                        rhs=wsx[:, g, :], start=True, stop=True, skip_group_check=True),
                        reads=[VG[i], WSB], writes=[PS[bk]], signal=(g == 3))
                tb = i % 2
                S.op("dve", lambda e, bk=bk, tb=tb: e.tensor_tensor(
                    out=tsg[:, tb, :], in0=psf(bk), in1=cstx[:].rearrange("p g t -> p (g t)"), op=ALU.add),
                    reads=[PS[bk], WSB], writes=[TSG[tb]])
                S.op("dve", lambda e, i=i, tb=tb: e.tensor_tensor(
                    out=ybin[:, :, i * 128:(i + 1) * 128],
                    in0=tsg[:, tb, :].rearrange("p (g t) -> p g t", t=128),
                    in1=usb[:, :, i * 128:(i + 1) * 128], op=ALU.mult),
                    reads=[TSG[tb]] + [U[n] for n in range(4)], writes=[YB[i]])

            cbanks = [next_bank() for _ in range(4)]
            for cc in range(4):
                bk = cbanks[cc]
                for k in range(NDVE, KW):
                    if sample:
                        out = psf(bk, 128).rearrange("p (s r) -> p s r", r=32)
                    else:
                        out = psf(bk, T)
                    fin = (NDVE == 0 and k == KW - 1)
                    S.op("pe", lambda e, k=k, cc=cc, out=out, fin=fin: e.matmul(
                        out, lhsT=diag[:, k, cc, :], rhs=gl_tap(cc, k), start=(k == NDVE), stop=fin),
                        reads=[GL[cc], DIAGB], writes=[PS[bk]], signal=fin)
            def ev_gb_of(hh):
                def ev_gb(n, bk):
                    c = hh * 4 + n
                    S.op("act", lambda e: e.activation(
                        out=RR[:, 8 + c, 0:T], in_=psf(bk, T), func=AF.Sigmoid, bias=col("b_in", 24 + c)),
                        reads=[PS[bk], CONST], writes=[R[8 + c]])
                return ev_gb
            ch0 = fm_stage("gb0", ev_gb_of(0))
            if NDVE > 0:
                ch0(0)
            for cc in range(4):
                bk = cbanks[cc]
                if NDVE > 0:
                    S.op("pe", lambda e, cc=cc, bk=bk: e.matmul(
                        psf(bk, T), lhsT=ident_b[:], rhs=accb[cc][:, 0:T], start=False, stop=True),
                        reads=[ACCB[cc], CONST], writes=[PS[bk]], signal=True)
                S.op("act", lambda e, cc=cc, bk=bk: e.activation(
                    out=ycsb[:, cc, 0:T], in_=psf(bk, T), func=AF.Identity, bias=col("b_dw", cc)),
                    reads=[PS[bk], CONST], writes=[YC[cc]])
            if not sample and not last:
                for cc in range(4):
                    S.op("act", lambda e, cc=cc: e.activation(out=gl[:, cc, 0:30], in_=gl[:, cc, T:T + 30],
                                                             func=AF.Copy), reads=[GL[cc]], writes=[GL[cc]])
            if want_state:
                bk = next_bank()
                for cc in range(4):
                    S.op("pe", lambda e, cc=cc, bk=bk: e.transpose(
                        out=ps[:, bk, cc * 128:(cc + 1) * 128], in_=glf[:, cc, :], identity=ident_f[:]),
                        reads=[GLF, CONST], writes=[PS[bk]], signal=(cc == 3))
                S.op("act", lambda e, bk=bk: e.activation(out=stf[:], in_=psf(bk), func=AF.Copy),
                     reads=[PS[bk]], writes=[STF])
                if sample:
                    for stq in range(4):
                        S.op("pool", lambda e, stq=stq: e.dma_start(
                            out=scs[30 * stq:30 * stq + 30, :], in_=stf[32 * stq + 2:32 * stq + 32, :]),
                            reads=[STF], dma="misc")
                else:
                    S.op("pool", lambda e: e.dma_start(out=scp[:, :], in_=stf[98:128, :]),
                         reads=[STF], dma="misc")

            if NDVE == 0:
                ch0(0)
            tbanks = []
            for i in range(NT):
                bk = next_bank()
                tbanks.append(bk)
                for cc in range(4):
                    S.op("pe", lambda e, cc=cc, i=i, bk=bk: e.transpose(
                        out=psb(bk)[:, cc * 128:(cc + 1) * 128], in_=ycsb[:, cc, i * 128:(i + 1) * 128],
                        identity=ident_b[:]), reads=[YC[cc], CONST], writes=[PS[bk]], signal=(cc == 3))
                S.op("dve", lambda e, i=i, bk=bk: e.bn_stats(out=st6[:, 1, i, :], in_=psb(bk)[:, 0:512]),
                     reads=[PS[bk]], writes=[ST6[1, i]])
                S.op("dve", lambda e, i=i: e.bn_aggr(out=mv[:, 1, i, :], in_=st6[:, 1, i, :]),
                     reads=[ST6[1, i]], writes=[MV[1]])
            ln_finish(1, NT)
            for i in range(NT):
                bk = tbanks[i]
                S.op("act", lambda e, i=i, bk=bk: e.activation(
                    out=yctm[:, i, :], in_=psb(bk)[:, 0:512], func=AF.Identity, scale=lr[:, 1, i:i + 1],
                    bias=lnb[:, 1, i:i + 1]), reads=[PS[bk], LR[1], LNB[1]], writes=[YT[i]])
            for n in range(1, 4):
                ch0(n)
            ch1 = fm_stage("gb1", ev_gb_of(1))
            ch1(0)
            ch1(1)
            for cc in range(4):
                bk = next_bank()
                for i in range(NT):
                    S.op("pe", lambda e, cc=cc, i=i, bk=bk: e.transpose(
                        out=psb(bk)[:, i * 128:(i + 1) * 128], in_=yctm[:, i, cc * 128:(cc + 1) * 128],
                        identity=ident_b[:]), reads=[YT[i], CONST], writes=[PS[bk]], signal=(i == NT - 1))
                S.op("act", lambda e, cc=cc, bk=bk: e.activation(
                    out=sa[:, cc, 0:T], in_=psb(bk)[:, 0:T], func=AF.Silu, scale=col("ln_a_g", cc),
                    bias=col("ln_a_b", cc)), reads=[PS[bk], CONST], writes=[SA[cc]])
            ch1(2)
            ch1(3)
            slot_a, Wa = acquire_stage("a_out")
            slot_b, Wb = acquire_stage("b_out", hold=1)
            for c in range(8):
                bka = next_bank()
                for kc in range(4):
                    S.op("pe", lambda e, kc=kc, c=c, bk=bka: e.matmul(
                        psf(bk, T), lhsT=Wa[:, kc, c * 128:(c + 1) * 128], rhs=sa[:, kc, 0:T],
                        start=(kc == 0), stop=(kc == 3)),
                        reads=[RING[slot_a], SA[kc]], writes=[PS[bka]], signal=(kc == 3))
                bkb = next_bank()
                for kc in range(4):
                    S.op("pe", lambda e, kc=kc, c=c, bk=bkb: e.matmul(
                        psf(bk, T), lhsT=Wb[:, kc, c * 128:(c + 1) * 128], rhs=ybin[:, kc, 0:T],
                        start=(kc == 0), stop=(kc == 3)),
                        reads=[RING[slot_b]] + [YB[i] for i in range(NT)], writes=[PS[bkb]], signal=(kc == 3))
                tb = c % 2
                S.op("act", lambda e, c=c, bk=bka, tb=tb: e.activation(
                    out=ya[:, tb, 0:T], in_=psf(bk, T), func=AF.Identity, bias=col("b_a_out", c)),
                    reads=[PS[bka], CONST], writes=[YA[tb]])
                S.op("dve", lambda e, c=c, tb=tb: e.tensor_tensor(
                    out=t1[:, tb, 0:T], in0=ya[:, tb, 0:T], in1=RR[:, c, 0:T], op=ALU.mult),
                    reads=[YA[tb], R[c]], writes=[T1[tb]])
                S.op("dve", lambda e, c=c, bk=bkb, tb=tb: e.tensor_tensor(
                    out=t2[:, tb, 0:T], in0=psf(bk, T), in1=RR[:, 8 + c, 0:T], op=ALU.mult),
                    reads=[PS[bkb], R[8 + c]], writes=[T2[tb]])
                S.op("dve", lambda e, c=c, tb=tb: e.tensor_tensor(
                    out=hT[:, c, 0:T], in0=t1[:, tb, 0:T], in1=t2[:, tb, 0:T], op=ALU.add),
                    reads=[T1[tb], T2[tb]], writes=HTall(c))
            if debug and first:
                S.op("pool", lambda e: e.dma_start(out=dbg_m, in_=hT[:]),
                     reads=[HT[kc, i] for kc in range(8) for i in range(4)], dma="misc")
            slot0, W0 = acquire_stage("wo0")
            slot1, W1 = acquire_stage("wo1", hold=1)
            for i in range(NT):
                for h in range(2):
                    slot, W = (slot0, W0) if h == 0 else (slot1, W1)
                    bk = next_bank()
                    for kc in range(8):
                        S.op("pe", lambda e, kc=kc, i=i, bk=bk, W=W: e.matmul(
                            psf(bk), lhsT=hT[:, kc, i * 128:(i + 1) * 128], rhs=W[:, kc, :],
                            start=(kc == 0), stop=(kc == 7)),
                            reads=[RING[slot], HT[kc, i]], writes=[PS[bk]], signal=(kc == 7))
                    S.op("dve", lambda e, i=i, h=h, bk=bk: e.tensor_tensor(
                        out=xa[:, xb, i, h * 512:(h + 1) * 512], in0=psf(bk),
                        in1=xa[:, xb, i, h * 512:(h + 1) * 512], op=ALU.add),
                        reads=[PS[bk]], writes=[X[xb, i, h]])
                    if NT > 1 and ((i == 2 and h == 1) or (i == 3 and h == 0)):
                        rms_T(i - 2, "act")
                rms_pre(xb, i, 1)
            if NT == 1:
                rms_T(0)
            for s_ in range(11):
                slot, W = acquire_stage("fi%d" % s_)
                grp = []
                for jj in range(2):
                    j = 2 * s_ + jj
                    grp.append((j, next_bank(), jj * 128))
                    grp.append((j, next_bank(), 256 + jj * 128))
                if s_ == 0 and NT > 1:
                    spans = [(0, T - 256, list(range(NT - 2))), (T - 256, T, [NT - 2, NT - 1])]
                else:
                    spans = [(0, T, list(range(NT)))]
                for si, (c0, c1, tiles) in enumerate(spans):
                    for gi, (j, bk, wc) in enumerate(grp):
                        if s_ == 0 and NT > 1 and si == 0 and gi == 2:
                            rms_T(NT - 2, "dve")
                        if s_ == 0 and NT > 1 and si == 1 and gi == 0:
                            rms_T(NT - 1, "act")
                        for kc in range(8):
                            S.op("pe", lambda e, kc=kc, bk=bk, wc=wc, W=W, c0=c0, c1=c1: e.matmul(
                                ps[:, bk, c0:c1], lhsT=W[:, kc, wc:wc + 128], rhs=hT[:, kc, c0:c1],
                                start=(kc == 0), stop=(kc == 7), skip_group_check=True),
                                reads=[RING[slot]] + [HT[kc, i] for i in tiles], writes=[PS[bk]],
                                signal=(kc == 7))
                for jj in range(2):
                    j, bkg, _ = grp[2 * jj]
                    _, bku, _ = grp[2 * jj + 1]
                    tb = j % 2
                    S.op("act", lambda e, bk=bkg, tb=tb: e.activation(out=sgt[:, tb, 0:T], in_=psf(bk, T),
                                                                     func=AF.Silu),
                         reads=[PS[bkg]], writes=[SGT[tb]])
                    S.op("dve", lambda e, j=j, bk=bku, tb=tb: e.tensor_tensor(
                        out=RR[:, j, 0:T], in0=psf(bk, T), in1=sgt[:, tb, 0:T], op=ALU.mult),
                        reads=[PS[bku], SGT[tb]], writes=[R[j]])
            if hook_pre is not None:
                hook_pre()
            for h in range(2):
                if h == 1 and hook_T is not None:
                    hook_T()
                obanks = [next_bank() for _ in range(NT)]
                for q in range(3):
                    slot, W = acquire_stage("fo%d%d" % (h, q))
                    nk = 8 if q < 2 else 6
                    for i in range(NT):
                        bk = obanks[i]
                        for kk in range(nk):
                            j = 8 * q + kk
                            S.op("pe", lambda e, kk=kk, j=j, i=i, bk=bk, W=W: e.matmul(
                                psf(bk), lhsT=RR[:, j, i * 128:(i + 1) * 128], rhs=W[:, kk, :],
                                start=(j == 0), stop=(j == NJ - 1)),
                                reads=[RING[slot], R[j]], writes=[PS[bk]], signal=(kk == nk - 1))
                if h == 1 and hook_end is not None:
                    hook_end()
                for i in range(NT):
                    bk = obanks[i]
                    S.op("dve", lambda e, i=i, h=h, bk=bk: e.tensor_tensor(
                        out=xa[:, xb, i, h * 512:(h + 1) * 512], in0=psf(bk),
                        in1=xa[:, xb, i, h * 512:(h + 1) * 512], op=ALU.add),
                        reads=[PS[bk]], writes=[X[xb, i, h]])
            if debug and first:
                S.op("pool", lambda e: e.dma_start(out=dbg_sa, in_=sa[:]), reads=[SA[c] for c in range(4)], dma="misc")
                S.op("pool", lambda e: e.dma_start(out=dbg_yb, in_=ybin[:]), reads=[YB[c] for c in range(4)], dma="misc")
                S.op("pool", lambda e: e.dma_start(out=dbg_h2, in_=hT[:]),
                     reads=[HT[kc, i] for kc in range(8) for i in range(4)], dma="misc")
                S.op("pool", lambda e: e.dma_start(out=dbg_hid, in_=RR[:]), reads=[R[j] for j in range(NJ)], dma="misc")
            for i in range(NT):
                S.op("act", lambda e, i=i: e.activation(out=junk[:], in_=xa[:, xb, i, :], func=AF.Square,
                                                       accum_out=ss[:, 2, i:i + 1]),
                     reads=xall(xb, i), writes=[SSb[2, i]])
                S.op("act", lambda e, i=i: e.activation(out=sd[:, 2, i:i + 1], in_=ss[:, 2, i:i + 1],
                                                       func=AF.Sqrt, scale=1.0 / D, bias=epst[:]),
                     reads=[SSb[2, i], CONST], writes=[SDb[2, i]])
                S.op("dve", lambda e, i=i: e.reciprocal(out=rs_[:, 2, i:i + 1], in_=sd[:, 2, i:i + 1]),
                     reads=[SDb[2, i]], writes=[RSb[2, i]])
                S.op("dve", lambda e, i=i: e.scalar_tensor_tensor(
                    out=xa[:, xb, i, :], in0=xa[:, xb, i, :], scalar=rs_[:, 2, i:i + 1], in1=gfb[:],
                    op0=ALU.mult, op1=ALU.mult), reads=[RSb[2, i], BC3], writes=xall(xb, i))
                if sample:
                    dst = ys[:, :]
                else:
                    dst = yp[p * TP + i * 128:p * TP + (i + 1) * 128, :]
                S.op("pool", lambda e, i=i, dst=dst: e.dma_start(out=dst, in_=xa[:, xb, i, :]),
                     reads=xall(xb, i), dma="yout%d" % xb)

        def ln_finish(which, NT):
            S.op("act", lambda e: e.activation(out=lsd[:, which, 0:NT], in_=mv[:, which, 0:NT, 1],
                                               func=AF.Sqrt, scale=1.0, bias=epst[:]),
                 reads=[MV[which], CONST], writes=[LSD[which]])
            S.op("dve", lambda e: e.reciprocal(out=lr[:, which, 0:NT], in_=lsd[:, which, 0:NT]),
                 reads=[LSD[which]], writes=[LR[which]])
            S.op("dve", lambda e: e.scalar_tensor_tensor(
                out=lnb[:, which, 0:NT], in0=mv[:, which, 0:NT, 0], scalar=-1.0, in1=lr[:, which, 0:NT],
                op0=ALU.mult, op1=ALU.mult), reads=[MV[which], LR[which]], writes=[LNB[which]])

        def load_x(kind, p, xb):
            wr = [X[xb, i, h] for i in range(4) for h in range(2)]
            if kind == "sample":
                S.op("pool", lambda e: e.dma_start(out=xa[:, xb, 0, :], in_=xs[:, :]),
                     writes=[X[xb, 0, 0], X[xb, 0, 1]], dma="xin%d" % xb)
            else:
                src = xp[p * TP:(p + 1) * TP, :].rearrange("(i q) d -> q i d", q=128)
                S.op("pool", lambda e: e.dma_start(out=xa[:, xb], in_=src), writes=wr, dma="xin%d" % xb)

        passes = [("prompt", p) for p in range(n_prompt_pass)]
        if do_sample:
            passes.append(("sample", 0))
        for _ in passes:
            load_plan.extend(range(NSTAGE))

        def nt_of(kind):
            return 1 if kind == "sample" else 4

        setup_early()
        load_x(passes[0][0], passes[0][1], 0)
        for i in range(nt_of(passes[0][0])):
            rms_pre(0, i, 0)
        for i in range(nt_of(passes[0][0])):
            rms_T(i)
        setup_rest_dma()

        for n, (kind, p) in enumerate(passes):
            xb = n % 2
            hook_pre = hook_T = hook_end = None
            if n + 1 < len(passes):
                nkind, np_ = passes[n + 1]
                nxb = (n + 1) % 2
                pre = lambda nkind=nkind, nxb=nxb: [rms_pre(nxb, i, 0) for i in range(nt_of(nkind))]
                tr = lambda nkind=nkind: [rms_T(i) for i in range(nt_of(nkind))]
                if n == 0:
                    hook_T = lambda nkind=nkind, np_=np_, nxb=nxb, pre=pre: (load_x(nkind, np_, nxb), pre())
                    hook_end = tr
                else:
                    load_x(nkind, np_, nxb)
                    hook_pre = pre
                    hook_T = tr
            run_pass(kind, p, xb, hook_pre, hook_T, hook_end)

        S.wait_all("pool", ["yout0", "yout1", "misc"])
        S.wait_all("sp", ["yout0", "yout1", "misc", "wout0", "wout1", "wout2", "wout3"])

        with nc.Block() as block:
            S.emit(block)
    _CACHE["sched"] = S
    return nc


def _get_program():
    if "nc" not in _CACHE:
        _CACHE["nc"] = build_program()
    return _CACHE["nc"]


def make_in_maps(inputs):
    f = lambda a: np.ascontiguousarray(np.asarray(a, dtype=np.float32))
    shared = {
        "g_mix": f(inputs["g_mix"][0]), "w_in": f(inputs["w_in"][0]), "b_in": f(inputs["b_in"][0]),
        "w_dw": f(inputs["w_dw"][0]), "b_dw": f(inputs["b_dw"][0]), "ln_a_g": f(inputs["ln_a_g"][0]),
        "ln_a_b": f(inputs["ln_a_b"][0]), "w_a_out": f(inputs["w_a_out"][0]),
        "b_a_out": f(inputs["b_a_out"][0]), "ln_v_g": f(inputs["ln_v_g"][0]),
        "ln_v_b": f(inputs["ln_v_b"][0]), "w_s": f(inputs["w_s"][0]), "b_s": f(inputs["b_s"][0]),
        "w_b_out": f(inputs["w_b_out"][0]), "w_o": f(inputs["w_o"][0]), "g_ffn": f(inputs["g_ffn"][0]),
        "w_ffn_in": f(inputs["w_ffn_in"][0]), "w_ffn_out": f(inputs["w_ffn_out"][0]),
        "g_final": f(inputs["g_final"]),
    }
    x_prompt = np.asarray(inputs["x_prompt"], dtype=np.float32)
    x_sample = np.asarray(inputs["x_sample"], dtype=np.float32)
    cache_conv = np.asarray(inputs["cache_conv"], dtype=np.float32)
    maps = []
    for c in range(NCORE):
        m = dict(shared)
        m["xp"] = np.ascontiguousarray(x_prompt[c])
        m["xs"] = np.ascontiguousarray(x_sample[4 * c:4 * c + 4].reshape(128, D))
        m["cch"] = np.ascontiguousarray(cache_conv[0, 4 * c:4 * c + 4].reshape(120, CC))
        maps.append(m)
    return maps


def kernel(**inputs):
    nc = _get_program()
    maps = make_in_maps(inputs)
    res = run_bass_kernel_spmd(nc, maps, core_ids=list(range(NCORE)))
    r = res.results
    y_prompt = np.stack([np.asarray(r[c]["yp"], dtype=np.float32) for c in range(NCORE)], axis=0)
    y_sample = np.concatenate([np.asarray(r[c]["ys"], dtype=np.float32).reshape(4, 32, D)
                               for c in range(NCORE)], axis=0)
    scp_ = np.stack([np.asarray(r[c]["scp"], dtype=np.float32) for c in range(NCORE)], axis=0)[None]
    scs_ = np.concatenate([np.asarray(r[c]["scs"], dtype=np.float32).reshape(4, 30, CC)
                           for c in range(NCORE)], axis=0)[None]
    sv_ = np.concatenate([np.asarray(r[c]["svo"], dtype=np.float32).reshape(4, 32, CS)
                          for c in range(NCORE)], axis=0)[None]
    return (y_prompt, y_sample, scp_, scs_, sv_)
```

```python
import numpy as np
from contextlib import ExitStack

import concourse.bass as bass
import concourse.mybir as mybir
from concourse.bass_utils import run_bass_kernel_spmd

F32 = mybir.dt.float32
BF16 = mybir.dt.bfloat16
AF = mybir.ActivationFunctionType
ALU = mybir.AluOpType

D = 1024
SEQ = 8192
NCORE = 8
CC = 512
CS = 512
KW = 31
DFF = 2816
NJ = DFF // 128
EPS = 1e-6
TP = 512
NPASS_PROMPT = SEQ // TP


class Buf:
    __slots__ = ("name", "w", "r")

    def __init__(self, name):
        self.name = name
        self.w = None
        self.r = []


class Sched:
    ENG = ("pe", "act", "dve", "pool", "sp")

    def __init__(self):
        self.ops = {e: [] for e in self.ENG}
        self.count = {e: 0 for e in self.ENG}
        self.waited = {e: {} for e in self.ENG}
        self.dcount = {}
        self.sem = {}
        self.pending_pe = False

    def add_dma_sem(self, key):
        self.dcount[key] = 0

    def _resolve(self, tok):
        sk, v = tok
        if sk in self.dcount:
            return sk, self.dcount[sk]
        return sk, v

    def op(self, eng, fn, reads=(), writes=(), signal=True, dma=None, ndma=1):
        deps = {}
        wset = set(id(b) for b in writes)
        for b in reads:
            if b.w is not None:
                sk, v = self._resolve(b.w)
                deps[sk] = max(deps.get(sk, 0), v)
        for b in writes:
            if b.w is not None:
                sk, v = self._resolve(b.w)
                deps[sk] = max(deps.get(sk, 0), v)
            for t in b.r:
                sk, v = self._resolve(t)
                deps[sk] = max(deps.get(sk, 0), v)
        waits = []
        wd = self.waited[eng]
        for sk, v in deps.items():
            if eng == "pe" and sk == "pe":
                continue
            if wd.get(sk, 0) >= v:
                continue
            wd[sk] = v
            waits.append((sk, v))
        if dma is not None:
            self.dcount[dma] += 16 * ndma
            tok = (dma, self.dcount[dma])
            inc = (dma, 16)
        elif signal:
            self.count[eng] += 1
            tok = (eng, self.count[eng])
            inc = (eng, 1)
            if eng == "pe":
                self.pending_pe = False
        else:
            tok = (eng, self.count[eng] + 1)
            inc = None
            if eng == "pe":
                self.pending_pe = True
        self.ops[eng].append((waits, fn, inc, ndma if dma is not None else 1))
        for b in writes:
            b.w = tok
            b.r = []
        for b in reads:
            if id(b) not in wset:
                b.r.append(tok)
        return tok

    def wait_all(self, eng, keys):
        waits = []
        for sk in keys:
            v = self.dcount[sk] if sk in self.dcount else self.count[sk]
            if v > 0:
                waits.append((sk, v))
        self.ops[eng].append((waits, None, None, 1))

    def emit(self, block):
        assert not self.pending_pe
        names = {"pe": "tensor", "act": "scalar", "dve": "vector", "pool": "gpsimd", "sp": "sync"}
        for eng in self.ENG:
            ops = self.ops[eng]
            if not ops:
                continue

            def body(e, ops=ops):
                for waits, fn, inc, _n in ops:
                    for sk, v in waits:
                        e.wait_ge(self.sem[sk], v)
                    if fn is None:
                        continue
                    ins = fn(e)
                    if inc is not None:
                        if isinstance(ins, (list, tuple)):
                            for i_ in ins:
                                i_.then_inc(self.sem[inc[0]], inc[1])
                        else:
                            ins.then_inc(self.sem[inc[0]], inc[1])

            getattr(block, names[eng])(body)


def build_stage_table():
    st = []
    for name, c0 in (("a_gate", 512), ("a_lin", 0), ("u", 1024), ("v", 1536),
                     ("ga0", 2048), ("ga1", 2560), ("gb0", 3072), ("gb1", 3584)):
        st.append(dict(name=name, src="w_in", nk=8, k0=0, cols=[(c0, 512)], scale="g_mix"))
    st.append(dict(name="a_out", src="w_a_out", nk=4, k0=0, cols=[(0, 1024)], scale=None))
    st.append(dict(name="b_out", src="w_b_out", nk=4, k0=0, cols=[(0, 1024)], scale=None))
    for h in range(2):
        st.append(dict(name=f"wo{h}", src="w_o", nk=8, k0=0, cols=[(h * 512, 512)], scale=None))
    for s in range(11):
        st.append(dict(name=f"fi{s}", src="w_ffn_in", nk=8, k0=0,
                       cols=[(256 * s, 256), (DFF + 256 * s, 256)], scale="g_ffn"))
    for h in range(2):
        for q in range(3):
            nk = 8 if q < 2 else 6
            st.append(dict(name=f"fo{h}{q}", src="w_ffn_out", nk=nk, k0=8 * q,
                           cols=[(h * 512, 512)], scale=None))
    for i, s in enumerate(st):
        s["idx"] = i
        s["ncol"] = sum(c[1] for c in s["cols"])
        assert s["nk"] * s["ncol"] <= 4096
    return st


STAGES = build_stage_table()
NSTAGE = len(STAGES)
STAGE_BY_NAME = {s["name"]: s for s in STAGES}

VEC_ROWS = {}
_r = 0
for _name, _n in (("b_in", 32), ("b_dw", 4), ("ln_a_g", 4), ("ln_a_b", 4), ("b_a_out", 8),
                  ("g_mix", 8), ("g_ffn", 8), ("ln_v_b", 4)):
    VEC_ROWS[_name] = (_r, _n)
    _r += _n
NVROW = _r


_CACHE = {}


def build_program(n_prompt_pass=NPASS_PROMPT, do_sample=True, debug=False):
    nc = bass.Bass("TRN2", target_bir_lowering=False)
    S = Sched()

    def din(name, shape):
        return nc.dram_tensor(name, list(shape), F32, kind="ExternalInput").ap()

    def dout(name, shape):
        return nc.dram_tensor(name, list(shape), F32, kind="ExternalOutput").ap()

    xp = din("xp", [SEQ, D])
    xs = din("xs", [128, D])
    cch = din("cch", [120, CC])
    g_mix = din("g_mix", [D])
    w_in = din("w_in", [D, 4096])
    b_in = din("b_in", [4096])
    w_dw = din("w_dw", [KW, CC])
    b_dw = din("b_dw", [CC])
    ln_a_g = din("ln_a_g", [CC])
    ln_a_b = din("ln_a_b", [CC])
    w_a_out = din("w_a_out", [CC, D])
    b_a_out = din("b_a_out", [D])
    ln_v_g = din("ln_v_g", [CS])
    ln_v_b = din("ln_v_b", [CS])
    w_s = din("w_s", [4, 128, 128])
    b_s = din("b_s", [4, 128])
    w_b_out = din("w_b_out", [CS, D])
    w_o = din("w_o", [D, D])
    g_ffn = din("g_ffn", [D])
    w_ffn_in = din("w_ffn_in", [D, 2 * DFF])
    w_ffn_out = din("w_ffn_out", [DFF, D])
    g_final = din("g_final", [D])
    dram_in = dict(w_in=w_in, w_a_out=w_a_out, w_b_out=w_b_out, w_o=w_o, w_ffn_in=w_ffn_in,
                   w_ffn_out=w_ffn_out, b_in=b_in, b_dw=b_dw, ln_a_g=ln_a_g, ln_a_b=ln_a_b,
                   b_a_out=b_a_out, g_mix=g_mix, g_ffn=g_ffn, ln_v_b=ln_v_b)

    yp = dout("yp", [SEQ, D])
    ys = dout("ys", [128, D])
    scp = dout("scp", [30, CC])
    scs = dout("scs", [120, CC])
    svo = dout("svo", [128, CS])

    wscr = nc.dram_tensor("wscr", [NSTAGE, 128, 4096], BF16, kind="Internal").ap()
    if debug:
        dbg_sa = nc.dram_tensor("dbg_sa", [128, 4, TP], BF16, kind="ExternalOutput").ap()
        dbg_yb = nc.dram_tensor("dbg_yb", [128, 4, TP], BF16, kind="ExternalOutput").ap()
        dbg_h2 = nc.dram_tensor("dbg_h2", [128, 8, TP], BF16, kind="ExternalOutput").ap()
        dbg_hid = nc.dram_tensor("dbg_hid", [128, NJ, TP], BF16, kind="ExternalOutput").ap()
        dbg_m = nc.dram_tensor("dbg_m", [128, 8, TP], BF16, kind="ExternalOutput").ap()

    es = ExitStack()

    def sb(name, shape, dt):
        return es.enter_context(nc.sbuf_tensor(name, list(shape), dt))

    with es:
        xa = sb("xa", [128, 2, 4, D], F32)
        htm = sb("htm", [128, 4, D], BF16)
        junk = sb("junk", [128, D], BF16)
        hT = sb("hT", [128, 8, TP], BF16)
        RR = sb("RR", [128, NJ, TP], BF16)
        gl = sb("gl", [128, 4, 544], BF16)
        glf = sb("glf", [128, 4, 128], F32)
        usb = sb("usb", [128, 4, TP], BF16)
        vhat = sb("vhat", [128, 2, CS], F32)
        vnf = sb("vnf", [128, CS], F32)
        vg = sb("vg", [128, 4, CS], BF16)
        ycsb = sb("ycsb", [128, 4, TP], BF16)
        yctm = sb("yctm", [128, 4, CC], BF16)
        sa = sb("sa", [128, 4, TP], BF16)
        tsg = sb("tsg", [128, 2, CS], F32)
        ybin = sb("ybin", [128, 4, TP], BF16)
        t1 = sb("t1", [128, 2, TP], BF16)
        t2 = sb("t2", [128, 2, TP], BF16)
        sgt = sb("sgt", [128, 2, TP], BF16)
        stf = sb("stf", [128, CC], F32)
        hst = sb("hst", [128, CC], F32)
        ring = sb("ring", [128, 4, 4096], BF16)
        diag = sb("diag", [128, KW, 4, 128], BF16)
        ident_b = sb("ident_b", [128, 128], BF16)
        ident_f = sb("ident_f", [128, 128], F32)
        ones_f = sb("ones_f", [128, 128], F32)
        wsT = sb("wsT", [128, 4, 128], BF16)
        wsTb = sb("wsTb", [128, 4, 128], BF16)
        cst = sb("cst", [128, 4, 128], F32)
        csts = sb("csts", [128, 4, 128], F32)
        gvb = sb("gvb", [128, CS], F32)
        bvb = sb("bvb", [128, CS], F32)
        bsb = sb("bsb", [128, 4, 128], F32)
        gfb = sb("gfb", [128, D], F32)
        vrows = sb("vrows", [128, 128], F32)
        vcol = sb("vcol", [128, 128], F32)
        wcol = sb("wcol", [128, 4, 32], F32)
        bvrow = sb("bvrow", [1, CS], BF16)
        ones_row = sb("ones_row", [1, 128], BF16)
        epst = sb("epst", [128, 1], F32)
        ss = sb("ss", [128, 3, 4], F32)
        sd = sb("sd", [128, 3, 4], F32)
        rs_ = sb("rs_", [128, 3, 4], F32)
        st6 = sb("st6", [128, 2, 4, 6], F32)
        mv = sb("mv", [128, 2, 4, 2], F32)
        lsd = sb("lsd", [128, 2, 4], F32)
        lr = sb("lr", [128, 2, 4], F32)
        lnb = sb("lnb", [128, 2, 4], F32)

        ps = es.enter_context(nc.psum_tensor("ps", [128, 8, 512], F32))
        wsn = stf[:].rearrange("p (g s) -> p g s", s=128)
        wsTf = vnf[:].rearrange("p (g s) -> p g s", s=128)
        wdwr = hst
        bvrow_f = glf[0:1, :, :].rearrange("p a b -> p (a b)")

        for e in Sched.ENG:
            S.sem[e] = es.enter_context(nc.semaphore("s_" + e))
        dma_keys = (["ring%d" % i for i in range(4)] + ["xin0", "xin1", "yout0", "yout1",
                    "cst", "c1", "c2", "c3", "c4", "c5", "stg0", "stg1", "stg2", "stg3", "wout0", "wout1", "wout2", "wout3", "misc"])
        for k in dma_keys:
            S.sem[k] = es.enter_context(nc.semaphore("d_" + k))
            S.add_dma_sem(k)

        B = {}

        def bufs(name, *dims):
            if not dims:
                B[name] = Buf(name)
                return B[name]
            import itertools
            arr = {}
            for idx in itertools.product(*[range(d) for d in dims]):
                arr[idx if len(idx) > 1 else idx[0]] = Buf(f"{name}{idx}")
            B[name] = arr
            return arr

        X = bufs("X", 2, 4, 2)
        HTM = bufs("HTM", 4)
        HT = bufs("HT", 8, 4)
        R = bufs("R", NJ)
        GL = bufs("GL", 4)
        GLF = bufs("GLF")
        U = bufs("U", 4)
        VH = bufs("VH", 2)
        VNF = bufs("VNF")
        VG = bufs("VG", 4)
        YC = bufs("YC", 4)
        YT = bufs("YT", 4)
        SA = bufs("SA", 4)
        TSG = bufs("TSG", 2)
        YB = bufs("YB", 4)
        T1 = bufs("T1", 2)
        T2 = bufs("T2", 2)
        SGT = bufs("SGT", 2)
        STF = bufs("STF")
        HST = bufs("HST")
        RING = bufs("RING", 4)
        PS = bufs("PS", 8)
        CONST = bufs("CONST")
        DIAGB = bufs("DIAGB")
        WSB = bufs("WSB")
        BC = bufs("BC")
        BC1 = bufs("BC1")
        BC2 = bufs("BC2")
        BC3 = bufs("BC3")
        VR = {r0: Buf("VR%d" % r0) for (r0, n_) in VEC_ROWS.values()}
        SSb = bufs("SS", 3, 4)
        SDb = bufs("SD", 3, 4)
        RSb = bufs("RS", 3, 4)
        ST6 = bufs("ST6", 2, 4)
        MV = bufs("MV", 2)
        LSD = bufs("LSD", 2)
        LR = bufs("LR", 2)
        LNB = bufs("LNB", 2)
        TMPB = bufs("TMPB", 8)
        WSCR = bufs("WSCR", NSTAGE)

        def xall(b, i):
            return [X[b, i, 0], X[b, i, 1]]

        psi = [0]

        def next_bank():
            b = psi[0] % 8
            psi[0] += 1
            return b

        def psf(b, n=512):
            return ps[:, b, 0:n]

        def psb(b):
            return ps[:, b, :].bitcast(BF16)

        def setup_early():
            C = [CONST]
            S.op("dve", lambda e: e.memset(epst[:], EPS), writes=C)
            S.op("dve", lambda e: e.memset(ones_f[:], 1.0), writes=C)
            S.op("dve", lambda e: e.memset(ones_row[:], 1.0), writes=C)
            S.op("dve", lambda e: e.memset(vrows[:], 0.0), writes=[TMPB[0]])
            S.op("pool", lambda e: e.affine_select(out=ident_f[:], in_=ones_f[:], pattern=[[-1, 128]],
                                                   compare_op=ALU.is_equal, fill=0.0, base=0,
                                                   channel_multiplier=1), reads=C, writes=C)
            S.op("dve", lambda e: e.tensor_copy(out=ident_b[:], in_=ident_f[:]), reads=C, writes=C)
            for name, (r0, n) in VEC_ROWS.items():
                src = dram_in[name].rearrange("(n p) -> n p", p=128)
                S.op("sp", lambda e, src=src, r0=r0, n=n: e.dma_start(out=vrows[r0:r0 + n, :], in_=src),
                     reads=[TMPB[0]], writes=[VR[r0]], dma="cst")
            b0 = next_bank()
            S.op("pe", lambda e: e.transpose(out=psf(b0, 128), in_=vrows[:], identity=ident_f[:]),
                 reads=[TMPB[0]] + list(VR.values()) + C, writes=[PS[b0]])
            S.op("act", lambda e: e.activation(out=vcol[:], in_=psf(b0, 128), func=AF.Copy),
                 reads=[PS[b0]], writes=C)

        def setup_rest_dma():
            C = [CONST]
            S.op("dve", lambda e: e.memset(wdwr[:], 0.0), writes=[HST])
            S.op("dve", lambda e: e.memset(wsn[:], 0.0), writes=[STF])
            S.op("pool", lambda e: e.dma_start(out=wdwr[0:KW, :], in_=w_dw), writes=[HST], dma="c1")
            S.op("pool", lambda e: e.dma_start(out=wsn[:], in_=w_s.rearrange("g t s -> t g s")),
                 writes=[STF], dma="c2")
            S.op("pool", lambda e: e.dma_start(
                out=bsb[:], in_=b_s.rearrange("g t -> (g t)").partition_broadcast(128)),
                writes=[TMPB[1]], dma="c3")
            S.op("pool", lambda e: e.dma_start(out=bvrow_f[:], in_=b_in[1536:2048].rearrange("(o n) -> o n", o=1)),
                 writes=[GLF], dma="c4")
            S.op("pool", lambda e: e.dma_start(out=gvb[:], in_=ln_v_g.partition_broadcast(128)),
                 writes=[BC1], dma="c5")
            S.op("pool", lambda e: e.dma_start(out=bvb[:], in_=ln_v_b.partition_broadcast(128)),
                 writes=[BC2], dma="c5")
            S.op("pool", lambda e: e.dma_start(out=gfb[:], in_=g_final.partition_broadcast(128)),
                 writes=[BC3], dma="c5")

        def setup_rest_compute():
            C = [CONST]
            S.op("dve", lambda e: e.tensor_copy(out=bvrow[:], in_=bvrow_f[:]), reads=[GLF], writes=[BC])
            b1 = next_bank()
            for cc in range(4):
                S.op("pe", lambda e, cc=cc: e.transpose(out=ps[:, b1, cc * 128:(cc + 1) * 128],
                                                       in_=wdwr[:, cc * 128:(cc + 1) * 128],
                                                       identity=ident_f[:]),
                     reads=[HST] + C, writes=[PS[b1]], signal=(cc == 3))
            S.op("act", lambda e: e.activation(
                out=wcol[:], in_=ps[:, b1, :].rearrange("p (c k) -> p c k", k=128)[:, :, 0:32],
                func=AF.Copy), reads=[PS[b1]], writes=[DIAGB])
            S.op("pool", lambda e: e.affine_select(out=wsn[:], in_=wsn[:], pattern=[[0, 4], [-1, 128]],
                                                   compare_op=ALU.is_ge, fill=0.0, base=0,
                                                   channel_multiplier=1), reads=[STF], writes=[STF])
            for cc in range(4):
                S.op("pool", lambda e, cc=cc: e.affine_select(
                    out=diag[:, :, cc, :], in_=wcol[:, cc, 0:KW].unsqueeze(2).to_broadcast([128, KW, 128]),
                    pattern=[[0, KW], [-1, 128]], compare_op=ALU.is_equal, fill=0.0, base=0,
                    channel_multiplier=1), reads=[DIAGB], writes=[DIAGB])
            b2 = next_bank()
            for g in range(4):
                S.op("pe", lambda e, g=g: e.transpose(out=ps[:, b2, g * 128:(g + 1) * 128],
                                                     in_=wsn[:, g, :], identity=ident_f[:]),
                     reads=[STF] + C, writes=[PS[b2]], signal=(g == 3))
            S.op("act", lambda e: e.activation(out=wsTf[:], in_=ps[:, b2, :].rearrange("p (g t) -> p g t", t=128),
                                               func=AF.Copy), reads=[PS[b2]], writes=[VNF])
            S.op("dve", lambda e: e.tensor_copy(out=wsT[:], in_=wsTf[:]), reads=[VNF], writes=[WSB])
            b3 = next_bank()
            for g in range(4):
                S.op("pe", lambda e, g=g: e.matmul(ps[:, b3, g * 128:(g + 1) * 128], lhsT=ones_f[:],
                                                  rhs=wsTf[:, g, :], start=True, stop=True,
                                                  skip_group_check=True),
                     reads=[VNF] + C, writes=[PS[b3]], signal=(g == 3))
            for g in range(4):
                S.op("dve", lambda e, g=g: e.scalar_tensor_tensor(
                    out=cst[:, g, :], in0=ps[:, b3, g * 128:(g + 1) * 128],
                    scalar=bvcol(g), in1=bsb[:, g, :], op0=ALU.mult, op1=ALU.add),
                    reads=[PS[b3], TMPB[1]] + C, writes=[WSB])

        def setup_sample_consts():
            C = [CONST]
            for stq in range(4):
                S.op("dve", lambda e, stq=stq: e.tensor_copy(out=csts[:, :, 32 * stq:32 * stq + 32],
                                                            in_=cst[:, :, 0:32]), reads=[WSB], writes=[WSB])
            S.op("dve", lambda e: e.memset(wsn[:], 0.0), writes=[STF])
            for stq in range(4):
                S.op("pool", lambda e, stq=stq: e.dma_start(
                    out=wsn[32 * stq:32 * stq + 32, :, 32 * stq:32 * stq + 32],
                    in_=w_s[:, 0:32, 0:32].rearrange("g t s -> t g s")), reads=[STF], writes=[TMPB[2 + stq]],
                    dma="c2")
            S.op("pool", lambda e: e.affine_select(out=wsn[:], in_=wsn[:], pattern=[[0, 4], [-1, 128]],
                                                   compare_op=ALU.is_ge, fill=0.0, base=0,
                                                   channel_multiplier=1), reads=[STF] + [TMPB[2 + q] for q in range(4)],
                 writes=[STF])
            b4 = next_bank()
            for g in range(4):
                S.op("pe", lambda e, g=g: e.transpose(out=ps[:, b4, g * 128:(g + 1) * 128],
                                                     in_=wsn[:, g, :], identity=ident_f[:]),
                     reads=[STF] + C, writes=[PS[b4]], signal=(g == 3))
            S.op("act", lambda e: e.activation(out=wsTb[:], in_=ps[:, b4, :].rearrange("p (g t) -> p g t", t=128),
                                               func=AF.Copy), reads=[PS[b4]], writes=[WSB])

        def bvcol(g):
            return col("ln_v_b", g)

        def col(name, j):
            r0, n = VEC_ROWS[name]
            assert j < n
            return vcol[:, r0 + j:r0 + j + 1]


        wstate = dict(next_load=0, nloads=0)
        load_plan = []

        stg_flat = xa[:, 1].rearrange("p a b -> p (a b)")
        stg_ctr = [0]

        def produce_stage(gidx):
            sidx = load_plan[gidx]
            s = STAGES[sidx]
            slot = gidx % 4
            nk, ncol = s["nk"], s["ncol"]
            n = nk * ncol
            src = dram_in[s["src"]]
            kph = 2048 // ncol
            for k0h in range(0, nk, kph):
                k1h = min(nk, k0h + kph)
                q = stg_ctr[0] % 2
                stg_ctr[0] += 1
                sbufs = [X[1, 2 * q + t, h] for t in range(2) for h in range(2)]
                sview = stg_flat[:, 2048 * q:2048 * q + (k1h - k0h) * ncol].rearrange("p (k n) -> p k n", n=ncol)
                pieces = []
                pos = 0
                for (c0, cn) in s["cols"]:
                    sap = src[(s["k0"] + k0h) * 128:(s["k0"] + k1h) * 128, c0:c0 + cn].rearrange(
                        "(k p) n -> p k n", p=128)
                    pieces.append((sap, pos, cn))
                    pos += cn
                S.op("sp", lambda e, pieces=pieces, sview=sview: [
                    e.dma_start(out=sview[:, :, pos:pos + cn], in_=sap) for (sap, pos, cn) in pieces],
                    writes=sbufs, dma="stg%d" % q, ndma=len(pieces))
                sc = s["scale"]
                for kc in range(k0h, k1h):
                    dst = ring[:, slot, kc * ncol:(kc + 1) * ncol]
                    srcv = sview[:, kc - k0h, :]
                    if False:
                        if sc is not None:
                            S.op("dve", lambda e, dst=dst, srcv=srcv, sc=sc, kc=kc: e.tensor_scalar(
                                out=dst, in0=srcv, scalar1=col(sc, kc), scalar2=None, op0=ALU.mult),
                                reads=sbufs + [CONST], writes=[RING[slot]])
                        else:
                            S.op("dve", lambda e, dst=dst, srcv=srcv: e.tensor_copy(out=dst, in_=srcv),
                                 reads=sbufs, writes=[RING[slot]])
                    else:
                        if sc is not None:
                            S.op("act", lambda e, dst=dst, srcv=srcv, sc=sc, kc=kc: e.activation(
                                out=dst, in_=srcv, func=AF.Copy, scale=col(sc, kc)),
                                reads=sbufs + [CONST], writes=[RING[slot]])
                        else:
                            S.op("act", lambda e, dst=dst, srcv=srcv: e.activation(out=dst, in_=srcv, func=AF.Copy),
                                 reads=sbufs, writes=[RING[slot]])
            S.op("pool", lambda e: e.dma_start(out=wscr[sidx, :, 0:n], in_=ring[:, slot, 0:n]),
                 reads=[RING[slot]], writes=[WSCR[sidx]], dma="wout%d" % slot)

        def issue_load(gidx):
            if gidx < NSTAGE:
                return produce_stage(gidx)
            sidx = load_plan[gidx]
            s = STAGES[sidx]
            slot = gidx % 4
            n = s["nk"] * s["ncol"]
            S.op("sp", lambda e: e.dma_start(out=ring[:, slot, 0:n], in_=wscr[sidx, :, 0:n]),
                 reads=[WSCR[sidx]], writes=[RING[slot]], dma="ring%d" % slot)

        def acquire_stage(name, hold=0):
            g = wstate["nloads"]
            assert STAGES[load_plan[g]]["name"] == name, (STAGES[load_plan[g]]["name"], name)
            while wstate["next_load"] <= min(g + 3 - hold, len(load_plan) - 1):
                issue_load(wstate["next_load"])
                wstate["next_load"] += 1
            wstate["nloads"] += 1
            s = STAGES[load_plan[g]]
            slot = g % 4
            view = ring[:, slot, 0:s["nk"] * s["ncol"]].rearrange("p (k n) -> p k n", n=s["ncol"])
            return slot, view

        def rms_pre(xb, i, which):
            S.op("act", lambda e: e.activation(out=junk[:], in_=xa[:, xb, i, :], func=AF.Square,
                                               accum_out=ss[:, which, i:i + 1]),
                 reads=xall(xb, i), writes=[SSb[which, i]])
            S.op("act", lambda e: e.activation(out=sd[:, which, i:i + 1], in_=ss[:, which, i:i + 1],
                                               func=AF.Sqrt, scale=1.0 / D, bias=epst[:]),
                 reads=[SSb[which, i], CONST], writes=[SDb[which, i]])
            S.op("dve", lambda e: e.reciprocal(out=rs_[:, which, i:i + 1], in_=sd[:, which, i:i + 1]),
                 reads=[SDb[which, i]], writes=[RSb[which, i]])
            S.op("dve", lambda e: e.tensor_scalar(
                out=htm[:, i, :], in0=xa[:, xb, i, :], scalar1=rs_[:, which, i:i + 1], scalar2=None,
                op0=ALU.mult), reads=xall(xb, i) + [RSb[which, i]], writes=[HTM[i]])

        def rms_T(i, eng="act"):
            bk = next_bank()
            for kc in range(8):
                S.op("pe", lambda e, kc=kc, bk=bk: e.transpose(
                    out=psb(bk)[:, kc * 128:(kc + 1) * 128], in_=htm[:, i, kc * 128:(kc + 1) * 128],
                    identity=ident_b[:]), reads=[HTM[i], CONST], writes=[PS[bk]], signal=(kc == 7))
            if eng == "both":
                S.op("act", lambda e, bk=bk: e.activation(
                    out=hT[:, 0:4, i * 128:(i + 1) * 128],
                    in_=psb(bk)[:, 0:512].rearrange("p (k t) -> p k t", t=128), func=AF.Copy),
                    reads=[PS[bk]], writes=[HT[kc, i] for kc in range(4)])
                S.op("dve", lambda e, bk=bk: e.tensor_copy(
                    out=hT[:, 4:8, i * 128:(i + 1) * 128],
                    in_=psb(bk)[:, 512:1024].rearrange("p (k t) -> p k t", t=128)),
                    reads=[PS[bk]], writes=[HT[kc, i] for kc in range(4, 8)])
            elif eng == "act":
                S.op("act", lambda e, bk=bk: e.activation(
                    out=hT[:, :, i * 128:(i + 1) * 128],
                    in_=psb(bk).rearrange("p (k t) -> p k t", t=128), func=AF.Copy),
                    reads=[PS[bk]], writes=[HT[kc, i] for kc in range(8)])
            else:
                S.op("dve", lambda e, bk=bk: e.tensor_copy(
                    out=hT[:, :, i * 128:(i + 1) * 128],
                    in_=psb(bk).rearrange("p (k t) -> p k t", t=128)),
                    reads=[PS[bk]], writes=[HT[kc, i] for kc in range(8)])

        def run_pass(kind, p, xb, hook_pre=None, hook_T=None, hook_end=None):
            sample = kind == "sample"
            NT = 1 if sample else 4
            T = 128 * NT
            first = (not sample) and p == 0
            last = (not sample) and p == NPASS_PROMPT - 1
            want_state = sample or last
            HTall = lambda kc: [HT[kc, i] for i in range(NT)]

            if first:
                S.op("dve", lambda e: e.memset(gl[:, :, 0:30], 0.0), writes=[GL[c] for c in range(4)])
            if sample:
                S.op("pool", lambda e: e.dma_start(out=hst[0:120, :], in_=cch), writes=[HST], dma="misc")
                bk = next_bank()
                for cc in range(4):
                    S.op("pe", lambda e, cc=cc, bk=bk: e.transpose(
                        out=ps[:, bk, cc * 120:(cc + 1) * 120], in_=hst[0:120, cc * 128:(cc + 1) * 128],
                        identity=ident_f[0:120, 0:120]), reads=[HST, CONST], writes=[PS[bk]], signal=(cc == 3))
                for cc in range(4):
                    S.op("act", lambda e, cc=cc, bk=bk: e.activation(
                        out=gl[:, cc, 0:248].rearrange("p (s r) -> p s r", r=62)[:, :, 0:30],
                        in_=ps[:, bk, cc * 120:(cc + 1) * 120].rearrange("p (s r) -> p s r", r=30),
                        func=AF.Copy), reads=[PS[bk]], writes=[GL[cc]])

            def gl_data(cc):
                if sample:
                    return gl[:, cc, 0:248].rearrange("p (s r) -> p s r", r=62)[:, :, 30:62]
                return gl[:, cc, 30:30 + T]

            def v3(ap):
                return ap.rearrange("p (s r) -> p s r", r=32) if sample else ap

            NDVE = 10
            accb = [t1[:, 0, :], t1[:, 1, :], t2[:, 0, :], t2[:, 1, :]]
            ACCB = [T1[0], T1[1], T2[0], T2[1]]

            def fm_stage(name, evac):
                slot, W = acquire_stage(name)

                def chunk(n):
                    bk = next_bank()
                    for kc in range(8):
                        S.op("pe", lambda e, kc=kc, n=n, bk=bk, W=W: e.matmul(
                            psf(bk, T), lhsT=W[:, kc, n * 128:(n + 1) * 128], rhs=hT[:, kc, 0:T],
                            start=(kc == 0), stop=(kc == 7)),
                            reads=[RING[slot]] + HTall(kc), writes=[PS[bk]], signal=(kc == 7))
                    evac(n, bk)
                return chunk

            def ev_gate(n, bk):
                S.op("act", lambda e: e.activation(
                    out=RR[:, 16 + n, 0:T], in_=psf(bk, T), func=AF.Sigmoid, bias=col("b_in", 4 + n)),
                    reads=[PS[bk], CONST], writes=[R[16 + n]])
            ch = fm_stage("a_gate", ev_gate)
            for n in range(4):
                ch(n)
            if not wstate.get("setup_done"):
                wstate["setup_done"] = True
                setup_rest_compute()
            elif not wstate.get("setup2_done"):
                wstate["setup2_done"] = True
                setup_sample_consts()

            def ev_lin(n, bk):
                S.op("dve", lambda e: e.scalar_tensor_tensor(
                    out=gl_data(n), in0=v3(psf(bk, T)), scalar=col("b_in", n), in1=v3(RR[:, 16 + n, 0:T]),
                    op0=ALU.add, op1=ALU.mult), reads=[PS[bk], R[16 + n], CONST], writes=[GL[n]])
                if want_state:
                    S.op("dve", lambda e: e.scalar_tensor_tensor(
                        out=glf[:, n, :], in0=ps[:, bk, T - 128:T], scalar=col("b_in", n),
                        in1=RR[:, 16 + n, T - 128:T], op0=ALU.add, op1=ALU.mult),
                        reads=[PS[bk], R[16 + n], CONST], writes=[GLF])
            ch = fm_stage("a_lin", ev_lin)
            for n in range(4):
                ch(n)

            def gl_tap(cc, k):
                if sample:
                    return gl[:, cc, 0:248].rearrange("p (s r) -> p s r", r=62)[:, :, k:k + 32]
                return gl[:, cc, k:k + T]

            def dve_conv(cc):
                if NDVE == 0:
                    return
                tb = cc % 2
                acc = v3(tsg[:, tb, 0:T])
                S.op("dve", lambda e: e.tensor_scalar(out=acc, in0=gl_tap(cc, 0), scalar1=wcol[:, cc, 0:1],
                                                      scalar2=None, op0=ALU.mult),
                     reads=[GL[cc], DIAGB], writes=[TSG[tb]])
                for k in range(1, NDVE):
                    lastk = k == NDVE - 1
                    out = v3(accb[cc][:, 0:T]) if lastk else acc
                    S.op("dve", lambda e, k=k, out=out: e.scalar_tensor_tensor(
                        out=out, in0=gl_tap(cc, k), scalar=wcol[:, cc, k:k + 1], in1=acc,
                        op0=ALU.mult, op1=ALU.add),
                        reads=[GL[cc], DIAGB, TSG[tb]], writes=[ACCB[cc]] if lastk else [TSG[tb]])

            dve_conv(0)

            def ev_u(n, bk):
                S.op("act", lambda e: e.activation(
                    out=usb[:, n, 0:T], in_=psf(bk, T), func=AF.Identity, bias=col("b_in", 8 + n)),
                    reads=[PS[bk], CONST], writes=[U[n]])
            ch = fm_stage("u", ev_u)
            for n in range(4):
                ch(n)

            slot, W = acquire_stage("v")
            vbanks = []
            for i in range(NT):
                bk = next_bank()
                vbanks.append(bk)
                for kc in range(8):
                    S.op("pe", lambda e, kc=kc, i=i, bk=bk, W=W: e.matmul(
                        psf(bk), lhsT=hT[:, kc, i * 128:(i + 1) * 128], rhs=W[:, kc, :],
                        start=(kc == 0), stop=False),
                        reads=[RING[slot], HT[kc, i]], writes=[PS[bk]], signal=False)
                S.op("pe", lambda e, bk=bk: e.matmul(psf(bk), lhsT=ones_row[0:1, :], rhs=bvrow[0:1, :],
                                                    start=False, stop=True),
                     reads=[CONST, BC], writes=[PS[bk]], signal=True)
                S.op("dve", lambda e, i=i, bk=bk: e.bn_stats(out=st6[:, 0, i, :], in_=psf(bk)),
                     reads=[PS[bk]], writes=[ST6[0, i]])
                S.op("dve", lambda e, i=i: e.bn_aggr(out=mv[:, 0, i, :], in_=st6[:, 0, i, :]),
                     reads=[ST6[0, i]], writes=[MV[0]])
            ln_finish(0, NT)
            for i in range(NT):
                bk = vbanks[i]
                vb = i % 2
                S.op("act", lambda e, i=i, bk=bk, vb=vb: e.activation(
                    out=vhat[:, vb, :], in_=psf(bk), func=AF.Identity, scale=lr[:, 0, i:i + 1],
                    bias=lnb[:, 0, i:i + 1]), reads=[PS[bk], LR[0], LNB[0]], writes=[VH[vb]])
                if sample:
                    S.op("dve", lambda e, vb=vb: e.tensor_tensor(out=vnf[:], in0=vhat[:, vb, :], in1=gvb[:],
                                                                op=ALU.mult),
                         reads=[VH[vb], BC1], writes=[VNF])
                    S.op("dve", lambda e, i=i: e.tensor_copy(out=vg[:, i, :], in_=vnf[:]),
                         reads=[VNF], writes=[VG[i]])
                    S.op("dve", lambda e: e.tensor_tensor(out=vnf[:], in0=vnf[:], in1=bvb[:], op=ALU.add),
                         reads=[VNF, BC2], writes=[VNF])
                    S.op("pool", lambda e: e.dma_start(out=svo[:, :], in_=vnf[:]), reads=[VNF], dma="misc")
                else:
                    S.op("dve", lambda e, i=i, vb=vb: e.tensor_tensor(
                        out=vg[:, i, :], in0=vhat[:, vb, :], in1=gvb[:], op=ALU.mult),
                        reads=[VH[vb], BC1], writes=[VG[i]])

            dve_conv(1)
            dve_conv(2)
            dve_conv(3)

            for hh in range(2):
                def ev_ga(n, bk, hh=hh):
                    c = hh * 4 + n
                    S.op("act", lambda e: e.activation(
                        out=RR[:, c, 0:T], in_=psf(bk, T), func=AF.Sigmoid, bias=col("b_in", 16 + c)),
                        reads=[PS[bk], CONST], writes=[R[c]])
                ch = fm_stage("ga%d" % hh, ev_ga)
                for n in range(4):
                    ch(n)

            wsx = wsTb if sample else wsT
            cstx = csts if sample else cst
            for i in range(NT):
                bk = next_bank()
                for g in range(4):
                    S.op("pe", lambda e, g=g, i=i, bk=bk: e.matmul(
                        ps[:, bk, g * 128:(g + 1) * 128], lhsT=vg[:, i, g * 128:(g + 1) * 128],
                        rhs=wsx[:, g, :], start=True, stop=True, skip_group_check=True),
                        reads=[VG[i], WSB], writes=[PS[bk]], signal=(g == 3))
                tb = i % 2
                S.op("dve", lambda e, bk=bk, tb=tb: e.tensor_tensor(
                    out=tsg[:, tb, :], in0=psf(bk), in1=cstx[:].rearrange("p g t -> p (g t)"), op=ALU.add),
                    reads=[PS[bk], WSB], writes=[TSG[tb]])
                S.op("dve", lambda e, i=i, tb=tb: e.tensor_tensor(
                    out=ybin[:, :, i * 128:(i + 1) * 128],
                    in0=tsg[:, tb, :].rearrange("p (g t) -> p g t", t=128),
                    in1=usb[:, :, i * 128:(i + 1) * 128], op=ALU.mult),
                    reads=[TSG[tb]] + [U[n] for n in range(4)], writes=[YB[i]])

            cbanks = [next_bank() for _ in range(4)]
            for cc in range(4):
                bk = cbanks[cc]
                for k in range(NDVE, KW):
                    if sample:
                        out = psf(bk, 128).rearrange("p (s r) -> p s r", r=32)
                    else:
                        out = psf(bk, T)
                    fin = (NDVE == 0 and k == KW - 1)
                    S.op("pe", lambda e, k=k, cc=cc, out=out, fin=fin: e.matmul(
                        out, lhsT=diag[:, k, cc, :], rhs=gl_tap(cc, k), start=(k == NDVE), stop=fin),
                        reads=[GL[cc], DIAGB], writes=[PS[bk]], signal=fin)
            def ev_gb_of(hh):
                def ev_gb(n, bk):
                    c = hh * 4 + n
                    S.op("act", lambda e: e.activation(
                        out=RR[:, 8 + c, 0:T], in_=psf(bk, T), func=AF.Sigmoid, bias=col("b_in", 24 + c)),
                        reads=[PS[bk], CONST], writes=[R[8 + c]])
                return ev_gb
            ch0 = fm_stage("gb0", ev_gb_of(0))
            if NDVE > 0:
                ch0(0)
            for cc in range(4):
                bk = cbanks[cc]
                if NDVE > 0:
                    S.op("pe", lambda e, cc=cc, bk=bk: e.matmul(
                        psf(bk, T), lhsT=ident_b[:], rhs=accb[cc][:, 0:T], start=False, stop=True),
                        reads=[ACCB[cc], CONST], writes=[PS[bk]], signal=True)
                S.op("act", lambda e, cc=cc, bk=bk: e.activation(
                    out=ycsb[:, cc, 0:T], in_=psf(bk, T), func=AF.Identity, bias=col("b_dw", cc)),
                    reads=[PS[bk], CONST], writes=[YC[cc]])
            if not sample and not last:
                for cc in range(4):
                    S.op("act", lambda e, cc=cc: e.activation(out=gl[:, cc, 0:30], in_=gl[:, cc, T:T + 30],
                                                             func=AF.Copy), reads=[GL[cc]], writes=[GL[cc]])
            if want_state:
                bk = next_bank()
                for cc in range(4):
                    S.op("pe", lambda e, cc=cc, bk=bk: e.transpose(
                        out=ps[:, bk, cc * 128:(cc + 1) * 128], in_=glf[:, cc, :], identity=ident_f[:]),
                        reads=[GLF, CONST], writes=[PS[bk]], signal=(cc == 3))
                S.op("act", lambda e, bk=bk: e.activation(out=stf[:], in_=psf(bk), func=AF.Copy),
                     reads=[PS[bk]], writes=[STF])
                if sample:
                    for stq in range(4):
                        S.op("pool", lambda e, stq=stq: e.dma_start(
                            out=scs[30 * stq:30 * stq + 30, :], in_=stf[32 * stq + 2:32 * stq + 32, :]),
                            reads=[STF], dma="misc")
                else:
                    S.op("pool", lambda e: e.dma_start(out=scp[:, :], in_=stf[98:128, :]),
                         reads=[STF], dma="misc")

            if NDVE == 0:
                ch0(0)
            ch0(1)
            tbanks = []
            for i in range(NT):
                bk = next_bank()
                tbanks.append(bk)
                for cc in range(4):
                    S.op("pe", lambda e, cc=cc, i=i, bk=bk: e.transpose(
                        out=psb(bk)[:, cc * 128:(cc + 1) * 128], in_=ycsb[:, cc, i * 128:(i + 1) * 128],
                        identity=ident_b[:]), reads=[YC[cc], CONST], writes=[PS[bk]], signal=(cc == 3))
                S.op("dve", lambda e, i=i, bk=bk: e.bn_stats(out=st6[:, 1, i, :], in_=psb(bk)[:, 0:512]),
                     reads=[PS[bk]], writes=[ST6[1, i]])
                S.op("dve", lambda e, i=i: e.bn_aggr(out=mv[:, 1, i, :], in_=st6[:, 1, i, :]),
                     reads=[ST6[1, i]], writes=[MV[1]])
            ln_finish(1, NT)
            for i in range(NT):
                bk = tbanks[i]
                S.op("act", lambda e, i=i, bk=bk: e.activation(
                    out=yctm[:, i, :], in_=psb(bk)[:, 0:512], func=AF.Identity, scale=lr[:, 1, i:i + 1],
                    bias=lnb[:, 1, i:i + 1]), reads=[PS[bk], LR[1], LNB[1]], writes=[YT[i]])
            for n in range(2, 4):
                ch0(n)
            ch1 = fm_stage("gb1", ev_gb_of(1))
            ch1(0)
            ch1(1)
            for cc in range(4):
                bk = next_bank()
                for i in range(NT):
                    S.op("pe", lambda e, cc=cc, i=i, bk=bk: e.transpose(
                        out=psb(bk)[:, i * 128:(i + 1) * 128], in_=yctm[:, i, cc * 128:(cc + 1) * 128],
                        identity=ident_b[:]), reads=[YT[i], CONST], writes=[PS[bk]], signal=(i == NT - 1))
                S.op("act", lambda e, cc=cc, bk=bk: e.activation(
                    out=sa[:, cc, 0:T], in_=psb(bk)[:, 0:T], func=AF.Silu, scale=col("ln_a_g", cc),
                    bias=col("ln_a_b", cc)), reads=[PS[bk], CONST], writes=[SA[cc]])
            ch1(2)
            ch1(3)
            slot_a, Wa = acquire_stage("a_out")
            slot_b, Wb = acquire_stage("b_out", hold=1)
            for c in range(8):
                bka = next_bank()
                for kc in range(4):
                    S.op("pe", lambda e, kc=kc, c=c, bk=bka: e.matmul(
                        psf(bk, T), lhsT=Wa[:, kc, c * 128:(c + 1) * 128], rhs=sa[:, kc, 0:T],
                        start=(kc == 0), stop=(kc == 3)),
                        reads=[RING[slot_a], SA[kc]], writes=[PS[bka]], signal=(kc == 3))
                bkb = next_bank()
                for kc in range(4):
                    S.op("pe", lambda e, kc=kc, c=c, bk=bkb: e.matmul(
                        psf(bk, T), lhsT=Wb[:, kc, c * 128:(c + 1) * 128], rhs=ybin[:, kc, 0:T],
                        start=(kc == 0), stop=(kc == 3)),
                        reads=[RING[slot_b]] + [YB[i] for i in range(NT)], writes=[PS[bkb]], signal=(kc == 3))
                tb = c % 2
                S.op("dve", lambda e, c=c, bk=bka, tb=tb: e.scalar_tensor_tensor(
                    out=t1[:, tb, 0:T], in0=psf(bk, T), scalar=col("b_a_out", c), in1=RR[:, c, 0:T],
                    op0=ALU.add, op1=ALU.mult), reads=[PS[bka], R[c], CONST], writes=[T1[tb]])
                S.op("dve", lambda e, c=c, bk=bkb, tb=tb: e.tensor_tensor(
                    out=t2[:, tb, 0:T], in0=psf(bk, T), in1=RR[:, 8 + c, 0:T], op=ALU.mult),
                    reads=[PS[bkb], R[8 + c]], writes=[T2[tb]])
                S.op("dve", lambda e, c=c, tb=tb: e.tensor_tensor(
                    out=hT[:, c, 0:T], in0=t1[:, tb, 0:T], in1=t2[:, tb, 0:T], op=ALU.add),
                    reads=[T1[tb], T2[tb]], writes=HTall(c))
            if debug and first:
                S.op("pool", lambda e: e.dma_start(out=dbg_m, in_=hT[:]),
                     reads=[HT[kc, i] for kc in range(8) for i in range(4)], dma="misc")
            slot0, W0 = acquire_stage("wo0")
            slot1, W1 = acquire_stage("wo1", hold=1)
            for i in range(NT):
                for h in range(2):
                    slot, W = (slot0, W0) if h == 0 else (slot1, W1)
                    bk = next_bank()
                    for kc in range(8):
                        S.op("pe", lambda e, kc=kc, i=i, bk=bk, W=W: e.matmul(
                            psf(bk), lhsT=hT[:, kc, i * 128:(i + 1) * 128], rhs=W[:, kc, :],
                            start=(kc == 0), stop=(kc == 7)),
                            reads=[RING[slot], HT[kc, i]], writes=[PS[bk]], signal=(kc == 7))
                    S.op("dve", lambda e, i=i, h=h, bk=bk: e.tensor_tensor(
                        out=xa[:, xb, i, h * 512:(h + 1) * 512], in0=psf(bk),
                        in1=xa[:, xb, i, h * 512:(h + 1) * 512], op=ALU.add),
                        reads=[PS[bk]], writes=[X[xb, i, h]])
                    if NT > 1 and ((i == 2 and h == 1) or (i == 3 and h == 0)):
                        rms_T(i - 2, "act")
                rms_pre(xb, i, 1)
            if NT == 1:
                rms_T(0)
            for s_ in range(11):
                slot, W = acquire_stage("fi%d" % s_)
                grp = []
                for jj in range(2):
                    j = 2 * s_ + jj
                    grp.append((j, next_bank(), jj * 128))
                    grp.append((j, next_bank(), 256 + jj * 128))
                if s_ == 0 and NT > 1:
                    spans = [(0, T - 256, list(range(NT - 2))), (T - 256, T, [NT - 2, NT - 1])]
                else:
                    spans = [(0, T, list(range(NT)))]
                for si, (c0, c1, tiles) in enumerate(spans):
                    for gi, (j, bk, wc) in enumerate(grp):
                        if s_ == 0 and NT > 1 and si == 0 and gi == 2:
                            rms_T(NT - 2, "dve")
                        if s_ == 0 and NT > 1 and si == 1 and gi == 0:
                            rms_T(NT - 1, "act")
                        for kc in range(8):
                            S.op("pe", lambda e, kc=kc, bk=bk, wc=wc, W=W, c0=c0, c1=c1: e.matmul(
                                ps[:, bk, c0:c1], lhsT=W[:, kc, wc:wc + 128], rhs=hT[:, kc, c0:c1],
                                start=(kc == 0), stop=(kc == 7), skip_group_check=True),
                                reads=[RING[slot]] + [HT[kc, i] for i in tiles], writes=[PS[bk]],
                                signal=(kc == 7))
                for jj in range(2):
                    j, bkg, _ = grp[2 * jj]
                    _, bku, _ = grp[2 * jj + 1]
                    tb = j % 2
                    S.op("act", lambda e, bk=bkg, tb=tb: e.activation(out=sgt[:, tb, 0:T], in_=psf(bk, T),
                                                                     func=AF.Silu),
                         reads=[PS[bkg]], writes=[SGT[tb]])
                    S.op("dve", lambda e, j=j, bk=bku, tb=tb: e.tensor_tensor(
                        out=RR[:, j, 0:T], in0=psf(bk, T), in1=sgt[:, tb, 0:T], op=ALU.mult),
                        reads=[PS[bku], SGT[tb]], writes=[R[j]])
            if hook_pre is not None:
                hook_pre()
            for h in range(2):
                if h == 1 and hook_T is not None:
                    hook_T()
                obanks = [next_bank() for _ in range(NT)]
                for q in range(3):
                    slot, W = acquire_stage("fo%d%d" % (h, q))
                    nk = 8 if q < 2 else 6
                    for i in range(NT):
                        bk = obanks[i]
                        for kk in range(nk):
                            j = 8 * q + kk
                            S.op("pe", lambda e, kk=kk, j=j, i=i, bk=bk, W=W: e.matmul(
                                psf(bk), lhsT=RR[:, j, i * 128:(i + 1) * 128], rhs=W[:, kk, :],
                                start=(j == 0), stop=(j == NJ - 1)),
                                reads=[RING[slot], R[j]], writes=[PS[bk]], signal=(kk == nk - 1))
                if h == 1 and hook_end is not None:
                    hook_end()
                for i in range(NT):
                    bk = obanks[i]
                    S.op("dve", lambda e, i=i, h=h, bk=bk: e.tensor_tensor(
                        out=xa[:, xb, i, h * 512:(h + 1) * 512], in0=psf(bk),
                        in1=xa[:, xb, i, h * 512:(h + 1) * 512], op=ALU.add),
                        reads=[PS[bk]], writes=[X[xb, i, h]])
            if debug and first:
                S.op("pool", lambda e: e.dma_start(out=dbg_sa, in_=sa[:]), reads=[SA[c] for c in range(4)], dma="misc")
                S.op("pool", lambda e: e.dma_start(out=dbg_yb, in_=ybin[:]), reads=[YB[c] for c in range(4)], dma="misc")
                S.op("pool", lambda e: e.dma_start(out=dbg_h2, in_=hT[:]),
                     reads=[HT[kc, i] for kc in range(8) for i in range(4)], dma="misc")
                S.op("pool", lambda e: e.dma_start(out=dbg_hid, in_=RR[:]), reads=[R[j] for j in range(NJ)], dma="misc")
            for i in range(NT):
                S.op("act", lambda e, i=i: e.activation(out=junk[:], in_=xa[:, xb, i, :], func=AF.Square,
                                                       accum_out=ss[:, 2, i:i + 1]),
                     reads=xall(xb, i), writes=[SSb[2, i]])
                S.op("act", lambda e, i=i: e.activation(out=sd[:, 2, i:i + 1], in_=ss[:, 2, i:i + 1],
                                                       func=AF.Sqrt, scale=1.0 / D, bias=epst[:]),
                     reads=[SSb[2, i], CONST], writes=[SDb[2, i]])
                S.op("dve", lambda e, i=i: e.reciprocal(out=rs_[:, 2, i:i + 1], in_=sd[:, 2, i:i + 1]),
                     reads=[SDb[2, i]], writes=[RSb[2, i]])
                S.op("dve", lambda e, i=i: e.scalar_tensor_tensor(
                    out=xa[:, xb, i, :], in0=xa[:, xb, i, :], scalar=rs_[:, 2, i:i + 1], in1=gfb[:],
                    op0=ALU.mult, op1=ALU.mult), reads=[RSb[2, i], BC3], writes=xall(xb, i))
                if sample:
                    dst = ys[:, :]
                else:
                    dst = yp[p * TP + i * 128:p * TP + (i + 1) * 128, :]
                S.op("pool", lambda e, i=i, dst=dst: e.dma_start(out=dst, in_=xa[:, xb, i, :]),
                     reads=xall(xb, i), dma="yout%d" % xb)

        def ln_finish(which, NT):
            S.op("act", lambda e: e.activation(out=lsd[:, which, 0:NT], in_=mv[:, which, 0:NT, 1],
                                               func=AF.Sqrt, scale=1.0, bias=epst[:]),
                 reads=[MV[which], CONST], writes=[LSD[which]])
            S.op("dve", lambda e: e.reciprocal(out=lr[:, which, 0:NT], in_=lsd[:, which, 0:NT]),
                 reads=[LSD[which]], writes=[LR[which]])
            S.op("dve", lambda e: e.scalar_tensor_tensor(
                out=lnb[:, which, 0:NT], in0=mv[:, which, 0:NT, 0], scalar=-1.0, in1=lr[:, which, 0:NT],
                op0=ALU.mult, op1=ALU.mult), reads=[MV[which], LR[which]], writes=[LNB[which]])

        def load_x(kind, p, xb):
            wr = [X[xb, i, h] for i in range(4) for h in range(2)]
            if kind == "sample":
                S.op("pool", lambda e: e.dma_start(out=xa[:, xb, 0, :], in_=xs[:, :]),
                     writes=[X[xb, 0, 0], X[xb, 0, 1]], dma="xin%d" % xb)
            else:
                src = xp[p * TP:(p + 1) * TP, :].rearrange("(i q) d -> q i d", q=128)
                S.op("pool", lambda e: e.dma_start(out=xa[:, xb], in_=src), writes=wr, dma="xin%d" % xb)

        passes = [("prompt", p) for p in range(n_prompt_pass)]
        if do_sample:
            passes.append(("sample", 0))
        for _ in passes:
            load_plan.extend(range(NSTAGE))

        def nt_of(kind):
            return 1 if kind == "sample" else 4

        setup_early()
        load_x(passes[0][0], passes[0][1], 0)
        for i in range(nt_of(passes[0][0])):
            rms_pre(0, i, 0)
        for i in range(nt_of(passes[0][0])):
            rms_T(i)
        setup_rest_dma()

        for n, (kind, p) in enumerate(passes):
            xb = n % 2
            hook_pre = hook_T = hook_end = None
            if n + 1 < len(passes):
                nkind, np_ = passes[n + 1]
                nxb = (n + 1) % 2
                pre = lambda nkind=nkind, nxb=nxb: [rms_pre(nxb, i, 0) for i in range(nt_of(nkind))]
                tr = lambda nkind=nkind: [rms_T(i) for i in range(nt_of(nkind))]
                if n == 0:
                    hook_T = lambda nkind=nkind, np_=np_, nxb=nxb, pre=pre: (load_x(nkind, np_, nxb), pre())
                    hook_end = tr
                else:
                    load_x(nkind, np_, nxb)
                    hook_pre = pre
                    hook_T = tr
            run_pass(kind, p, xb, hook_pre, hook_T, hook_end)

        S.wait_all("pool", ["yout0", "yout1", "misc"])
        S.wait_all("sp", ["yout0", "yout1", "misc", "wout0", "wout1", "wout2", "wout3"])

        with nc.Block() as block:
            S.emit(block)
    _CACHE["sched"] = S
    return nc


def _get_program():
    if "nc" not in _CACHE:
        _CACHE["nc"] = build_program()
    return _CACHE["nc"]


def make_in_maps(inputs):
    f = lambda a: np.ascontiguousarray(np.asarray(a, dtype=np.float32))
    shared = {
        "g_mix": f(inputs["g_mix"][0]), "w_in": f(inputs["w_in"][0]), "b_in": f(inputs["b_in"][0]),
        "w_dw": f(inputs["w_dw"][0]), "b_dw": f(inputs["b_dw"][0]), "ln_a_g": f(inputs["ln_a_g"][0]),
        "ln_a_b": f(inputs["ln_a_b"][0]), "w_a_out": f(inputs["w_a_out"][0]),
        "b_a_out": f(inputs["b_a_out"][0]), "ln_v_g": f(inputs["ln_v_g"][0]),
        "ln_v_b": f(inputs["ln_v_b"][0]), "w_s": f(inputs["w_s"][0]), "b_s": f(inputs["b_s"][0]),
        "w_b_out": f(inputs["w_b_out"][0]), "w_o": f(inputs["w_o"][0]), "g_ffn": f(inputs["g_ffn"][0]),
        "w_ffn_in": f(inputs["w_ffn_in"][0]), "w_ffn_out": f(inputs["w_ffn_out"][0]),
        "g_final": f(inputs["g_final"]),
    }
    x_prompt = np.asarray(inputs["x_prompt"], dtype=np.float32)
    x_sample = np.asarray(inputs["x_sample"], dtype=np.float32)
    cache_conv = np.asarray(inputs["cache_conv"], dtype=np.float32)
    maps = []
    for c in range(NCORE):
        m = dict(shared)
        m["xp"] = np.ascontiguousarray(x_prompt[c])
        m["xs"] = np.ascontiguousarray(x_sample[4 * c:4 * c + 4].reshape(128, D))
        m["cch"] = np.ascontiguousarray(cache_conv[0, 4 * c:4 * c + 4].reshape(120, CC))
        maps.append(m)
    return maps


def kernel(**inputs):
    nc = _get_program()
    maps = make_in_maps(inputs)
    res = run_bass_kernel_spmd(nc, maps, core_ids=list(range(NCORE)))
    r = res.results
    y_prompt = np.stack([np.asarray(r[c]["yp"], dtype=np.float32) for c in range(NCORE)], axis=0)
    y_sample = np.concatenate([np.asarray(r[c]["ys"], dtype=np.float32).reshape(4, 32, D)
                               for c in range(NCORE)], axis=0)
    scp_ = np.stack([np.asarray(r[c]["scp"], dtype=np.float32) for c in range(NCORE)], axis=0)[None]
    scs_ = np.concatenate([np.asarray(r[c]["scs"], dtype=np.float32).reshape(4, 30, CC)
                           for c in range(NCORE)], axis=0)[None]
    sv_ = np.concatenate([np.asarray(r[c]["svo"], dtype=np.float32).reshape(4, 32, CS)
                          for c in range(NCORE)], axis=0)[None]
    return (y_prompt, y_sample, scp_, scs_, sv_)
```

```python
import numpy as np
from contextlib import ExitStack

import concourse.bass as bass
import concourse.mybir as mybir
from concourse.bass_utils import run_bass_kernel_spmd

F32 = mybir.dt.float32
BF16 = mybir.dt.bfloat16
AF = mybir.ActivationFunctionType
ALU = mybir.AluOpType

D = 1024
SEQ = 8192
NCORE = 8
CC = 512
CS = 512
KW = 31
DFF = 2816
NJ = DFF // 128
EPS = 1e-6
TP = 512
NPASS_PROMPT = SEQ // TP


class Buf:
    __slots__ = ("name", "w", "r")

    def __init__(self, name):
        self.name = name
        self.w = None
        self.r = []


class Sched:
    ENG = ("pe", "act", "dve", "pool", "sp")

    def __init__(self):
        self.ops = {e: [] for e in self.ENG}
        self.count = {e: 0 for e in self.ENG}
        self.waited = {e: {} for e in self.ENG}
        self.dcount = {}
        self.sem = {}
        self.pending_pe = False

    def add_dma_sem(self, key):
        self.dcount[key] = 0

    def _resolve(self, tok):
        sk, v = tok
        if sk in self.dcount:
            return sk, self.dcount[sk]
        return sk, v

    def op(self, eng, fn, reads=(), writes=(), signal=True, dma=None, ndma=1):
        deps = {}
        wset = set(id(b) for b in writes)
        for b in reads:
            if b.w is not None:
                sk, v = self._resolve(b.w)
                deps[sk] = max(deps.get(sk, 0), v)
        for b in writes:
            if b.w is not None:
                sk, v = self._resolve(b.w)
                deps[sk] = max(deps.get(sk, 0), v)
            for t in b.r:
                sk, v = self._resolve(t)
                deps[sk] = max(deps.get(sk, 0), v)
        waits = []
        wd = self.waited[eng]
        for sk, v in deps.items():
            if eng == "pe" and sk == "pe":
                continue
            if wd.get(sk, 0) >= v:
                continue
            wd[sk] = v
            waits.append((sk, v))
        if dma is not None:
            self.dcount[dma] += 16 * ndma
            tok = (dma, self.dcount[dma])
            inc = (dma, 16)
        elif signal:
            self.count[eng] += 1
            tok = (eng, self.count[eng])
            inc = (eng, 1)
            if eng == "pe":
                self.pending_pe = False
        else:
            tok = (eng, self.count[eng] + 1)
            inc = None
            if eng == "pe":
                self.pending_pe = True
        self.ops[eng].append((waits, fn, inc, ndma if dma is not None else 1))
        for b in writes:
            b.w = tok
            b.r = []
        for b in reads:
            if id(b) not in wset:
                b.r.append(tok)
        return tok

    def wait_all(self, eng, keys):
        waits = []
        for sk in keys:
            v = self.dcount[sk] if sk in self.dcount else self.count[sk]
            if v > 0:
                waits.append((sk, v))
        self.ops[eng].append((waits, None, None, 1))

    def emit(self, block):
        assert not self.pending_pe
        names = {"pe": "tensor", "act": "scalar", "dve": "vector", "pool": "gpsimd", "sp": "sync"}
        for eng in self.ENG:
            ops = self.ops[eng]
            if not ops:
                continue

            def body(e, ops=ops):
                for waits, fn, inc, _n in ops:
                    for sk, v in waits:
                        e.wait_ge(self.sem[sk], v)
                    if fn is None:
                        continue
                    ins = fn(e)
                    if inc is not None:
                        if isinstance(ins, (list, tuple)):
                            for i_ in ins:
                                i_.then_inc(self.sem[inc[0]], inc[1])
                        else:
                            ins.then_inc(self.sem[inc[0]], inc[1])

            getattr(block, names[eng])(body)


def build_stage_table():
    st = []
    for name, c0 in (("a_gate", 512), ("a_lin", 0), ("u", 1024), ("v", 1536),
                     ("ga0", 2048), ("ga1", 2560), ("gb0", 3072), ("gb1", 3584)):
        st.append(dict(name=name, src="w_in", nk=8, k0=0, cols=[(c0, 512)], scale="g_mix"))
    st.append(dict(name="a_out", src="w_a_out", nk=4, k0=0, cols=[(0, 1024)], scale=None))
    st.append(dict(name="b_out", src="w_b_out", nk=4, k0=0, cols=[(0, 1024)], scale=None))
    for h in range(2):
        st.append(dict(name=f"wo{h}", src="w_o", nk=8, k0=0, cols=[(h * 512, 512)], scale=None))
    for s in range(11):
        st.append(dict(name=f"fi{s}", src="w_ffn_in", nk=8, k0=0,
                       cols=[(256 * s, 256), (DFF + 256 * s, 256)], scale="g_ffn"))
    for h in range(2):
        for q in range(3):
            nk = 8 if q < 2 else 6
            st.append(dict(name=f"fo{h}{q}", src="w_ffn_out", nk=nk, k0=8 * q,
                           cols=[(h * 512, 512)], scale=None))
    for i, s in enumerate(st):
        s["idx"] = i
        s["ncol"] = sum(c[1] for c in s["cols"])
        assert s["nk"] * s["ncol"] <= 4096
    return st


STAGES = build_stage_table()
NSTAGE = len(STAGES)
STAGE_BY_NAME = {s["name"]: s for s in STAGES}

VEC_ROWS = {}
_r = 0
for _name, _n in (("b_in", 32), ("b_dw", 4), ("ln_a_g", 4), ("ln_a_b", 4), ("b_a_out", 8),
                  ("g_mix", 8), ("g_ffn", 8), ("ln_v_b", 4)):
    VEC_ROWS[_name] = (_r, _n)
    _r += _n
NVROW = _r


_CACHE = {}


def build_program(n_prompt_pass=NPASS_PROMPT, do_sample=True, debug=False):
    nc = bass.Bass("TRN2", target_bir_lowering=False)
    S = Sched()

    def din(name, shape):
        return nc.dram_tensor(name, list(shape), F32, kind="ExternalInput").ap()

    def dout(name, shape):
        return nc.dram_tensor(name, list(shape), F32, kind="ExternalOutput").ap()

    xp = din("xp", [SEQ, D])
    xs = din("xs", [128, D])
    cch = din("cch", [120, CC])
    g_mix = din("g_mix", [D])
    w_in = din("w_in", [D, 4096])
    b_in = din("b_in", [4096])
    w_dw = din("w_dw", [KW, CC])
    b_dw = din("b_dw", [CC])
    ln_a_g = din("ln_a_g", [CC])
    ln_a_b = din("ln_a_b", [CC])
    w_a_out = din("w_a_out", [CC, D])
    b_a_out = din("b_a_out", [D])
    ln_v_g = din("ln_v_g", [CS])
    ln_v_b = din("ln_v_b", [CS])
    w_s = din("w_s", [4, 128, 128])
    b_s = din("b_s", [4, 128])
    w_b_out = din("w_b_out", [CS, D])
    w_o = din("w_o", [D, D])
    g_ffn = din("g_ffn", [D])
    w_ffn_in = din("w_ffn_in", [D, 2 * DFF])
    w_ffn_out = din("w_ffn_out", [DFF, D])
    g_final = din("g_final", [D])
    dram_in = dict(w_in=w_in, w_a_out=w_a_out, w_b_out=w_b_out, w_o=w_o, w_ffn_in=w_ffn_in,
                   w_ffn_out=w_ffn_out, b_in=b_in, b_dw=b_dw, ln_a_g=ln_a_g, ln_a_b=ln_a_b,
                   b_a_out=b_a_out, g_mix=g_mix, g_ffn=g_ffn, ln_v_b=ln_v_b)

    yp = dout("yp", [SEQ, D])
    ys = dout("ys", [128, D])
    scp = dout("scp", [30, CC])
    scs = dout("scs", [120, CC])
    svo = dout("svo", [128, CS])

    wscr = nc.dram_tensor("wscr", [NSTAGE, 128, 4096], BF16, kind="Internal").ap()
    if debug:
        dbg_sa = nc.dram_tensor("dbg_sa", [128, 4, TP], BF16, kind="ExternalOutput").ap()
        dbg_yb = nc.dram_tensor("dbg_yb", [128, 4, TP], BF16, kind="ExternalOutput").ap()
        dbg_h2 = nc.dram_tensor("dbg_h2", [128, 8, TP], BF16, kind="ExternalOutput").ap()
        dbg_hid = nc.dram_tensor("dbg_hid", [128, NJ, TP], BF16, kind="ExternalOutput").ap()
        dbg_m = nc.dram_tensor("dbg_m", [128, 8, TP], BF16, kind="ExternalOutput").ap()

    es = ExitStack()

    def sb(name, shape, dt):
        return es.enter_context(nc.sbuf_tensor(name, list(shape), dt))

    with es:
        xa = sb("xa", [128, 2, 4, D], F32)
        htm = sb("htm", [128, 4, D], BF16)
        junk = sb("junk", [128, D], BF16)
        hT = sb("hT", [128, 8, TP], BF16)
        RR = sb("RR", [128, NJ, TP], BF16)
        gl = sb("gl", [128, 4, 544], BF16)
        glf = sb("glf", [128, 4, 128], F32)
        usb = sb("usb", [128, 4, TP], BF16)
        vhat = sb("vhat", [128, 2, CS], F32)
        vnf = sb("vnf", [128, CS], F32)
        vg = sb("vg", [128, 4, CS], BF16)
        ycsb = sb("ycsb", [128, 4, TP], BF16)
        yctm = sb("yctm", [128, 4, CC], BF16)
        sa = sb("sa", [128, 4, TP], BF16)
        tsg = sb("tsg", [128, 2, CS], F32)
        ybin = sb("ybin", [128, 4, TP], BF16)
        t1 = sb("t1", [128, 2, TP], BF16)
        t2 = sb("t2", [128, 2, TP], BF16)
        sgt = sb("sgt", [128, 2, TP], BF16)
        stf = sb("stf", [128, CC], F32)
        hst = sb("hst", [128, CC], F32)
        ring = sb("ring", [128, 4, 4096], BF16)
        diag = sb("diag", [128, KW, 4, 128], BF16)
        ident_b = sb("ident_b", [128, 128], BF16)
        ident_f = sb("ident_f", [128, 128], F32)
        ones_f = sb("ones_f", [128, 128], F32)
        wsT = sb("wsT", [128, 4, 128], BF16)
        wsTb = sb("wsTb", [128, 4, 128], BF16)
        cst = sb("cst", [128, 4, 128], F32)
        csts = sb("csts", [128, 4, 128], F32)
        gvb = sb("gvb", [128, CS], F32)
        bvb = sb("bvb", [128, CS], F32)
        bsb = sb("bsb", [128, 4, 128], F32)
        gfb = sb("gfb", [128, D], F32)
        vrows = sb("vrows", [128, 128], F32)
        vcol = sb("vcol", [128, 128], F32)
        wcol = sb("wcol", [128, 4, 32], F32)
        bvrow = sb("bvrow", [1, CS], BF16)
        ones_row = sb("ones_row", [1, 128], BF16)
        epst = sb("epst", [128, 1], F32)
        ss = sb("ss", [128, 3, 4], F32)
        sd = sb("sd", [128, 3, 4], F32)
        rs_ = sb("rs_", [128, 3, 4], F32)
        st6 = sb("st6", [128, 2, 4, 6], F32)
        mv = sb("mv", [128, 2, 4, 2], F32)
        lsd = sb("lsd", [128, 2, 4], F32)
        lr = sb("lr", [128, 2, 4], F32)
        lnb = sb("lnb", [128, 2, 4], F32)

        ps = es.enter_context(nc.psum_tensor("ps", [128, 8, 512], F32))
        wsn = stf[:].rearrange("p (g s) -> p g s", s=128)
        wsTf = vnf[:].rearrange("p (g s) -> p g s", s=128)
        wdwr = hst
        bvrow_f = glf[0:1, :, :].rearrange("p a b -> p (a b)")

        for e in Sched.ENG:
            S.sem[e] = es.enter_context(nc.semaphore("s_" + e))
        dma_keys = (["ring%d" % i for i in range(4)] + ["xin0", "xin1", "yout0", "yout1",
                    "cst", "c1", "c2", "c3", "c4", "c5", "stg0", "stg1", "stg2", "stg3", "wout0", "wout1", "wout2", "wout3", "misc"])
        for k in dma_keys:
            S.sem[k] = es.enter_context(nc.semaphore("d_" + k))
            S.add_dma_sem(k)

        B = {}

        def bufs(name, *dims):
            if not dims:
                B[name] = Buf(name)
                return B[name]
            import itertools
            arr = {}
            for idx in itertools.product(*[range(d) for d in dims]):
                arr[idx if len(idx) > 1 else idx[0]] = Buf(f"{name}{idx}")
            B[name] = arr
            return arr

        X = bufs("X", 2, 4, 2)
        HTM = bufs("HTM", 4)
        HT = bufs("HT", 8, 4)
        R = bufs("R", NJ)
        GL = bufs("GL", 4)
        GLF = bufs("GLF")
        U = bufs("U", 4)
        VH = bufs("VH", 2)
        VNF = bufs("VNF")
        VG = bufs("VG", 4)
        YC = bufs("YC", 4)
        YT = bufs("YT", 4)
        SA = bufs("SA", 4)
        TSG = bufs("TSG", 2)
        YB = bufs("YB", 4)
        T1 = bufs("T1", 2)
        T2 = bufs("T2", 2)
        SGT = bufs("SGT", 2)
        STF = bufs("STF")
        HST = bufs("HST")
        RING = bufs("RING", 4)
        PS = bufs("PS", 8)
        CONST = bufs("CONST")
        DIAGB = bufs("DIAGB")
        WSB = bufs("WSB")
        BC = bufs("BC")
        BC1 = bufs("BC1")
        BC2 = bufs("BC2")
        BC3 = bufs("BC3")
        VR = {r0: Buf("VR%d" % r0) for (r0, n_) in VEC_ROWS.values()}
        SSb = bufs("SS", 3, 4)
        SDb = bufs("SD", 3, 4)
        RSb = bufs("RS", 3, 4)
        ST6 = bufs("ST6", 2, 4)
        MV = bufs("MV", 2)
        LSD = bufs("LSD", 2)
        LR = bufs("LR", 2)
        LNB = bufs("LNB", 2)
        TMPB = bufs("TMPB", 8)
        WSCR = bufs("WSCR", NSTAGE)

        def xall(b, i):
            return [X[b, i, 0], X[b, i, 1]]

        psi = [0]

        def next_bank():
            b = psi[0] % 8
            psi[0] += 1
            return b

        def psf(b, n=512):
            return ps[:, b, 0:n]

        def psb(b):
            return ps[:, b, :].bitcast(BF16)

        def setup_early():
            C = [CONST]
            S.op("dve", lambda e: e.memset(epst[:], EPS), writes=C)
            S.op("dve", lambda e: e.memset(ones_f[:], 1.0), writes=C)
            S.op("dve", lambda e: e.memset(ones_row[:], 1.0), writes=C)
            S.op("dve", lambda e: e.memset(vrows[:], 0.0), writes=[TMPB[0]])
            S.op("pool", lambda e: e.affine_select(out=ident_f[:], in_=ones_f[:], pattern=[[-1, 128]],
                                                   compare_op=ALU.is_equal, fill=0.0, base=0,
                                                   channel_multiplier=1), reads=C, writes=C)
            S.op("dve", lambda e: e.tensor_copy(out=ident_b[:], in_=ident_f[:]), reads=C, writes=C)
            for name, (r0, n) in VEC_ROWS.items():
                src = dram_in[name].rearrange("(n p) -> n p", p=128)
                S.op("sp", lambda e, src=src, r0=r0, n=n: e.dma_start(out=vrows[r0:r0 + n, :], in_=src),
                     reads=[TMPB[0]], writes=[VR[r0]], dma="cst")
            b0 = next_bank()
            S.op("pe", lambda e: e.transpose(out=psf(b0, 128), in_=vrows[:], identity=ident_f[:]),
                 reads=[TMPB[0]] + list(VR.values()) + C, writes=[PS[b0]])
            S.op("act", lambda e: e.activation(out=vcol[:], in_=psf(b0, 128), func=AF.Copy),
                 reads=[PS[b0]], writes=C)

        def setup_rest_dma():
            C = [CONST]
            S.op("dve", lambda e: e.memset(wdwr[:], 0.0), writes=[HST])
            S.op("dve", lambda e: e.memset(wsn[:], 0.0), writes=[STF])
            S.op("pool", lambda e: e.dma_start(out=wdwr[0:KW, :], in_=w_dw), writes=[HST], dma="c1")
            S.op("pool", lambda e: e.dma_start(out=wsn[:], in_=w_s.rearrange("g t s -> t g s")),
                 writes=[STF], dma="c2")
            S.op("pool", lambda e: e.dma_start(
                out=bsb[:], in_=b_s.rearrange("g t -> (g t)").partition_broadcast(128)),
                writes=[TMPB[1]], dma="c3")
            S.op("pool", lambda e: e.dma_start(out=bvrow_f[:], in_=b_in[1536:2048].rearrange("(o n) -> o n", o=1)),
                 writes=[GLF], dma="c4")
            S.op("pool", lambda e: e.dma_start(out=gvb[:], in_=ln_v_g.partition_broadcast(128)),
                 writes=[BC1], dma="c5")
            S.op("pool", lambda e: e.dma_start(out=bvb[:], in_=ln_v_b.partition_broadcast(128)),
                 writes=[BC2], dma="c5")
            S.op("pool", lambda e: e.dma_start(out=gfb[:], in_=g_final.partition_broadcast(128)),
                 writes=[BC3], dma="c5")

        def setup_rest_compute():
            C = [CONST]
            S.op("dve", lambda e: e.tensor_copy(out=bvrow[:], in_=bvrow_f[:]), reads=[GLF], writes=[BC])
            b1 = next_bank()
            for cc in range(4):
                S.op("pe", lambda e, cc=cc: e.transpose(out=ps[:, b1, cc * 128:(cc + 1) * 128],
                                                       in_=wdwr[:, cc * 128:(cc + 1) * 128],
                                                       identity=ident_f[:]),
                     reads=[HST] + C, writes=[PS[b1]], signal=(cc == 3))
            S.op("act", lambda e: e.activation(
                out=wcol[:], in_=ps[:, b1, :].rearrange("p (c k) -> p c k", k=128)[:, :, 0:32],
                func=AF.Copy), reads=[PS[b1]], writes=[DIAGB])
            S.op("pool", lambda e: e.affine_select(out=wsn[:], in_=wsn[:], pattern=[[0, 4], [-1, 128]],
                                                   compare_op=ALU.is_ge, fill=0.0, base=0,
                                                   channel_multiplier=1), reads=[STF], writes=[STF])
            for cc in range(4):
                S.op("pool", lambda e, cc=cc: e.affine_select(
                    out=diag[:, :, cc, :], in_=wcol[:, cc, 0:KW].unsqueeze(2).to_broadcast([128, KW, 128]),
                    pattern=[[0, KW], [-1, 128]], compare_op=ALU.is_equal, fill=0.0, base=0,
                    channel_multiplier=1), reads=[DIAGB], writes=[DIAGB])
            b2 = next_bank()
            for g in range(4):
                S.op("pe", lambda e, g=g: e.transpose(out=ps[:, b2, g * 128:(g + 1) * 128],
                                                     in_=wsn[:, g, :], identity=ident_f[:]),
                     reads=[STF] + C, writes=[PS[b2]], signal=(g == 3))
            S.op("act", lambda e: e.activation(out=wsTf[:], in_=ps[:, b2, :].rearrange("p (g t) -> p g t", t=128),
                                               func=AF.Copy), reads=[PS[b2]], writes=[VNF])
            S.op("dve", lambda e: e.tensor_copy(out=wsT[:], in_=wsTf[:]), reads=[VNF], writes=[WSB])
            b3 = next_bank()
            for g in range(4):
                S.op("pe", lambda e, g=g: e.matmul(ps[:, b3, g * 128:(g + 1) * 128], lhsT=ones_f[:],
                                                  rhs=wsTf[:, g, :], start=True, stop=True,
                                                  skip_group_check=True),
                     reads=[VNF] + C, writes=[PS[b3]], signal=(g == 3))
            for g in range(4):
                S.op("dve", lambda e, g=g: e.scalar_tensor_tensor(
                    out=cst[:, g, :], in0=ps[:, b3, g * 128:(g + 1) * 128],
                    scalar=bvcol(g), in1=bsb[:, g, :], op0=ALU.mult, op1=ALU.add),
                    reads=[PS[b3], TMPB[1]] + C, writes=[WSB])

        def setup_sample_consts():
            C = [CONST]
            for stq in range(4):
                S.op("dve", lambda e, stq=stq: e.tensor_copy(out=csts[:, :, 32 * stq:32 * stq + 32],
                                                            in_=cst[:, :, 0:32]), reads=[WSB], writes=[WSB])
            S.op("dve", lambda e: e.memset(wsn[:], 0.0), writes=[STF])
            for stq in range(4):
                S.op("pool", lambda e, stq=stq: e.dma_start(
                    out=wsn[32 * stq:32 * stq + 32, :, 32 * stq:32 * stq + 32],
                    in_=w_s[:, 0:32, 0:32].rearrange("g t s -> t g s")), reads=[STF], writes=[TMPB[2 + stq]],
                    dma="c2")
            S.op("pool", lambda e: e.affine_select(out=wsn[:], in_=wsn[:], pattern=[[0, 4], [-1, 128]],
                                                   compare_op=ALU.is_ge, fill=0.0, base=0,
                                                   channel_multiplier=1), reads=[STF] + [TMPB[2 + q] for q in range(4)],
                 writes=[STF])
            b4 = next_bank()
            for g in range(4):
                S.op("pe", lambda e, g=g: e.transpose(out=ps[:, b4, g * 128:(g + 1) * 128],
                                                     in_=wsn[:, g, :], identity=ident_f[:]),
                     reads=[STF] + C, writes=[PS[b4]], signal=(g == 3))
            S.op("act", lambda e: e.activation(out=wsTb[:], in_=ps[:, b4, :].rearrange("p (g t) -> p g t", t=128),
                                               func=AF.Copy), reads=[PS[b4]], writes=[WSB])

        def bvcol(g):
            return col("ln_v_b", g)

        def col(name, j):
            r0, n = VEC_ROWS[name]
            assert j < n
            return vcol[:, r0 + j:r0 + j + 1]


        wstate = dict(next_load=0, nloads=0)
        load_plan = []

        stg_flat = xa[:, 1].rearrange("p a b -> p (a b)")
        stg_ctr = [0]

        def produce_stage(gidx):
            sidx = load_plan[gidx]
            s = STAGES[sidx]
            slot = gidx % 4
            nk, ncol = s["nk"], s["ncol"]
            n = nk * ncol
            src = dram_in[s["src"]]
            kph = 2048 // ncol
            for k0h in range(0, nk, kph):
                k1h = min(nk, k0h + kph)
                q = stg_ctr[0] % 2
                stg_ctr[0] += 1
                sbufs = [X[1, 2 * q + t, h] for t in range(2) for h in range(2)]
                sview = stg_flat[:, 2048 * q:2048 * q + (k1h - k0h) * ncol].rearrange("p (k n) -> p k n", n=ncol)
                pieces = []
                pos = 0
                for (c0, cn) in s["cols"]:
                    sap = src[(s["k0"] + k0h) * 128:(s["k0"] + k1h) * 128, c0:c0 + cn].rearrange(
                        "(k p) n -> p k n", p=128)
                    pieces.append((sap, pos, cn))
                    pos += cn
                S.op("sp", lambda e, pieces=pieces, sview=sview: [
                    e.dma_start(out=sview[:, :, pos:pos + cn], in_=sap) for (sap, pos, cn) in pieces],
                    writes=sbufs, dma="stg%d" % q, ndma=len(pieces))
                sc = s["scale"]
                for kc in range(k0h, k1h):
                    dst = ring[:, slot, kc * ncol:(kc + 1) * ncol]
                    srcv = sview[:, kc - k0h, :]
                    if False:
                        if sc is not None:
                            S.op("dve", lambda e, dst=dst, srcv=srcv, sc=sc, kc=kc: e.tensor_scalar(
                                out=dst, in0=srcv, scalar1=col(sc, kc), scalar2=None, op0=ALU.mult),
                                reads=sbufs + [CONST], writes=[RING[slot]])
                        else:
                            S.op("dve", lambda e, dst=dst, srcv=srcv: e.tensor_copy(out=dst, in_=srcv),
                                 reads=sbufs, writes=[RING[slot]])
                    else:
                        if sc is not None:
                            S.op("act", lambda e, dst=dst, srcv=srcv, sc=sc, kc=kc: e.activation(
                                out=dst, in_=srcv, func=AF.Copy, scale=col(sc, kc)),
                                reads=sbufs + [CONST], writes=[RING[slot]])
                        else:
                            S.op("act", lambda e, dst=dst, srcv=srcv: e.activation(out=dst, in_=srcv, func=AF.Copy),
                                 reads=sbufs, writes=[RING[slot]])
            S.op("pool", lambda e: e.dma_start(out=wscr[sidx, :, 0:n], in_=ring[:, slot, 0:n]),
                 reads=[RING[slot]], writes=[WSCR[sidx]], dma="wout%d" % slot)

        def issue_load(gidx):
            if gidx < NSTAGE:
                return produce_stage(gidx)
            sidx = load_plan[gidx]
            s = STAGES[sidx]
            slot = gidx % 4
            n = s["nk"] * s["ncol"]
            S.op("sp", lambda e: e.dma_start(out=ring[:, slot, 0:n], in_=wscr[sidx, :, 0:n]),
                 reads=[WSCR[sidx]], writes=[RING[slot]], dma="ring%d" % slot)

        def acquire_stage(name, hold=0):
            g = wstate["nloads"]
            assert STAGES[load_plan[g]]["name"] == name, (STAGES[load_plan[g]]["name"], name)
            while wstate["next_load"] <= min(g + 3 - hold, len(load_plan) - 1):
                issue_load(wstate["next_load"])
                wstate["next_load"] += 1
            wstate["nloads"] += 1
            s = STAGES[load_plan[g]]
            slot = g % 4
            view = ring[:, slot, 0:s["nk"] * s["ncol"]].rearrange("p (k n) -> p k n", n=s["ncol"])
            return slot, view

        def rms_pre(xb, i, which):
            S.op("act", lambda e: e.activation(out=junk[:], in_=xa[:, xb, i, :], func=AF.Square,
                                               accum_out=ss[:, which, i:i + 1]),
                 reads=xall(xb, i), writes=[SSb[which, i]])
            S.op("act", lambda e: e.activation(out=sd[:, which, i:i + 1], in_=ss[:, which, i:i + 1],
                                               func=AF.Sqrt, scale=1.0 / D, bias=epst[:]),
                 reads=[SSb[which, i], CONST], writes=[SDb[which, i]])
            S.op("dve", lambda e: e.reciprocal(out=rs_[:, which, i:i + 1], in_=sd[:, which, i:i + 1]),
                 reads=[SDb[which, i]], writes=[RSb[which, i]])
            S.op("dve", lambda e: e.tensor_scalar(
                out=htm[:, i, :], in0=xa[:, xb, i, :], scalar1=rs_[:, which, i:i + 1], scalar2=None,
                op0=ALU.mult), reads=xall(xb, i) + [RSb[which, i]], writes=[HTM[i]])

        def rms_T(i, eng="act"):
            bk = next_bank()
            for kc in range(8):
                S.op("pe", lambda e, kc=kc, bk=bk: e.transpose(
                    out=psb(bk)[:, kc * 128:(kc + 1) * 128], in_=htm[:, i, kc * 128:(kc + 1) * 128],
                    identity=ident_b[:]), reads=[HTM[i], CONST], writes=[PS[bk]], signal=(kc == 7))
            if eng == "both":
                S.op("act", lambda e, bk=bk: e.activation(
                    out=hT[:, 0:4, i * 128:(i + 1) * 128],
                    in_=psb(bk)[:, 0:512].rearrange("p (k t) -> p k t", t=128), func=AF.Copy),
                    reads=[PS[bk]], writes=[HT[kc, i] for kc in range(4)])
                S.op("dve", lambda e, bk=bk: e.tensor_copy(
                    out=hT[:, 4:8, i * 128:(i + 1) * 128],
                    in_=psb(bk)[:, 512:1024].rearrange("p (k t) -> p k t", t=128)),
                    reads=[PS[bk]], writes=[HT[kc, i] for kc in range(4, 8)])
            elif eng == "act":
                S.op("act", lambda e, bk=bk: e.activation(
                    out=hT[:, :, i * 128:(i + 1) * 128],
                    in_=psb(bk).rearrange("p (k t) -> p k t", t=128), func=AF.Copy),
                    reads=[PS[bk]], writes=[HT[kc, i] for kc in range(8)])
            else:
                S.op("dve", lambda e, bk=bk: e.tensor_copy(
                    out=hT[:, :, i * 128:(i + 1) * 128],
                    in_=psb(bk).rearrange("p (k t) -> p k t", t=128)),
                    reads=[PS[bk]], writes=[HT[kc, i] for kc in range(8)])

        def run_pass(kind, p, xb, hook_pre=None, hook_T=None, hook_end=None):
            sample = kind == "sample"
            NT = 1 if sample else 4
            T = 128 * NT
            first = (not sample) and p == 0
            last = (not sample) and p == NPASS_PROMPT - 1
            want_state = sample or last
            HTall = lambda kc: [HT[kc, i] for i in range(NT)]

            if first:
                S.op("dve", lambda e: e.memset(gl[:, :, 0:30], 0.0), writes=[GL[c] for c in range(4)])
            if sample:
                S.op("pool", lambda e: e.dma_start(out=hst[0:120, :], in_=cch), writes=[HST], dma="misc")
                bk = next_bank()
                for cc in range(4):
                    S.op("pe", lambda e, cc=cc, bk=bk: e.transpose(
                        out=ps[:, bk, cc * 120:(cc + 1) * 120], in_=hst[0:120, cc * 128:(cc + 1) * 128],
                        identity=ident_f[0:120, 0:120]), reads=[HST, CONST], writes=[PS[bk]], signal=(cc == 3))
                for cc in range(4):
                    S.op("act", lambda e, cc=cc, bk=bk: e.activation(
                        out=gl[:, cc, 0:248].rearrange("p (s r) -> p s r", r=62)[:, :, 0:30],
                        in_=ps[:, bk, cc * 120:(cc + 1) * 120].rearrange("p (s r) -> p s r", r=30),
                        func=AF.Copy), reads=[PS[bk]], writes=[GL[cc]])

            def gl_data(cc):
                if sample:
                    return gl[:, cc, 0:248].rearrange("p (s r) -> p s r", r=62)[:, :, 30:62]
                return gl[:, cc, 30:30 + T]

            def v3(ap):
                return ap.rearrange("p (s r) -> p s r", r=32) if sample else ap

            NDVE = 10
            accb = [t1[:, 0, :], t1[:, 1, :], t2[:, 0, :], t2[:, 1, :]]
            ACCB = [T1[0], T1[1], T2[0], T2[1]]

            def fm_stage(name, evac):
                slot, W = acquire_stage(name)

                def chunk(n):
                    bk = next_bank()
                    for kc in range(8):
                        S.op("pe", lambda e, kc=kc, n=n, bk=bk, W=W: e.matmul(
                            psf(bk, T), lhsT=W[:, kc, n * 128:(n + 1) * 128], rhs=hT[:, kc, 0:T],
                            start=(kc == 0), stop=(kc == 7)),
                            reads=[RING[slot]] + HTall(kc), writes=[PS[bk]], signal=(kc == 7))
                    evac(n, bk)
                return chunk

            def ev_gate(n, bk):
                S.op("act", lambda e: e.activation(
                    out=RR[:, 16 + n, 0:T], in_=psf(bk, T), func=AF.Sigmoid, bias=col("b_in", 4 + n)),
                    reads=[PS[bk], CONST], writes=[R[16 + n]])
            ch = fm_stage("a_gate", ev_gate)
            for n in range(4):
                ch(n)
            if not wstate.get("setup_done"):
                wstate["setup_done"] = True
                setup_rest_compute()
            elif not wstate.get("setup2_done"):
                wstate["setup2_done"] = True
                setup_sample_consts()

            def ev_lin(n, bk):
                S.op("dve", lambda e: e.scalar_tensor_tensor(
                    out=gl_data(n), in0=v3(psf(bk, T)), scalar=col("b_in", n), in1=v3(RR[:, 16 + n, 0:T]),
                    op0=ALU.add, op1=ALU.mult), reads=[PS[bk], R[16 + n], CONST], writes=[GL[n]])
                if want_state:
                    S.op("dve", lambda e: e.scalar_tensor_tensor(
                        out=glf[:, n, :], in0=ps[:, bk, T - 128:T], scalar=col("b_in", n),
                        in1=RR[:, 16 + n, T - 128:T], op0=ALU.add, op1=ALU.mult),
                        reads=[PS[bk], R[16 + n], CONST], writes=[GLF])
            ch = fm_stage("a_lin", ev_lin)
            for n in range(4):
                ch(n)

            def gl_tap(cc, k):
                if sample:
                    return gl[:, cc, 0:248].rearrange("p (s r) -> p s r", r=62)[:, :, k:k + 32]
                return gl[:, cc, k:k + T]

            def dve_conv(cc):
                if NDVE == 0:
                    return
                tb = cc % 2
                acc = v3(tsg[:, tb, 0:T])
                S.op("dve", lambda e: e.tensor_scalar(out=acc, in0=gl_tap(cc, 0), scalar1=wcol[:, cc, 0:1],
                                                      scalar2=None, op0=ALU.mult),
                     reads=[GL[cc], DIAGB], writes=[TSG[tb]])
                for k in range(1, NDVE):
                    lastk = k == NDVE - 1
                    out = v3(accb[cc][:, 0:T]) if lastk else acc
                    S.op("dve", lambda e, k=k, out=out: e.scalar_tensor_tensor(
                        out=out, in0=gl_tap(cc, k), scalar=wcol[:, cc, k:k + 1], in1=acc,
                        op0=ALU.mult, op1=ALU.add),
                        reads=[GL[cc], DIAGB, TSG[tb]], writes=[ACCB[cc]] if lastk else [TSG[tb]])

            dve_conv(0)

            def ev_u(n, bk):
                S.op("act", lambda e: e.activation(
                    out=usb[:, n, 0:T], in_=psf(bk, T), func=AF.Identity, bias=col("b_in", 8 + n)),
                    reads=[PS[bk], CONST], writes=[U[n]])
            ch = fm_stage("u", ev_u)
            for n in range(4):
                ch(n)

            slot, W = acquire_stage("v")
            vbanks = []
            for i in range(NT):
                bk = next_bank()
                vbanks.append(bk)
                for kc in range(8):
                    S.op("pe", lambda e, kc=kc, i=i, bk=bk, W=W: e.matmul(
                        psf(bk), lhsT=hT[:, kc, i * 128:(i + 1) * 128], rhs=W[:, kc, :],
                        start=(kc == 0), stop=False),
                        reads=[RING[slot], HT[kc, i]], writes=[PS[bk]], signal=False)
                S.op("pe", lambda e, bk=bk: e.matmul(psf(bk), lhsT=ones_row[0:1, :], rhs=bvrow[0:1, :],
                                                    start=False, stop=True),
                     reads=[CONST, BC], writes=[PS[bk]], signal=True)
                S.op("dve", lambda e, i=i, bk=bk: e.bn_stats(out=st6[:, 0, i, :], in_=psf(bk)),
                     reads=[PS[bk]], writes=[ST6[0, i]])
                S.op("dve", lambda e, i=i: e.bn_aggr(out=mv[:, 0, i, :], in_=st6[:, 0, i, :]),
                     reads=[ST6[0, i]], writes=[MV[0]])
            ln_finish(0, NT)
            for i in range(NT):
                bk = vbanks[i]
                vb = i % 2
                S.op("act", lambda e, i=i, bk=bk, vb=vb: e.activation(
                    out=vhat[:, vb, :], in_=psf(bk), func=AF.Identity, scale=lr[:, 0, i:i + 1],
                    bias=lnb[:, 0, i:i + 1]), reads=[PS[bk], LR[0], LNB[0]], writes=[VH[vb]])
                if sample:
                    S.op("dve", lambda e, vb=vb: e.tensor_tensor(out=vnf[:], in0=vhat[:, vb, :], in1=gvb[:],
                                                                op=ALU.mult),
                         reads=[VH[vb], BC1], writes=[VNF])
                    S.op("dve", lambda e, i=i: e.tensor_copy(out=vg[:, i, :], in_=vnf[:]),
                         reads=[VNF], writes=[VG[i]])
                    S.op("dve", lambda e: e.tensor_tensor(out=vnf[:], in0=vnf[:], in1=bvb[:], op=ALU.add),
                         reads=[VNF, BC2], writes=[VNF])
                    S.op("pool", lambda e: e.dma_start(out=svo[:, :], in_=vnf[:]), reads=[VNF], dma="misc")
                else:
                    S.op("dve", lambda e, i=i, vb=vb: e.tensor_tensor(
                        out=vg[:, i, :], in0=vhat[:, vb, :], in1=gvb[:], op=ALU.mult),
                        reads=[VH[vb], BC1], writes=[VG[i]])

            dve_conv(1)
            dve_conv(2)
            dve_conv(3)

            for hh in range(2):
                def ev_ga(n, bk, hh=hh):
                    c = hh * 4 + n
                    S.op("act", lambda e: e.activation(
                        out=RR[:, c, 0:T], in_=psf(bk, T), func=AF.Sigmoid, bias=col("b_in", 16 + c)),
                        reads=[PS[bk], CONST], writes=[R[c]])
                ch = fm_stage("ga%d" % hh, ev_ga)
                for n in range(4):
                    ch(n)

            wsx = wsTb if sample else wsT
            cstx = csts if sample else cst
            for i in range(NT):
                bk = next_bank()
                for g in range(4):
                    S.op("pe", lambda e, g=g, i=i, bk=bk: e.matmul(
                        ps[:, bk, g * 128:(g + 1) * 128], lhsT=vg[:, i, g * 128:(g + 1) * 128],
                        rhs=wsx[:, g, :], start=True, stop=True, skip_group_check=True),
                        reads=[VG[i], WSB], writes=[PS[bk]], signal=(g == 3))
                tb = i % 2
                S.op("dve", lambda e, bk=bk, tb=tb: e.tensor_tensor(
                    out=tsg[:, tb, :], in0=psf(bk), in1=cstx[:].rearrange("p g t -> p (g t)"), op=ALU.add),
                    reads=[PS[bk], WSB], writes=[TSG[tb]])
                S.op("dve", lambda e, i=i, tb=tb: e.tensor_tensor(
                    out=ybin[:, :, i * 128:(i + 1) * 128],
                    in0=tsg[:, tb, :].rearrange("p (g t) -> p g t", t=128),
                    in1=usb[:, :, i * 128:(i + 1) * 128], op=ALU.mult),
                    reads=[TSG[tb]] + [U[n] for n in range(4)], writes=[YB[i]])

            cbanks = [next_bank() for _ in range(4)]
            for cc in range(4):
                bk = cbanks[cc]
                for k in range(NDVE, KW):
                    if sample:
                        out = psf(bk, 128).rearrange("p (s r) -> p s r", r=32)
                    else:
                        out = psf(bk, T)
                    fin = (NDVE == 0 and k == KW - 1)
                    S.op("pe", lambda e, k=k, cc=cc, out=out, fin=fin: e.matmul(
                        out, lhsT=diag[:, k, cc, :], rhs=gl_tap(cc, k), start=(k == NDVE), stop=fin),
                        reads=[GL[cc], DIAGB], writes=[PS[bk]], signal=fin)
            def ev_gb_of(hh):
                def ev_gb(n, bk):
                    c = hh * 4 + n
                    S.op("act", lambda e: e.activation(
                        out=RR[:, 8 + c, 0:T], in_=psf(bk, T), func=AF.Sigmoid, bias=col("b_in", 24 + c)),
                        reads=[PS[bk], CONST], writes=[R[8 + c]])
                return ev_gb
            ch0 = fm_stage("gb0", ev_gb_of(0))
            if NDVE > 0:
                ch0(0)
            for cc in range(4):
                bk = cbanks[cc]
                if NDVE > 0:
                    S.op("pe", lambda e, cc=cc, bk=bk: e.matmul(
                        psf(bk, T), lhsT=ident_b[:], rhs=accb[cc][:, 0:T], start=False, stop=True),
                        reads=[ACCB[cc], CONST], writes=[PS[bk]], signal=True)
                S.op("act", lambda e, cc=cc, bk=bk: e.activation(
                    out=ycsb[:, cc, 0:T], in_=psf(bk, T), func=AF.Identity, bias=col("b_dw", cc)),
                    reads=[PS[bk], CONST], writes=[YC[cc]])
            if not sample and not last:
                for cc in range(4):
                    S.op("act", lambda e, cc=cc: e.activation(out=gl[:, cc, 0:30], in_=gl[:, cc, T:T + 30],
                                                             func=AF.Copy), reads=[GL[cc]], writes=[GL[cc]])
            if want_state:
                bk = next_bank()
                for cc in range(4):
                    S.op("pe", lambda e, cc=cc, bk=bk: e.transpose(
                        out=ps[:, bk, cc * 128:(cc + 1) * 128], in_=glf[:, cc, :], identity=ident_f[:]),
                        reads=[GLF, CONST], writes=[PS[bk]], signal=(cc == 3))
                S.op("act", lambda e, bk=bk: e.activation(out=stf[:], in_=psf(bk), func=AF.Copy),
                     reads=[PS[bk]], writes=[STF])
                if sample:
                    for stq in range(4):
                        S.op("pool", lambda e, stq=stq: e.dma_start(
                            out=scs[30 * stq:30 * stq + 30, :], in_=stf[32 * stq + 2:32 * stq + 32, :]),
                            reads=[STF], dma="misc")
                else:
                    S.op("pool", lambda e: e.dma_start(out=scp[:, :], in_=stf[98:128, :]),
                         reads=[STF], dma="misc")

            if NDVE == 0:
                ch0(0)
            ch0(1)
            tbanks = []
            for i in range(NT):
                bk = next_bank()
                tbanks.append(bk)
                for cc in range(4):
                    S.op("pe", lambda e, cc=cc, i=i, bk=bk: e.transpose(
                        out=psb(bk)[:, cc * 128:(cc + 1) * 128], in_=ycsb[:, cc, i * 128:(i + 1) * 128],
                        identity=ident_b[:]), reads=[YC[cc], CONST], writes=[PS[bk]], signal=(cc == 3))
                S.op("dve", lambda e, i=i, bk=bk: e.bn_stats(out=st6[:, 1, i, :], in_=psb(bk)[:, 0:512]),
                     reads=[PS[bk]], writes=[ST6[1, i]])
                S.op("dve", lambda e, i=i: e.bn_aggr(out=mv[:, 1, i, :], in_=st6[:, 1, i, :]),
                     reads=[ST6[1, i]], writes=[MV[1]])
            ln_finish(1, NT)
            for i in range(NT):
                bk = tbanks[i]
                S.op("act", lambda e, i=i, bk=bk: e.activation(
                    out=yctm[:, i, :], in_=psb(bk)[:, 0:512], func=AF.Identity, scale=lr[:, 1, i:i + 1],
                    bias=lnb[:, 1, i:i + 1]), reads=[PS[bk], LR[1], LNB[1]], writes=[YT[i]])
            for n in range(2, 4):
                ch0(n)
            ch1 = fm_stage("gb1", ev_gb_of(1))
            ch1(0)
            ch1(1)
            for cc in range(4):
                bk = next_bank()
                for i in range(NT):
                    S.op("pe", lambda e, cc=cc, i=i, bk=bk: e.transpose(
                        out=psb(bk)[:, i * 128:(i + 1) * 128], in_=yctm[:, i, cc * 128:(cc + 1) * 128],
                        identity=ident_b[:]), reads=[YT[i], CONST], writes=[PS[bk]], signal=(i == NT - 1))
                S.op("act", lambda e, cc=cc, bk=bk: e.activation(
                    out=sa[:, cc, 0:T], in_=psb(bk)[:, 0:T], func=AF.Silu, scale=col("ln_a_g", cc),
                    bias=col("ln_a_b", cc)), reads=[PS[bk], CONST], writes=[SA[cc]])
            ch1(2)
            ch1(3)
            slot_a, Wa = acquire_stage("a_out")
            slot_b, Wb = acquire_stage("b_out", hold=1)
            for c in range(8):
                bka = next_bank()
                for kc in range(4):
                    S.op("pe", lambda e, kc=kc, c=c, bk=bka: e.matmul(
                        psf(bk, T), lhsT=Wa[:, kc, c * 128:(c + 1) * 128], rhs=sa[:, kc, 0:T],
                        start=(kc == 0), stop=(kc == 3)),
                        reads=[RING[slot_a], SA[kc]], writes=[PS[bka]], signal=(kc == 3))
                bkb = next_bank()
                for kc in range(4):
                    S.op("pe", lambda e, kc=kc, c=c, bk=bkb: e.matmul(
                        psf(bk, T), lhsT=Wb[:, kc, c * 128:(c + 1) * 128], rhs=ybin[:, kc, 0:T],
                        start=(kc == 0), stop=(kc == 3)),
                        reads=[RING[slot_b]] + [YB[i] for i in range(NT)], writes=[PS[bkb]], signal=(kc == 3))
                tb = c % 2
                S.op("dve", lambda e, c=c, bk=bka, tb=tb: e.scalar_tensor_tensor(
                    out=t1[:, tb, 0:T], in0=psf(bk, T), scalar=col("b_a_out", c), in1=RR[:, c, 0:T],
                    op0=ALU.add, op1=ALU.mult), reads=[PS[bka], R[c], CONST], writes=[T1[tb]])
                S.op("dve", lambda e, c=c, bk=bkb, tb=tb: e.tensor_tensor(
                    out=t2[:, tb, 0:T], in0=psf(bk, T), in1=RR[:, 8 + c, 0:T], op=ALU.mult),
                    reads=[PS[bkb], R[8 + c]], writes=[T2[tb]])
                S.op("dve", lambda e, c=c, tb=tb: e.tensor_tensor(
                    out=hT[:, c, 0:T], in0=t1[:, tb, 0:T], in1=t2[:, tb, 0:T], op=ALU.add),
                    reads=[T1[tb], T2[tb]], writes=HTall(c))
            if debug and first:
                S.op("pool", lambda e: e.dma_start(out=dbg_m, in_=hT[:]),
                     reads=[HT[kc, i] for kc in range(8) for i in range(4)], dma="misc")
            slot0, W0 = acquire_stage("wo0")
            slot1, W1 = acquire_stage("wo1", hold=1)
            for i in range(NT):
                for h in range(2):
                    slot, W = (slot0, W0) if h == 0 else (slot1, W1)
                    bk = next_bank()
                    for kc in range(8):
                        S.op("pe", lambda e, kc=kc, i=i, bk=bk, W=W: e.matmul(
                            psf(bk), lhsT=hT[:, kc, i * 128:(i + 1) * 128], rhs=W[:, kc, :],
                            start=(kc == 0), stop=(kc == 7)),
                            reads=[RING[slot], HT[kc, i]], writes=[PS[bk]], signal=(kc == 7))
                    S.op("dve", lambda e, i=i, h=h, bk=bk: e.tensor_tensor(
                        out=xa[:, xb, i, h * 512:(h + 1) * 512], in0=psf(bk),
                        in1=xa[:, xb, i, h * 512:(h + 1) * 512], op=ALU.add),
                        reads=[PS[bk]], writes=[X[xb, i, h]])
                    if NT > 1 and ((i == 2 and h == 1) or (i == 3 and h == 0)):
                        rms_T(i - 2, "act")
                rms_pre(xb, i, 1)
            if NT == 1:
                rms_T(0)
            for s_ in range(11):
                slot, W = acquire_stage("fi%d" % s_)
                grp = []
                for jj in range(2):
                    j = 2 * s_ + jj
                    grp.append((j, next_bank(), jj * 128))
                    grp.append((j, next_bank(), 256 + jj * 128))
                if s_ == 0 and NT > 1:
                    spans = [(0, T - 256, list(range(NT - 2))), (T - 256, T - 128, [NT - 2]),
                             (T - 128, T, [NT - 1])]
                else:
                    spans = [(0, T, list(range(NT)))]
                for si, (c0, c1, tiles) in enumerate(spans):
                    for gi, (j, bk, wc) in enumerate(grp):
                        if s_ == 0 and NT > 1 and si == 0 and gi == 2:
                            rms_T(NT - 2, "dve")
                        if s_ == 0 and NT > 1 and si == 1 and gi == 0:
                            rms_T(NT - 1, "act")
                        for kc in range(8):
                            S.op("pe", lambda e, kc=kc, bk=bk, wc=wc, W=W, c0=c0, c1=c1: e.matmul(
                                ps[:, bk, c0:c1], lhsT=W[:, kc, wc:wc + 128], rhs=hT[:, kc, c0:c1],
                                start=(kc == 0), stop=(kc == 7), skip_group_check=True),
                                reads=[RING[slot]] + [HT[kc, i] for i in tiles], writes=[PS[bk]],
                                signal=(kc == 7))
                for jj in range(2):
                    j, bkg, _ = grp[2 * jj]
                    _, bku, _ = grp[2 * jj + 1]
                    tb = j % 2
                    S.op("act", lambda e, bk=bkg, tb=tb: e.activation(out=sgt[:, tb, 0:T], in_=psf(bk, T),
                                                                     func=AF.Silu),
                         reads=[PS[bkg]], writes=[SGT[tb]])
                    S.op("dve", lambda e, j=j, bk=bku, tb=tb: e.tensor_tensor(
                        out=RR[:, j, 0:T], in0=psf(bk, T), in1=sgt[:, tb, 0:T], op=ALU.mult),
                        reads=[PS[bku], SGT[tb]], writes=[R[j]])
            if hook_pre is not None:
                hook_pre()
            for h in range(2):
                if h == 1 and hook_T is not None:
                    hook_T()
                obanks = [next_bank() for _ in range(NT)]
                for q in range(3):
                    slot, W = acquire_stage("fo%d%d" % (h, q))
                    nk = 8 if q < 2 else 6
                    for i in range(NT):
                        bk = obanks[i]
                        for kk in range(nk):
                            j = 8 * q + kk
                            S.op("pe", lambda e, kk=kk, j=j, i=i, bk=bk, W=W: e.matmul(
                                psf(bk), lhsT=RR[:, j, i * 128:(i + 1) * 128], rhs=W[:, kk, :],
                                start=(j == 0), stop=(j == NJ - 1)),
                                reads=[RING[slot], R[j]], writes=[PS[bk]], signal=(kk == nk - 1))
                if h == 1 and hook_end is not None:
                    hook_end()
                for i in range(NT):
                    bk = obanks[i]
                    S.op("dve", lambda e, i=i, h=h, bk=bk: e.tensor_tensor(
                        out=xa[:, xb, i, h * 512:(h + 1) * 512], in0=psf(bk),
                        in1=xa[:, xb, i, h * 512:(h + 1) * 512], op=ALU.add),
                        reads=[PS[bk]], writes=[X[xb, i, h]])
            if debug and first:
                S.op("pool", lambda e: e.dma_start(out=dbg_sa, in_=sa[:]), reads=[SA[c] for c in range(4)], dma="misc")
                S.op("pool", lambda e: e.dma_start(out=dbg_yb, in_=ybin[:]), reads=[YB[c] for c in range(4)], dma="misc")
                S.op("pool", lambda e: e.dma_start(out=dbg_h2, in_=hT[:]),
                     reads=[HT[kc, i] for kc in range(8) for i in range(4)], dma="misc")
                S.op("pool", lambda e: e.dma_start(out=dbg_hid, in_=RR[:]), reads=[R[j] for j in range(NJ)], dma="misc")
            for i in range(NT):
                S.op("act", lambda e, i=i: e.activation(out=junk[:], in_=xa[:, xb, i, :], func=AF.Square,
                                                       accum_out=ss[:, 2, i:i + 1]),
                     reads=xall(xb, i), writes=[SSb[2, i]])
                S.op("act", lambda e, i=i: e.activation(out=sd[:, 2, i:i + 1], in_=ss[:, 2, i:i + 1],
                                                       func=AF.Sqrt, scale=1.0 / D, bias=epst[:]),
                     reads=[SSb[2, i], CONST], writes=[SDb[2, i]])
                S.op("dve", lambda e, i=i: e.reciprocal(out=rs_[:, 2, i:i + 1], in_=sd[:, 2, i:i + 1]),
                     reads=[SDb[2, i]], writes=[RSb[2, i]])
                S.op("dve", lambda e, i=i: e.scalar_tensor_tensor(
                    out=xa[:, xb, i, :], in0=xa[:, xb, i, :], scalar=rs_[:, 2, i:i + 1], in1=gfb[:],
                    op0=ALU.mult, op1=ALU.mult), reads=[RSb[2, i], BC3], writes=xall(xb, i))
                if sample:
                    dst = ys[:, :]
                else:
                    dst = yp[p * TP + i * 128:p * TP + (i + 1) * 128, :]
                S.op("pool", lambda e, i=i, dst=dst: e.dma_start(out=dst, in_=xa[:, xb, i, :]),
                     reads=xall(xb, i), dma="yout%d" % xb)

        def ln_finish(which, NT):
            S.op("act", lambda e: e.activation(out=lsd[:, which, 0:NT], in_=mv[:, which, 0:NT, 1],
                                               func=AF.Sqrt, scale=1.0, bias=epst[:]),
                 reads=[MV[which], CONST], writes=[LSD[which]])
            S.op("dve", lambda e: e.reciprocal(out=lr[:, which, 0:NT], in_=lsd[:, which, 0:NT]),
                 reads=[LSD[which]], writes=[LR[which]])
            S.op("dve", lambda e: e.scalar_tensor_tensor(
                out=lnb[:, which, 0:NT], in0=mv[:, which, 0:NT, 0], scalar=-1.0, in1=lr[:, which, 0:NT],
                op0=ALU.mult, op1=ALU.mult), reads=[MV[which], LR[which]], writes=[LNB[which]])

        def load_x(kind, p, xb):
            wr = [X[xb, i, h] for i in range(4) for h in range(2)]
            if kind == "sample":
                S.op("pool", lambda e: e.dma_start(out=xa[:, xb, 0, :], in_=xs[:, :]),
                     writes=[X[xb, 0, 0], X[xb, 0, 1]], dma="xin%d" % xb)
            else:
                src = xp[p * TP:(p + 1) * TP, :].rearrange("(i q) d -> q i d", q=128)
                S.op("pool", lambda e: e.dma_start(out=xa[:, xb], in_=src), writes=wr, dma="xin%d" % xb)

        passes = [("prompt", p) for p in range(n_prompt_pass)]
        if do_sample:
            passes.append(("sample", 0))
        for _ in passes:
            load_plan.extend(range(NSTAGE))

        def nt_of(kind):
            return 1 if kind == "sample" else 4

        setup_early()
        load_x(passes[0][0], passes[0][1], 0)
        for i in range(nt_of(passes[0][0])):
            rms_pre(0, i, 0)
        for i in range(nt_of(passes[0][0])):
            rms_T(i)
        setup_rest_dma()

        for n, (kind, p) in enumerate(passes):
            xb = n % 2
            hook_pre = hook_T = hook_end = None
            if n + 1 < len(passes):
                nkind, np_ = passes[n + 1]
                nxb = (n + 1) % 2
                pre = lambda nkind=nkind, nxb=nxb: [rms_pre(nxb, i, 0) for i in range(nt_of(nkind))]
                tr = lambda nkind=nkind: [rms_T(i) for i in range(nt_of(nkind))]
                if n == 0:
                    hook_T = lambda nkind=nkind, np_=np_, nxb=nxb, pre=pre: (load_x(nkind, np_, nxb), pre())
                    hook_end = tr
                else:
                    load_x(nkind, np_, nxb)
                    hook_pre = pre
                    hook_T = tr
            run_pass(kind, p, xb, hook_pre, hook_T, hook_end)

        S.wait_all("pool", ["yout0", "yout1", "misc"])
        S.wait_all("sp", ["yout0", "yout1", "misc", "wout0", "wout1", "wout2", "wout3"])

        with nc.Block() as block:
            S.emit(block)
    _CACHE["sched"] = S
    return nc


def _get_program():
    if "nc" not in _CACHE:
        _CACHE["nc"] = build_program()
    return _CACHE["nc"]


def make_in_maps(inputs):
    f = lambda a: np.ascontiguousarray(np.asarray(a, dtype=np.float32))
    shared = {
        "g_mix": f(inputs["g_mix"][0]), "w_in": f(inputs["w_in"][0]), "b_in": f(inputs["b_in"][0]),
        "w_dw": f(inputs["w_dw"][0]), "b_dw": f(inputs["b_dw"][0]), "ln_a_g": f(inputs["ln_a_g"][0]),
        "ln_a_b": f(inputs["ln_a_b"][0]), "w_a_out": f(inputs["w_a_out"][0]),
        "b_a_out": f(inputs["b_a_out"][0]), "ln_v_g": f(inputs["ln_v_g"][0]),
        "ln_v_b": f(inputs["ln_v_b"][0]), "w_s": f(inputs["w_s"][0]), "b_s": f(inputs["b_s"][0]),
        "w_b_out": f(inputs["w_b_out"][0]), "w_o": f(inputs["w_o"][0]), "g_ffn": f(inputs["g_ffn"][0]),
        "w_ffn_in": f(inputs["w_ffn_in"][0]), "w_ffn_out": f(inputs["w_ffn_out"][0]),
        "g_final": f(inputs["g_final"]),
    }
    x_prompt = np.asarray(inputs["x_prompt"], dtype=np.float32)
    x_sample = np.asarray(inputs["x_sample"], dtype=np.float32)
    cache_conv = np.asarray(inputs["cache_conv"], dtype=np.float32)
    maps = []
    for c in range(NCORE):
        m = dict(shared)
        m["xp"] = np.ascontiguousarray(x_prompt[c])
        m["xs"] = np.ascontiguousarray(x_sample[4 * c:4 * c + 4].reshape(128, D))
        m["cch"] = np.ascontiguousarray(cache_conv[0, 4 * c:4 * c + 4].reshape(120, CC))
        maps.append(m)
    return maps


def kernel(**inputs):
    nc = _get_program()
    maps = make_in_maps(inputs)
    res = run_bass_kernel_spmd(nc, maps, core_ids=list(range(NCORE)))
    r = res.results
    y_prompt = np.stack([np.asarray(r[c]["yp"], dtype=np.float32) for c in range(NCORE)], axis=0)
    y_sample = np.concatenate([np.asarray(r[c]["ys"], dtype=np.float32).reshape(4, 32, D)
                               for c in range(NCORE)], axis=0)
    scp_ = np.stack([np.asarray(r[c]["scp"], dtype=np.float32) for c in range(NCORE)], axis=0)[None]
    scs_ = np.concatenate([np.asarray(r[c]["scs"], dtype=np.float32).reshape(4, 30, CC)
                           for c in range(NCORE)], axis=0)[None]
    sv_ = np.concatenate([np.asarray(r[c]["svo"], dtype=np.float32).reshape(4, 32, CS)
                          for c in range(NCORE)], axis=0)[None]
    return (y_prompt, y_sample, scp_, scs_, sv_)
```

```python
import numpy as np
from contextlib import ExitStack

import concourse.bass as bass
import concourse.mybir as mybir
from concourse.bass_utils import run_bass_kernel_spmd

F32 = mybir.dt.float32
BF16 = mybir.dt.bfloat16
AF = mybir.ActivationFunctionType
ALU = mybir.AluOpType

D = 1024
SEQ = 8192
NCORE = 8
CC = 512
CS = 512
KW = 31
DFF = 2816
NJ = DFF // 128
EPS = 1e-6
TP = 512
NPASS_PROMPT = SEQ // TP


class Buf:
    __slots__ = ("name", "w", "r")

    def __init__(self, name):
        self.name = name
        self.w = None
        self.r = []


class Sched:
    ENG = ("pe", "act", "dve", "pool", "sp")

    def __init__(self):
        self.ops = {e: [] for e in self.ENG}
        self.count = {e: 0 for e in self.ENG}
        self.waited = {e: {} for e in self.ENG}
        self.dcount = {}
        self.sem = {}
        self.pending_pe = False

    def add_dma_sem(self, key):
        self.dcount[key] = 0

    def _resolve(self, tok):
        sk, v = tok
        if sk in self.dcount:
            return sk, self.dcount[sk]
        return sk, v

    def op(self, eng, fn, reads=(), writes=(), signal=True, dma=None, ndma=1):
        deps = {}
        wset = set(id(b) for b in writes)
        for b in reads:
            if b.w is not None:
                sk, v = self._resolve(b.w)
                deps[sk] = max(deps.get(sk, 0), v)
        for b in writes:
            if b.w is not None:
                sk, v = self._resolve(b.w)
                deps[sk] = max(deps.get(sk, 0), v)
            for t in b.r:
                sk, v = self._resolve(t)
                deps[sk] = max(deps.get(sk, 0), v)
        waits = []
        wd = self.waited[eng]
        for sk, v in deps.items():
            if eng == "pe" and sk == "pe":
                continue
            if wd.get(sk, 0) >= v:
                continue
            wd[sk] = v
            waits.append((sk, v))
        if dma is not None:
            self.dcount[dma] += 16 * ndma
            tok = (dma, self.dcount[dma])
            inc = (dma, 16)
        elif signal:
            self.count[eng] += 1
            tok = (eng, self.count[eng])
            inc = (eng, 1)
            if eng == "pe":
                self.pending_pe = False
        else:
            tok = (eng, self.count[eng] + 1)
            inc = None
            if eng == "pe":
                self.pending_pe = True
        self.ops[eng].append((waits, fn, inc, ndma if dma is not None else 1))
        for b in writes:
            b.w = tok
            b.r = []
        for b in reads:
            if id(b) not in wset:
                b.r.append(tok)
        return tok

    def wait_all(self, eng, keys):
        waits = []
        for sk in keys:
            v = self.dcount[sk] if sk in self.dcount else self.count[sk]
            if v > 0:
                waits.append((sk, v))
        self.ops[eng].append((waits, None, None, 1))

    def emit(self, block):
        assert not self.pending_pe
        names = {"pe": "tensor", "act": "scalar", "dve": "vector", "pool": "gpsimd", "sp": "sync"}
        for eng in self.ENG:
            ops = self.ops[eng]
            if not ops:
                continue

            def body(e, ops=ops):
                for waits, fn, inc, _n in ops:
                    for sk, v in waits:
                        e.wait_ge(self.sem[sk], v)
                    if fn is None:
                        continue
                    ins = fn(e)
                    if inc is not None:
                        if isinstance(ins, (list, tuple)):
                            for i_ in ins:
                                i_.then_inc(self.sem[inc[0]], inc[1])
                        else:
                            ins.then_inc(self.sem[inc[0]], inc[1])

            getattr(block, names[eng])(body)


def build_stage_table():
    st = []
    for name, c0 in (("a_gate", 512), ("a_lin", 0), ("u", 1024), ("v", 1536),
                     ("ga0", 2048), ("ga1", 2560), ("gb0", 3072), ("gb1", 3584)):
        st.append(dict(name=name, src="w_in", nk=8, k0=0, cols=[(c0, 512)], scale="g_mix"))
    st.append(dict(name="a_out", src="w_a_out", nk=4, k0=0, cols=[(0, 1024)], scale=None))
    st.append(dict(name="b_out", src="w_b_out", nk=4, k0=0, cols=[(0, 1024)], scale=None))
    for h in range(2):
        st.append(dict(name=f"wo{h}", src="w_o", nk=8, k0=0, cols=[(h * 512, 512)], scale=None))
    for s in range(11):
        st.append(dict(name=f"fi{s}", src="w_ffn_in", nk=8, k0=0,
                       cols=[(256 * s, 256), (DFF + 256 * s, 256)], scale="g_ffn"))
    for h in range(2):
        for q in range(3):
            nk = 8 if q < 2 else 6
            st.append(dict(name=f"fo{h}{q}", src="w_ffn_out", nk=nk, k0=8 * q,
                           cols=[(h * 512, 512)], scale=None))
    for i, s in enumerate(st):
        s["idx"] = i
        s["ncol"] = sum(c[1] for c in s["cols"])
        assert s["nk"] * s["ncol"] <= 4096
    return st


STAGES = build_stage_table()
NSTAGE = len(STAGES)
STAGE_BY_NAME = {s["name"]: s for s in STAGES}

VEC_ROWS = {}
_r = 0
for _name, _n in (("b_in", 32), ("b_dw", 4), ("ln_a_g", 4), ("ln_a_b", 4), ("b_a_out", 8),
                  ("g_mix", 8), ("g_ffn", 8), ("ln_v_b", 4)):
    VEC_ROWS[_name] = (_r, _n)
    _r += _n
NVROW = _r


_CACHE = {}


def build_program(n_prompt_pass=NPASS_PROMPT, do_sample=True, debug=False):
    nc = bass.Bass("TRN2", target_bir_lowering=False)
    S = Sched()

    def din(name, shape):
        return nc.dram_tensor(name, list(shape), F32, kind="ExternalInput").ap()

    def dout(name, shape):
        return nc.dram_tensor(name, list(shape), F32, kind="ExternalOutput").ap()

    xp = din("xp", [SEQ, D])
    xs = din("xs", [128, D])
    cch = din("cch", [120, CC])
    g_mix = din("g_mix", [D])
    w_in = din("w_in", [D, 4096])
    b_in = din("b_in", [4096])
    w_dw = din("w_dw", [KW, CC])
    b_dw = din("b_dw", [CC])
    ln_a_g = din("ln_a_g", [CC])
    ln_a_b = din("ln_a_b", [CC])
    w_a_out = din("w_a_out", [CC, D])
    b_a_out = din("b_a_out", [D])
    ln_v_g = din("ln_v_g", [CS])
    ln_v_b = din("ln_v_b", [CS])
    w_s = din("w_s", [4, 128, 128])
    b_s = din("b_s", [4, 128])
    w_b_out = din("w_b_out", [CS, D])
    w_o = din("w_o", [D, D])
    g_ffn = din("g_ffn", [D])
    w_ffn_in = din("w_ffn_in", [D, 2 * DFF])
    w_ffn_out = din("w_ffn_out", [DFF, D])
    g_final = din("g_final", [D])
    dram_in = dict(w_in=w_in, w_a_out=w_a_out, w_b_out=w_b_out, w_o=w_o, w_ffn_in=w_ffn_in,
                   w_ffn_out=w_ffn_out, b_in=b_in, b_dw=b_dw, ln_a_g=ln_a_g, ln_a_b=ln_a_b,
                   b_a_out=b_a_out, g_mix=g_mix, g_ffn=g_ffn, ln_v_b=ln_v_b)

    yp = dout("yp", [SEQ, D])
    ys = dout("ys", [128, D])
    scp = dout("scp", [30, CC])
    scs = dout("scs", [120, CC])
    svo = dout("svo", [128, CS])

    wscr = nc.dram_tensor("wscr", [NSTAGE, 128, 4096], BF16, kind="Internal").ap()
    if debug:
        dbg_sa = nc.dram_tensor("dbg_sa", [128, 4, TP], BF16, kind="ExternalOutput").ap()
        dbg_yb = nc.dram_tensor("dbg_yb", [128, 4, TP], BF16, kind="ExternalOutput").ap()
        dbg_h2 = nc.dram_tensor("dbg_h2", [128, 8, TP], BF16, kind="ExternalOutput").ap()
        dbg_hid = nc.dram_tensor("dbg_hid", [128, NJ, TP], BF16, kind="ExternalOutput").ap()
        dbg_m = nc.dram_tensor("dbg_m", [128, 8, TP], BF16, kind="ExternalOutput").ap()

    es = ExitStack()

    def sb(name, shape, dt):
        return es.enter_context(nc.sbuf_tensor(name, list(shape), dt))

    with es:
        xa = sb("xa", [128, 2, 4, D], F32)
        htm = sb("htm", [128, 4, D], BF16)
        junk = sb("junk", [128, D], BF16)
        hT = sb("hT", [128, 8, TP], BF16)
        RR = sb("RR", [128, NJ, TP], BF16)
        gl = sb("gl", [128, 4, 544], BF16)
        glf = sb("glf", [128, 4, 128], F32)
        usb = sb("usb", [128, 4, TP], BF16)
        vhat = sb("vhat", [128, 2, CS], F32)
        vnf = sb("vnf", [128, CS], F32)
        vg = sb("vg", [128, 4, CS], BF16)
        ycsb = sb("ycsb", [128, 4, TP], BF16)
        yctm = sb("yctm", [128, 4, CC], BF16)
        sa = sb("sa", [128, 4, TP], BF16)
        tsg = sb("tsg", [128, 2, CS], F32)
        ybin = sb("ybin", [128, 4, TP], BF16)
        t1 = sb("t1", [128, 2, TP], BF16)
        t2 = sb("t2", [128, 2, TP], BF16)
        sgt = sb("sgt", [128, 2, TP], BF16)
        stf = sb("stf", [128, CC], F32)
        hst = sb("hst", [128, CC], F32)
        ring = sb("ring", [128, 4, 4096], BF16)
        diag = sb("diag", [128, KW, 4, 128], BF16)
        ident_b = sb("ident_b", [128, 128], BF16)
        ident_f = sb("ident_f", [128, 128], F32)
        ones_f = sb("ones_f", [128, 128], F32)
        wsT = sb("wsT", [128, 4, 128], BF16)
        wsTb = sb("wsTb", [128, 4, 128], BF16)
        cst = sb("cst", [128, 4, 128], F32)
        csts = sb("csts", [128, 4, 128], F32)
        gvb = sb("gvb", [128, CS], F32)
        bvb = sb("bvb", [128, CS], F32)
        bsb = sb("bsb", [128, 4, 128], F32)
        gfb = sb("gfb", [128, D], F32)
        vrows = sb("vrows", [128, 128], F32)
        vcol = sb("vcol", [128, 128], F32)
        wcol = sb("wcol", [128, 4, 32], F32)
        bvrow = sb("bvrow", [1, CS], BF16)
        ones_row = sb("ones_row", [1, 128], BF16)
        epst = sb("epst", [128, 1], F32)
        ss = sb("ss", [128, 3, 4], F32)
        sd = sb("sd", [128, 3, 4], F32)
        rs_ = sb("rs_", [128, 3, 4], F32)
        st6 = sb("st6", [128, 2, 4, 6], F32)
        mv = sb("mv", [128, 2, 4, 2], F32)
        lsd = sb("lsd", [128, 2, 4], F32)
        lr = sb("lr", [128, 2, 4], F32)
        lnb = sb("lnb", [128, 2, 4], F32)

        ps = es.enter_context(nc.psum_tensor("ps", [128, 8, 512], F32))
        wsn = stf[:].rearrange("p (g s) -> p g s", s=128)
        wsTf = vnf[:].rearrange("p (g s) -> p g s", s=128)
        wdwr = hst
        bvrow_f = glf[0:1, :, :].rearrange("p a b -> p (a b)")

        for e in Sched.ENG:
            S.sem[e] = es.enter_context(nc.semaphore("s_" + e))
        dma_keys = (["ring%d" % i for i in range(4)] + ["xin0", "xin1", "yout0", "yout1",
                    "cst", "c1", "c2", "c3", "c4", "c5", "stg0", "stg1", "stg2", "stg3", "wout0", "wout1", "wout2", "wout3", "misc"])
        for k in dma_keys:
            S.sem[k] = es.enter_context(nc.semaphore("d_" + k))
            S.add_dma_sem(k)

        B = {}

        def bufs(name, *dims):
            if not dims:
                B[name] = Buf(name)
                return B[name]
            import itertools
            arr = {}
            for idx in itertools.product(*[range(d) for d in dims]):
                arr[idx if len(idx) > 1 else idx[0]] = Buf(f"{name}{idx}")
            B[name] = arr
            return arr

        X = bufs("X", 2, 4, 2)
        HTM = bufs("HTM", 4)
        HT = bufs("HT", 8, 4)
        R = bufs("R", NJ)
        GL = bufs("GL", 4)
        GLF = bufs("GLF")
        U = bufs("U", 4)
        VH = bufs("VH", 2)
        VNF = bufs("VNF")
        VG = bufs("VG", 4)
        YC = bufs("YC", 4)
        YT = bufs("YT", 4)
        SA = bufs("SA", 4)
        TSG = bufs("TSG", 2)
        YB = bufs("YB", 4)
        T1 = bufs("T1", 2)
        T2 = bufs("T2", 2)
        SGT = bufs("SGT", 2)
        STF = bufs("STF")
        HST = bufs("HST")
        RING = bufs("RING", 4)
        PS = bufs("PS", 8)
        CONST = bufs("CONST")
        DIAGB = bufs("DIAGB")
        WSB = bufs("WSB")
        BC = bufs("BC")
        BC1 = bufs("BC1")
        BC2 = bufs("BC2")
        BC3 = bufs("BC3")
        VR = {r0: Buf("VR%d" % r0) for (r0, n_) in VEC_ROWS.values()}
        SSb = bufs("SS", 3, 4)
        SDb = bufs("SD", 3, 4)
        RSb = bufs("RS", 3, 4)
        ST6 = bufs("ST6", 2, 4)
        MV = bufs("MV", 2)
        LSD = bufs("LSD", 2)
        LR = bufs("LR", 2)
        LNB = bufs("LNB", 2)
        TMPB = bufs("TMPB", 8)
        WSCR = bufs("WSCR", NSTAGE)

        def xall(b, i):
            return [X[b, i, 0], X[b, i, 1]]

        psi = [0]

        def next_bank():
            b = psi[0] % 8
            psi[0] += 1
            return b

        def psf(b, n=512):
            return ps[:, b, 0:n]

        def psb(b):
            return ps[:, b, :].bitcast(BF16)

        def setup_early():
            C = [CONST]
            S.op("dve", lambda e: e.memset(epst[:], EPS), writes=C)
            S.op("dve", lambda e: e.memset(ones_f[:], 1.0), writes=C)
            S.op("dve", lambda e: e.memset(ones_row[:], 1.0), writes=C)
            S.op("dve", lambda e: e.memset(vrows[:], 0.0), writes=[TMPB[0]])
            S.op("pool", lambda e: e.affine_select(out=ident_f[:], in_=ones_f[:], pattern=[[-1, 128]],
                                                   compare_op=ALU.is_equal, fill=0.0, base=0,
                                                   channel_multiplier=1), reads=C, writes=C)
            S.op("dve", lambda e: e.tensor_copy(out=ident_b[:], in_=ident_f[:]), reads=C, writes=C)
            for name, (r0, n) in VEC_ROWS.items():
                src = dram_in[name].rearrange("(n p) -> n p", p=128)
                S.op("sp", lambda e, src=src, r0=r0, n=n: e.dma_start(out=vrows[r0:r0 + n, :], in_=src),
                     reads=[TMPB[0]], writes=[VR[r0]], dma="cst")
            b0 = next_bank()
            S.op("pe", lambda e: e.transpose(out=psf(b0, 128), in_=vrows[:], identity=ident_f[:]),
                 reads=[TMPB[0]] + list(VR.values()) + C, writes=[PS[b0]])
            S.op("act", lambda e: e.activation(out=vcol[:], in_=psf(b0, 128), func=AF.Copy),
                 reads=[PS[b0]], writes=C)

        def setup_rest_dma():
            C = [CONST]
            S.op("dve", lambda e: e.memset(wdwr[:], 0.0), writes=[HST])
            S.op("dve", lambda e: e.memset(wsn[:], 0.0), writes=[STF])
            S.op("pool", lambda e: e.dma_start(out=wdwr[0:KW, :], in_=w_dw), writes=[HST], dma="c1")
            S.op("pool", lambda e: e.dma_start(out=wsn[:], in_=w_s.rearrange("g t s -> t g s")),
                 writes=[STF], dma="c2")
            S.op("pool", lambda e: e.dma_start(
                out=bsb[:], in_=b_s.rearrange("g t -> (g t)").partition_broadcast(128)),
                writes=[TMPB[1]], dma="c3")
            S.op("pool", lambda e: e.dma_start(out=bvrow_f[:], in_=b_in[1536:2048].rearrange("(o n) -> o n", o=1)),
                 writes=[GLF], dma="c4")
            S.op("pool", lambda e: e.dma_start(out=gvb[:], in_=ln_v_g.partition_broadcast(128)),
                 writes=[BC1], dma="c5")
            S.op("pool", lambda e: e.dma_start(out=bvb[:], in_=ln_v_b.partition_broadcast(128)),
                 writes=[BC2], dma="c5")
            S.op("pool", lambda e: e.dma_start(out=gfb[:], in_=g_final.partition_broadcast(128)),
                 writes=[BC3], dma="c5")

        def setup_rest_compute():
            C = [CONST]
            S.op("dve", lambda e: e.tensor_copy(out=bvrow[:], in_=bvrow_f[:]), reads=[GLF], writes=[BC])
            b1 = next_bank()
            for cc in range(4):
                S.op("pe", lambda e, cc=cc: e.transpose(out=ps[:, b1, cc * 128:(cc + 1) * 128],
                                                       in_=wdwr[:, cc * 128:(cc + 1) * 128],
                                                       identity=ident_f[:]),
                     reads=[HST] + C, writes=[PS[b1]], signal=(cc == 3))
            S.op("act", lambda e: e.activation(
                out=wcol[:], in_=ps[:, b1, :].rearrange("p (c k) -> p c k", k=128)[:, :, 0:32],
                func=AF.Copy), reads=[PS[b1]], writes=[DIAGB])
            S.op("pool", lambda e: e.affine_select(out=wsn[:], in_=wsn[:], pattern=[[0, 4], [-1, 128]],
                                                   compare_op=ALU.is_ge, fill=0.0, base=0,
                                                   channel_multiplier=1), reads=[STF], writes=[STF])
            for cc in range(4):
                S.op("pool", lambda e, cc=cc: e.affine_select(
                    out=diag[:, :, cc, :], in_=wcol[:, cc, 0:KW].unsqueeze(2).to_broadcast([128, KW, 128]),
                    pattern=[[0, KW], [-1, 128]], compare_op=ALU.is_equal, fill=0.0, base=0,
                    channel_multiplier=1), reads=[DIAGB], writes=[DIAGB])
            b2 = next_bank()
            for g in range(4):
                S.op("pe", lambda e, g=g: e.transpose(out=ps[:, b2, g * 128:(g + 1) * 128],
                                                     in_=wsn[:, g, :], identity=ident_f[:]),
                     reads=[STF] + C, writes=[PS[b2]], signal=(g == 3))
            S.op("act", lambda e: e.activation(out=wsTf[:], in_=ps[:, b2, :].rearrange("p (g t) -> p g t", t=128),
                                               func=AF.Copy), reads=[PS[b2]], writes=[VNF])
            S.op("dve", lambda e: e.tensor_copy(out=wsT[:], in_=wsTf[:]), reads=[VNF], writes=[WSB])
            b3 = next_bank()
            for g in range(4):
                S.op("pe", lambda e, g=g: e.matmul(ps[:, b3, g * 128:(g + 1) * 128], lhsT=ones_f[:],
                                                  rhs=wsTf[:, g, :], start=True, stop=True,
                                                  skip_group_check=True),
                     reads=[VNF] + C, writes=[PS[b3]], signal=(g == 3))
            for g in range(4):
                S.op("dve", lambda e, g=g: e.scalar_tensor_tensor(
                    out=cst[:, g, :], in0=ps[:, b3, g * 128:(g + 1) * 128],
                    scalar=bvcol(g), in1=bsb[:, g, :], op0=ALU.mult, op1=ALU.add),
                    reads=[PS[b3], TMPB[1]] + C, writes=[WSB])

        def setup_sample_consts():
            C = [CONST]
            for stq in range(4):
                S.op("dve", lambda e, stq=stq: e.tensor_copy(out=csts[:, :, 32 * stq:32 * stq + 32],
                                                            in_=cst[:, :, 0:32]), reads=[WSB], writes=[WSB])
            S.op("dve", lambda e: e.memset(wsn[:], 0.0), writes=[STF])
            for stq in range(4):
                S.op("pool", lambda e, stq=stq: e.dma_start(
                    out=wsn[32 * stq:32 * stq + 32, :, 32 * stq:32 * stq + 32],
                    in_=w_s[:, 0:32, 0:32].rearrange("g t s -> t g s")), reads=[STF], writes=[TMPB[2 + stq]],
                    dma="c2")
            S.op("pool", lambda e: e.affine_select(out=wsn[:], in_=wsn[:], pattern=[[0, 4], [-1, 128]],
                                                   compare_op=ALU.is_ge, fill=0.0, base=0,
                                                   channel_multiplier=1), reads=[STF] + [TMPB[2 + q] for q in range(4)],
                 writes=[STF])
            b4 = next_bank()
            for g in range(4):
                S.op("pe", lambda e, g=g: e.transpose(out=ps[:, b4, g * 128:(g + 1) * 128],
                                                     in_=wsn[:, g, :], identity=ident_f[:]),
                     reads=[STF] + C, writes=[PS[b4]], signal=(g == 3))
            S.op("act", lambda e: e.activation(out=wsTb[:], in_=ps[:, b4, :].rearrange("p (g t) -> p g t", t=128),
                                               func=AF.Copy), reads=[PS[b4]], writes=[WSB])

        def bvcol(g):
            return col("ln_v_b", g)

        def col(name, j):
            r0, n = VEC_ROWS[name]
            assert j < n
            return vcol[:, r0 + j:r0 + j + 1]


        wstate = dict(next_load=0, nloads=0)
        load_plan = []

        stg_flat = xa[:, 1].rearrange("p a b -> p (a b)")
        stg_ctr = [0]

        def produce_stage(gidx):
            sidx = load_plan[gidx]
            s = STAGES[sidx]
            slot = gidx % 4
            nk, ncol = s["nk"], s["ncol"]
            n = nk * ncol
            src = dram_in[s["src"]]
            kph = 2048 // ncol
            for k0h in range(0, nk, kph):
                k1h = min(nk, k0h + kph)
                q = stg_ctr[0] % 2
                stg_ctr[0] += 1
                sbufs = [X[1, 2 * q + t, h] for t in range(2) for h in range(2)]
                sview = stg_flat[:, 2048 * q:2048 * q + (k1h - k0h) * ncol].rearrange("p (k n) -> p k n", n=ncol)
                pieces = []
                pos = 0
                for (c0, cn) in s["cols"]:
                    sap = src[(s["k0"] + k0h) * 128:(s["k0"] + k1h) * 128, c0:c0 + cn].rearrange(
                        "(k p) n -> p k n", p=128)
                    pieces.append((sap, pos, cn))
                    pos += cn
                S.op("sp", lambda e, pieces=pieces, sview=sview: [
                    e.dma_start(out=sview[:, :, pos:pos + cn], in_=sap) for (sap, pos, cn) in pieces],
                    writes=sbufs, dma="stg%d" % q, ndma=len(pieces))
                sc = s["scale"]
                for kc in range(k0h, k1h):
                    dst = ring[:, slot, kc * ncol:(kc + 1) * ncol]
                    srcv = sview[:, kc - k0h, :]
                    if False:
                        if sc is not None:
                            S.op("dve", lambda e, dst=dst, srcv=srcv, sc=sc, kc=kc: e.tensor_scalar(
                                out=dst, in0=srcv, scalar1=col(sc, kc), scalar2=None, op0=ALU.mult),
                                reads=sbufs + [CONST], writes=[RING[slot]])
                        else:
                            S.op("dve", lambda e, dst=dst, srcv=srcv: e.tensor_copy(out=dst, in_=srcv),
                                 reads=sbufs, writes=[RING[slot]])
                    else:
                        if sc is not None:
                            S.op("act", lambda e, dst=dst, srcv=srcv, sc=sc, kc=kc: e.activation(
                                out=dst, in_=srcv, func=AF.Copy, scale=col(sc, kc)),
                                reads=sbufs + [CONST], writes=[RING[slot]])
                        else:
                            S.op("act", lambda e, dst=dst, srcv=srcv: e.activation(out=dst, in_=srcv, func=AF.Copy),
                                 reads=sbufs, writes=[RING[slot]])
            S.op("pool", lambda e: e.dma_start(out=wscr[sidx, :, 0:n], in_=ring[:, slot, 0:n]),
                 reads=[RING[slot]], writes=[WSCR[sidx]], dma="wout%d" % slot)

        def issue_load(gidx):
            if gidx < NSTAGE:
                return produce_stage(gidx)
            sidx = load_plan[gidx]
            s = STAGES[sidx]
            slot = gidx % 4
            n = s["nk"] * s["ncol"]
            S.op("sp", lambda e: e.dma_start(out=ring[:, slot, 0:n], in_=wscr[sidx, :, 0:n]),
                 reads=[WSCR[sidx]], writes=[RING[slot]], dma="ring%d" % slot)

        def acquire_stage(name, hold=0):
            g = wstate["nloads"]
            assert STAGES[load_plan[g]]["name"] == name, (STAGES[load_plan[g]]["name"], name)
            while wstate["next_load"] <= min(g + 3 - hold, len(load_plan) - 1):
                issue_load(wstate["next_load"])
                wstate["next_load"] += 1
            wstate["nloads"] += 1
            s = STAGES[load_plan[g]]
            slot = g % 4
            view = ring[:, slot, 0:s["nk"] * s["ncol"]].rearrange("p (k n) -> p k n", n=s["ncol"])
            return slot, view

        def rms_pre(xb, i, which):
            S.op("act", lambda e: e.activation(out=junk[:], in_=xa[:, xb, i, :], func=AF.Square,
                                               accum_out=ss[:, which, i:i + 1]),
                 reads=xall(xb, i), writes=[SSb[which, i]])
            S.op("act", lambda e: e.activation(out=sd[:, which, i:i + 1], in_=ss[:, which, i:i + 1],
                                               func=AF.Sqrt, scale=1.0 / D, bias=epst[:]),
                 reads=[SSb[which, i], CONST], writes=[SDb[which, i]])
            S.op("dve", lambda e: e.reciprocal(out=rs_[:, which, i:i + 1], in_=sd[:, which, i:i + 1]),
                 reads=[SDb[which, i]], writes=[RSb[which, i]])
            S.op("dve", lambda e: e.tensor_scalar(
                out=htm[:, i, :], in0=xa[:, xb, i, :], scalar1=rs_[:, which, i:i + 1], scalar2=None,
                op0=ALU.mult), reads=xall(xb, i) + [RSb[which, i]], writes=[HTM[i]])

        def rms_T(i, eng="act"):
            bk = next_bank()
            for kc in range(8):
                S.op("pe", lambda e, kc=kc, bk=bk: e.transpose(
                    out=psb(bk)[:, kc * 128:(kc + 1) * 128], in_=htm[:, i, kc * 128:(kc + 1) * 128],
                    identity=ident_b[:]), reads=[HTM[i], CONST], writes=[PS[bk]], signal=(kc == 7))
            if eng == "both":
                S.op("act", lambda e, bk=bk: e.activation(
                    out=hT[:, 0:4, i * 128:(i + 1) * 128],
                    in_=psb(bk)[:, 0:512].rearrange("p (k t) -> p k t", t=128), func=AF.Copy),
                    reads=[PS[bk]], writes=[HT[kc, i] for kc in range(4)])
                S.op("dve", lambda e, bk=bk: e.tensor_copy(
                    out=hT[:, 4:8, i * 128:(i + 1) * 128],
                    in_=psb(bk)[:, 512:1024].rearrange("p (k t) -> p k t", t=128)),
                    reads=[PS[bk]], writes=[HT[kc, i] for kc in range(4, 8)])
            elif eng == "act":
                S.op("act", lambda e, bk=bk: e.activation(
                    out=hT[:, :, i * 128:(i + 1) * 128],
                    in_=psb(bk).rearrange("p (k t) -> p k t", t=128), func=AF.Copy),
                    reads=[PS[bk]], writes=[HT[kc, i] for kc in range(8)])
            else:
                S.op("dve", lambda e, bk=bk: e.tensor_copy(
                    out=hT[:, :, i * 128:(i + 1) * 128],
                    in_=psb(bk).rearrange("p (k t) -> p k t", t=128)),
                    reads=[PS[bk]], writes=[HT[kc, i] for kc in range(8)])

        def run_pass(kind, p, xb, hook_pre=None, hook_T=None, hook_end=None):
            sample = kind == "sample"
            NT = 1 if sample else 4
            T = 128 * NT
            first = (not sample) and p == 0
            last = (not sample) and p == NPASS_PROMPT - 1
            want_state = sample or last
            HTall = lambda kc: [HT[kc, i] for i in range(NT)]

            if first:
                S.op("dve", lambda e: e.memset(gl[:, :, 0:30], 0.0), writes=[GL[c] for c in range(4)])
            if sample:
                S.op("pool", lambda e: e.dma_start(out=hst[0:120, :], in_=cch), writes=[HST], dma="misc")
                bk = next_bank()
                for cc in range(4):
                    S.op("pe", lambda e, cc=cc, bk=bk: e.transpose(
                        out=ps[:, bk, cc * 120:(cc + 1) * 120], in_=hst[0:120, cc * 128:(cc + 1) * 128],
                        identity=ident_f[0:120, 0:120]), reads=[HST, CONST], writes=[PS[bk]], signal=(cc == 3))
                for cc in range(4):
                    S.op("act", lambda e, cc=cc, bk=bk: e.activation(
                        out=gl[:, cc, 0:248].rearrange("p (s r) -> p s r", r=62)[:, :, 0:30],
                        in_=ps[:, bk, cc * 120:(cc + 1) * 120].rearrange("p (s r) -> p s r", r=30),
                        func=AF.Copy), reads=[PS[bk]], writes=[GL[cc]])

            def gl_data(cc):
                if sample:
                    return gl[:, cc, 0:248].rearrange("p (s r) -> p s r", r=62)[:, :, 30:62]
                return gl[:, cc, 30:30 + T]

            def v3(ap):
                return ap.rearrange("p (s r) -> p s r", r=32) if sample else ap

            NDVE = 10
            accb = [t1[:, 0, :], t1[:, 1, :], t2[:, 0, :], t2[:, 1, :]]
            ACCB = [T1[0], T1[1], T2[0], T2[1]]

            def fm_stage(name, evac):
                slot, W = acquire_stage(name)

                def chunk(n):
                    bk = next_bank()
                    for kc in range(8):
                        S.op("pe", lambda e, kc=kc, n=n, bk=bk, W=W: e.matmul(
                            psf(bk, T), lhsT=W[:, kc, n * 128:(n + 1) * 128], rhs=hT[:, kc, 0:T],
                            start=(kc == 0), stop=(kc == 7)),
                            reads=[RING[slot]] + HTall(kc), writes=[PS[bk]], signal=(kc == 7))
                    evac(n, bk)
                return chunk

            def ev_gate(n, bk):
                S.op("act", lambda e: e.activation(
                    out=RR[:, 16 + n, 0:T], in_=psf(bk, T), func=AF.Sigmoid, bias=col("b_in", 4 + n)),
                    reads=[PS[bk], CONST], writes=[R[16 + n]])
            ch = fm_stage("a_gate", ev_gate)
            for n in range(4):
                ch(n)
            if not wstate.get("setup_done"):
                wstate["setup_done"] = True
                setup_rest_compute()
            elif not wstate.get("setup2_done"):
                wstate["setup2_done"] = True
                setup_sample_consts()

            def ev_lin(n, bk):
                S.op("dve", lambda e: e.scalar_tensor_tensor(
                    out=gl_data(n), in0=v3(psf(bk, T)), scalar=col("b_in", n), in1=v3(RR[:, 16 + n, 0:T]),
                    op0=ALU.add, op1=ALU.mult), reads=[PS[bk], R[16 + n], CONST], writes=[GL[n]])
                if want_state:
                    S.op("dve", lambda e: e.scalar_tensor_tensor(
                        out=glf[:, n, :], in0=ps[:, bk, T - 128:T], scalar=col("b_in", n),
                        in1=RR[:, 16 + n, T - 128:T], op0=ALU.add, op1=ALU.mult),
                        reads=[PS[bk], R[16 + n], CONST], writes=[GLF])
            ch = fm_stage("a_lin", ev_lin)
            for n in range(4):
                ch(n)

            def gl_tap(cc, k):
                if sample:
                    return gl[:, cc, 0:248].rearrange("p (s r) -> p s r", r=62)[:, :, k:k + 32]
                return gl[:, cc, k:k + T]

            def dve_conv(cc):
                if NDVE == 0:
                    return
                tb = cc % 2
                acc = v3(tsg[:, tb, 0:T])
                S.op("dve", lambda e: e.tensor_scalar(out=acc, in0=gl_tap(cc, 0), scalar1=wcol[:, cc, 0:1],
                                                      scalar2=None, op0=ALU.mult),
                     reads=[GL[cc], DIAGB], writes=[TSG[tb]])
                for k in range(1, NDVE):
                    lastk = k == NDVE - 1
                    out = v3(accb[cc][:, 0:T]) if lastk else acc
                    S.op("dve", lambda e, k=k, out=out: e.scalar_tensor_tensor(
                        out=out, in0=gl_tap(cc, k), scalar=wcol[:, cc, k:k + 1], in1=acc,
                        op0=ALU.mult, op1=ALU.add),
                        reads=[GL[cc], DIAGB, TSG[tb]], writes=[ACCB[cc]] if lastk else [TSG[tb]])

            dve_conv(0)

            def ev_u(n, bk):
                S.op("act", lambda e: e.activation(
                    out=usb[:, n, 0:T], in_=psf(bk, T), func=AF.Identity, bias=col("b_in", 8 + n)),
                    reads=[PS[bk], CONST], writes=[U[n]])
            ch = fm_stage("u", ev_u)
            for n in range(4):
                ch(n)

            slot, W = acquire_stage("v")
            vbanks = []
            for i in range(NT):
                bk = next_bank()
                vbanks.append(bk)
                for kc in range(8):
                    S.op("pe", lambda e, kc=kc, i=i, bk=bk, W=W: e.matmul(
                        psf(bk), lhsT=hT[:, kc, i * 128:(i + 1) * 128], rhs=W[:, kc, :],
                        start=(kc == 0), stop=False),
                        reads=[RING[slot], HT[kc, i]], writes=[PS[bk]], signal=False)
                S.op("pe", lambda e, bk=bk: e.matmul(psf(bk), lhsT=ones_row[0:1, :], rhs=bvrow[0:1, :],
                                                    start=False, stop=True),
                     reads=[CONST, BC], writes=[PS[bk]], signal=True)
                S.op("dve", lambda e, i=i, bk=bk: e.bn_stats(out=st6[:, 0, i, :], in_=psf(bk)),
                     reads=[PS[bk]], writes=[ST6[0, i]])
                S.op("dve", lambda e, i=i: e.bn_aggr(out=mv[:, 0, i, :], in_=st6[:, 0, i, :]),
                     reads=[ST6[0, i]], writes=[MV[0]])
            ln_finish(0, NT)
            for i in range(NT):
                bk = vbanks[i]
                vb = i % 2
                S.op("act", lambda e, i=i, bk=bk, vb=vb: e.activation(
                    out=vhat[:, vb, :], in_=psf(bk), func=AF.Identity, scale=lr[:, 0, i:i + 1],
                    bias=lnb[:, 0, i:i + 1]), reads=[PS[bk], LR[0], LNB[0]], writes=[VH[vb]])
                if sample:
                    S.op("dve", lambda e, vb=vb: e.tensor_tensor(out=vnf[:], in0=vhat[:, vb, :], in1=gvb[:],
                                                                op=ALU.mult),
                         reads=[VH[vb], BC1], writes=[VNF])
                    S.op("dve", lambda e, i=i: e.tensor_copy(out=vg[:, i, :], in_=vnf[:]),
                         reads=[VNF], writes=[VG[i]])
                    S.op("dve", lambda e: e.tensor_tensor(out=vnf[:], in0=vnf[:], in1=bvb[:], op=ALU.add),
                         reads=[VNF, BC2], writes=[VNF])
                    S.op("pool", lambda e: e.dma_start(out=svo[:, :], in_=vnf[:]), reads=[VNF], dma="misc")
                else:
                    S.op("dve", lambda e, i=i, vb=vb: e.tensor_tensor(
                        out=vg[:, i, :], in0=vhat[:, vb, :], in1=gvb[:], op=ALU.mult),
                        reads=[VH[vb], BC1], writes=[VG[i]])

            dve_conv(1)
            dve_conv(2)
            dve_conv(3)

            for hh in range(2):
                def ev_ga(n, bk, hh=hh):
                    c = hh * 4 + n
                    S.op("act", lambda e: e.activation(
                        out=RR[:, c, 0:T], in_=psf(bk, T), func=AF.Sigmoid, bias=col("b_in", 16 + c)),
                        reads=[PS[bk], CONST], writes=[R[c]])
                ch = fm_stage("ga%d" % hh, ev_ga)
                for n in range(4):
                    ch(n)

            wsx = wsTb if sample else wsT
            cstx = csts if sample else cst
            for i in range(NT):
                bk = next_bank()
                for g in range(4):
                    S.op("pe", lambda e, g=g, i=i, bk=bk: e.matmul(
                        ps[:, bk, g * 128:(g + 1) * 128], lhsT=vg[:, i, g * 128:(g + 1) * 128],
                        rhs=wsx[:, g, :], start=True, stop=True, skip_group_check=True),
                        reads=[VG[i], WSB], writes=[PS[bk]], signal=(g == 3))
                tb = i % 2
                S.op("dve", lambda e, bk=bk, tb=tb: e.tensor_tensor(
                    out=tsg[:, tb, :], in0=psf(bk), in1=cstx[:].rearrange("p g t -> p (g t)"), op=ALU.add),
                    reads=[PS[bk], WSB], writes=[TSG[tb]])
                S.op("dve", lambda e, i=i, tb=tb: e.tensor_tensor(
                    out=ybin[:, :, i * 128:(i + 1) * 128],
                    in0=tsg[:, tb, :].rearrange("p (g t) -> p g t", t=128),
                    in1=usb[:, :, i * 128:(i + 1) * 128], op=ALU.mult),
                    reads=[TSG[tb]] + [U[n] for n in range(4)], writes=[YB[i]])

            cbanks = [next_bank() for _ in range(4)]
            for cc in range(4):
                bk = cbanks[cc]
                for k in range(NDVE, KW):
                    if sample:
                        out = psf(bk, 128).rearrange("p (s r) -> p s r", r=32)
                    else:
                        out = psf(bk, T)
                    fin = (NDVE == 0 and k == KW - 1)
                    S.op("pe", lambda e, k=k, cc=cc, out=out, fin=fin: e.matmul(
                        out, lhsT=diag[:, k, cc, :], rhs=gl_tap(cc, k), start=(k == NDVE), stop=fin),
                        reads=[GL[cc], DIAGB], writes=[PS[bk]], signal=fin)
            def ev_gb_of(hh):
                def ev_gb(n, bk):
                    c = hh * 4 + n
                    S.op("act", lambda e: e.activation(
                        out=RR[:, 8 + c, 0:T], in_=psf(bk, T), func=AF.Sigmoid, bias=col("b_in", 24 + c)),
                        reads=[PS[bk], CONST], writes=[R[8 + c]])
                return ev_gb
            ch0 = fm_stage("gb0", ev_gb_of(0))
            if NDVE > 0:
                ch0(0)
            for cc in range(4):
                bk = cbanks[cc]
                if NDVE > 0:
                    S.op("pe", lambda e, cc=cc, bk=bk: e.matmul(
                        psf(bk, T), lhsT=ident_b[:], rhs=accb[cc][:, 0:T], start=False, stop=True),
                        reads=[ACCB[cc], CONST], writes=[PS[bk]], signal=True)
                S.op("act", lambda e, cc=cc, bk=bk: e.activation(
                    out=ycsb[:, cc, 0:T], in_=psf(bk, T), func=AF.Identity, bias=col("b_dw", cc)),
                    reads=[PS[bk], CONST], writes=[YC[cc]])
            if not sample and not last:
                for cc in range(4):
                    S.op("act", lambda e, cc=cc: e.activation(out=gl[:, cc, 0:30], in_=gl[:, cc, T:T + 30],
                                                             func=AF.Copy), reads=[GL[cc]], writes=[GL[cc]])
            if want_state:
                bk = next_bank()
                for cc in range(4):
                    S.op("pe", lambda e, cc=cc, bk=bk: e.transpose(
                        out=ps[:, bk, cc * 128:(cc + 1) * 128], in_=glf[:, cc, :], identity=ident_f[:]),
                        reads=[GLF, CONST], writes=[PS[bk]], signal=(cc == 3))
                S.op("act", lambda e, bk=bk: e.activation(out=stf[:], in_=psf(bk), func=AF.Copy),
                     reads=[PS[bk]], writes=[STF])
                if sample:
                    for stq in range(4):
                        S.op("pool", lambda e, stq=stq: e.dma_start(
                            out=scs[30 * stq:30 * stq + 30, :], in_=stf[32 * stq + 2:32 * stq + 32, :]),
                            reads=[STF], dma="misc")
                else:
                    S.op("pool", lambda e: e.dma_start(out=scp[:, :], in_=stf[98:128, :]),
                         reads=[STF], dma="misc")

            if NDVE == 0:
                ch0(0)
            ch0(1)
            tbanks = []
            for i in range(NT):
                bk = next_bank()
                tbanks.append(bk)
                for cc in range(4):
                    S.op("pe", lambda e, cc=cc, i=i, bk=bk: e.transpose(
                        out=psb(bk)[:, cc * 128:(cc + 1) * 128], in_=ycsb[:, cc, i * 128:(i + 1) * 128],
                        identity=ident_b[:]), reads=[YC[cc], CONST], writes=[PS[bk]], signal=(cc == 3))
                S.op("dve", lambda e, i=i, bk=bk: e.bn_stats(out=st6[:, 1, i, :], in_=psb(bk)[:, 0:512]),
                     reads=[PS[bk]], writes=[ST6[1, i]])
                S.op("dve", lambda e, i=i: e.bn_aggr(out=mv[:, 1, i, :], in_=st6[:, 1, i, :]),
                     reads=[ST6[1, i]], writes=[MV[1]])
            ln_finish(1, NT)
            for i in range(NT):
                bk = tbanks[i]
                S.op("act", lambda e, i=i, bk=bk: e.activation(
                    out=yctm[:, i, :], in_=psb(bk)[:, 0:512], func=AF.Identity, scale=lr[:, 1, i:i + 1],
                    bias=lnb[:, 1, i:i + 1]), reads=[PS[bk], LR[1], LNB[1]], writes=[YT[i]])
            for n in range(2, 4):
                ch0(n)
            ch1 = fm_stage("gb1", ev_gb_of(1))
            ch1(0)
            ch1(1)
            for cc in range(4):
                bk = next_bank()
                for i in range(NT):
                    S.op("pe", lambda e, cc=cc, i=i, bk=bk: e.transpose(
                        out=psb(bk)[:, i * 128:(i + 1) * 128], in_=yctm[:, i, cc * 128:(cc + 1) * 128],
                        identity=ident_b[:]), reads=[YT[i], CONST], writes=[PS[bk]], signal=(i == NT - 1))
                S.op("act", lambda e, cc=cc, bk=bk: e.activation(
                    out=sa[:, cc, 0:T], in_=psb(bk)[:, 0:T], func=AF.Silu, scale=col("ln_a_g", cc),
                    bias=col("ln_a_b", cc)), reads=[PS[bk], CONST], writes=[SA[cc]])
            ch1(2)
            ch1(3)
            slot_a, Wa = acquire_stage("a_out")
            slot_b, Wb = acquire_stage("b_out", hold=1)
            for c in range(8):
                bka = next_bank()
                for kc in range(4):
                    S.op("pe", lambda e, kc=kc, c=c, bk=bka: e.matmul(
                        psf(bk, T), lhsT=Wa[:, kc, c * 128:(c + 1) * 128], rhs=sa[:, kc, 0:T],
                        start=(kc == 0), stop=(kc == 3)),
                        reads=[RING[slot_a], SA[kc]], writes=[PS[bka]], signal=(kc == 3))
                bkb = next_bank()
                for kc in range(4):
                    S.op("pe", lambda e, kc=kc, c=c, bk=bkb: e.matmul(
                        psf(bk, T), lhsT=Wb[:, kc, c * 128:(c + 1) * 128], rhs=ybin[:, kc, 0:T],
                        start=(kc == 0), stop=(kc == 3)),
                        reads=[RING[slot_b]] + [YB[i] for i in range(NT)], writes=[PS[bkb]], signal=(kc == 3))
                tb = c % 2
                S.op("dve", lambda e, c=c, bk=bka, tb=tb: e.scalar_tensor_tensor(
                    out=t1[:, tb, 0:T], in0=psf(bk, T), scalar=col("b_a_out", c), in1=RR[:, c, 0:T],
                    op0=ALU.add, op1=ALU.mult), reads=[PS[bka], R[c], CONST], writes=[T1[tb]])
                S.op("dve", lambda e, c=c, bk=bkb, tb=tb: e.tensor_tensor(
                    out=t2[:, tb, 0:T], in0=psf(bk, T), in1=RR[:, 8 + c, 0:T], op=ALU.mult),
                    reads=[PS[bkb], R[8 + c]], writes=[T2[tb]])
                S.op("dve", lambda e, c=c, tb=tb: e.tensor_tensor(
                    out=hT[:, c, 0:T], in0=t1[:, tb, 0:T], in1=t2[:, tb, 0:T], op=ALU.add),
                    reads=[T1[tb], T2[tb]], writes=HTall(c))
            if debug and first:
                S.op("pool", lambda e: e.dma_start(out=dbg_m, in_=hT[:]),
                     reads=[HT[kc, i] for kc in range(8) for i in range(4)], dma="misc")
            slot0, W0 = acquire_stage("wo0")
            slot1, W1 = acquire_stage("wo1", hold=1)
            for i in range(NT):
                for h in range(2):
                    slot, W = (slot0, W0) if h == 0 else (slot1, W1)
                    bk = next_bank()
                    for kc in range(8):
                        S.op("pe", lambda e, kc=kc, i=i, bk=bk, W=W: e.matmul(
                            psf(bk), lhsT=hT[:, kc, i * 128:(i + 1) * 128], rhs=W[:, kc, :],
                            start=(kc == 0), stop=(kc == 7)),
                            reads=[RING[slot], HT[kc, i]], writes=[PS[bk]], signal=(kc == 7))
                    S.op("dve", lambda e, i=i, h=h, bk=bk: e.tensor_tensor(
                        out=xa[:, xb, i, h * 512:(h + 1) * 512], in0=psf(bk),
                        in1=xa[:, xb, i, h * 512:(h + 1) * 512], op=ALU.add),
                        reads=[PS[bk]], writes=[X[xb, i, h]])
                    if NT > 1 and ((i == 2 and h == 1) or (i == 3 and h == 0)):
                        rms_T(i - 2, "act")
                rms_pre(xb, i, 1)
            if NT == 1:
                rms_T(0)
            for s_ in range(11):
                slot, W = acquire_stage("fi%d" % s_)
                grp = []
                for jj in range(2):
                    j = 2 * s_ + jj
                    grp.append((j, next_bank(), jj * 128))
                    grp.append((j, next_bank(), 256 + jj * 128))
                if s_ == 0 and NT > 1:
                    spans = [(128 * t_, 128 * (t_ + 1), [t_]) for t_ in range(NT)]
                else:
                    spans = [(0, T, list(range(NT)))]
                for si, (c0, c1, tiles) in enumerate(spans):
                    for gi, (j, bk, wc) in enumerate(grp):
                        if s_ == 0 and NT > 1 and si == NT - 3 and gi == 0:
                            rms_T(NT - 2, "dve")
                        if s_ == 0 and NT > 1 and si == NT - 2 and gi == 0:
                            rms_T(NT - 1, "act")
                        for kc in range(8):
                            S.op("pe", lambda e, kc=kc, bk=bk, wc=wc, W=W, c0=c0, c1=c1: e.matmul(
                                ps[:, bk, c0:c1], lhsT=W[:, kc, wc:wc + 128], rhs=hT[:, kc, c0:c1],
                                start=(kc == 0), stop=(kc == 7), skip_group_check=True),
                                reads=[RING[slot]] + [HT[kc, i] for i in tiles], writes=[PS[bk]],
                                signal=(kc == 7))
                for jj in range(2):
                    j, bkg, _ = grp[2 * jj]
                    _, bku, _ = grp[2 * jj + 1]
                    tb = j % 2
                    S.op("act", lambda e, bk=bkg, tb=tb: e.activation(out=sgt[:, tb, 0:T], in_=psf(bk, T),
                                                                     func=AF.Silu),
                         reads=[PS[bkg]], writes=[SGT[tb]])
                    S.op("dve", lambda e, j=j, bk=bku, tb=tb: e.tensor_tensor(
                        out=RR[:, j, 0:T], in0=psf(bk, T), in1=sgt[:, tb, 0:T], op=ALU.mult),
                        reads=[PS[bku], SGT[tb]], writes=[R[j]])
            if hook_pre is not None:
                hook_pre()
            for h in range(2):
                if h == 1 and hook_T is not None:
                    hook_T()
                obanks = [next_bank() for _ in range(NT)]
                for q in range(3):
                    slot, W = acquire_stage("fo%d%d" % (h, q))
                    nk = 8 if q < 2 else 6
                    for i in range(NT):
                        bk = obanks[i]
                        for kk in range(nk):
                            j = 8 * q + kk
                            S.op("pe", lambda e, kk=kk, j=j, i=i, bk=bk, W=W: e.matmul(
                                psf(bk), lhsT=RR[:, j, i * 128:(i + 1) * 128], rhs=W[:, kk, :],
                                start=(j == 0), stop=(j == NJ - 1)),
                                reads=[RING[slot], R[j]], writes=[PS[bk]], signal=(kk == nk - 1))
                if h == 1 and hook_end is not None:
                    hook_end()
                for i in range(NT):
                    bk = obanks[i]
                    S.op("dve", lambda e, i=i, h=h, bk=bk: e.tensor_tensor(
                        out=xa[:, xb, i, h * 512:(h + 1) * 512], in0=psf(bk),
                        in1=xa[:, xb, i, h * 512:(h + 1) * 512], op=ALU.add),
                        reads=[PS[bk]], writes=[X[xb, i, h]])
            if debug and first:
                S.op("pool", lambda e: e.dma_start(out=dbg_sa, in_=sa[:]), reads=[SA[c] for c in range(4)], dma="misc")
                S.op("pool", lambda e: e.dma_start(out=dbg_yb, in_=ybin[:]), reads=[YB[c] for c in range(4)], dma="misc")
                S.op("pool", lambda e: e.dma_start(out=dbg_h2, in_=hT[:]),
                     reads=[HT[kc, i] for kc in range(8) for i in range(4)], dma="misc")
                S.op("pool", lambda e: e.dma_start(out=dbg_hid, in_=RR[:]), reads=[R[j] for j in range(NJ)], dma="misc")
            for i in range(NT):
                S.op("act", lambda e, i=i: e.activation(out=junk[:], in_=xa[:, xb, i, :], func=AF.Square,
                                                       accum_out=ss[:, 2, i:i + 1]),
                     reads=xall(xb, i), writes=[SSb[2, i]])
                S.op("act", lambda e, i=i: e.activation(out=sd[:, 2, i:i + 1], in_=ss[:, 2, i:i + 1],
                                                       func=AF.Sqrt, scale=1.0 / D, bias=epst[:]),
                     reads=[SSb[2, i], CONST], writes=[SDb[2, i]])
                S.op("dve", lambda e, i=i: e.reciprocal(out=rs_[:, 2, i:i + 1], in_=sd[:, 2, i:i + 1]),
                     reads=[SDb[2, i]], writes=[RSb[2, i]])
                S.op("dve", lambda e, i=i: e.scalar_tensor_tensor(
                    out=xa[:, xb, i, :], in0=xa[:, xb, i, :], scalar=rs_[:, 2, i:i + 1], in1=gfb[:],
                    op0=ALU.mult, op1=ALU.mult), reads=[RSb[2, i], BC3], writes=xall(xb, i))
                if sample:
                    dst = ys[:, :]
                else:
                    dst = yp[p * TP + i * 128:p * TP + (i + 1) * 128, :]
                S.op("pool", lambda e, i=i, dst=dst: e.dma_start(out=dst, in_=xa[:, xb, i, :]),
                     reads=xall(xb, i), dma="yout%d" % xb)

        def ln_finish(which, NT):
            S.op("act", lambda e: e.activation(out=lsd[:, which, 0:NT], in_=mv[:, which, 0:NT, 1],
                                               func=AF.Sqrt, scale=1.0, bias=epst[:]),
                 reads=[MV[which], CONST], writes=[LSD[which]])
            S.op("dve", lambda e: e.reciprocal(out=lr[:, which, 0:NT], in_=lsd[:, which, 0:NT]),
                 reads=[LSD[which]], writes=[LR[which]])
            S.op("dve", lambda e: e.scalar_tensor_tensor(
                out=lnb[:, which, 0:NT], in0=mv[:, which, 0:NT, 0], scalar=-1.0, in1=lr[:, which, 0:NT],
                op0=ALU.mult, op1=ALU.mult), reads=[MV[which], LR[which]], writes=[LNB[which]])

        def load_x(kind, p, xb):
            wr = [X[xb, i, h] for i in range(4) for h in range(2)]
            if kind == "sample":
                S.op("pool", lambda e: e.dma_start(out=xa[:, xb, 0, :], in_=xs[:, :]),
                     writes=[X[xb, 0, 0], X[xb, 0, 1]], dma="xin%d" % xb)
            else:
                src = xp[p * TP:(p + 1) * TP, :].rearrange("(i q) d -> q i d", q=128)
                S.op("pool", lambda e: e.dma_start(out=xa[:, xb], in_=src), writes=wr, dma="xin%d" % xb)

        passes = [("prompt", p) for p in range(n_prompt_pass)]
        if do_sample:
            passes.append(("sample", 0))
        for _ in passes:
            load_plan.extend(range(NSTAGE))

        def nt_of(kind):
            return 1 if kind == "sample" else 4

        setup_early()
        load_x(passes[0][0], passes[0][1], 0)
        for i in range(nt_of(passes[0][0])):
            rms_pre(0, i, 0)
        for i in range(nt_of(passes[0][0])):
            rms_T(i)
        setup_rest_dma()

        for n, (kind, p) in enumerate(passes):
            xb = n % 2
            hook_pre = hook_T = hook_end = None
            if n + 1 < len(passes):
                nkind, np_ = passes[n + 1]
                nxb = (n + 1) % 2
                pre = lambda nkind=nkind, nxb=nxb: [rms_pre(nxb, i, 0) for i in range(nt_of(nkind))]
                tr = lambda nkind=nkind: [rms_T(i) for i in range(nt_of(nkind))]
                if n == 0:
                    hook_T = lambda nkind=nkind, np_=np_, nxb=nxb, pre=pre: (load_x(nkind, np_, nxb), pre())
                    hook_end = tr
                else:
                    load_x(nkind, np_, nxb)
                    hook_pre = pre
                    hook_T = tr
            run_pass(kind, p, xb, hook_pre, hook_T, hook_end)

        S.wait_all("pool", ["yout0", "yout1", "misc"])
        S.wait_all("sp", ["yout0", "yout1", "misc", "wout0", "wout1", "wout2", "wout3"])

        with nc.Block() as block:
            S.emit(block)
    _CACHE["sched"] = S
    return nc


def _get_program():
    if "nc" not in _CACHE:
        _CACHE["nc"] = build_program()
    return _CACHE["nc"]


def make_in_maps(inputs):
    f = lambda a: np.ascontiguousarray(np.asarray(a, dtype=np.float32))
    shared = {
        "g_mix": f(inputs["g_mix"][0]), "w_in": f(inputs["w_in"][0]), "b_in": f(inputs["b_in"][0]),
        "w_dw": f(inputs["w_dw"][0]), "b_dw": f(inputs["b_dw"][0]), "ln_a_g": f(inputs["ln_a_g"][0]),
        "ln_a_b": f(inputs["ln_a_b"][0]), "w_a_out": f(inputs["w_a_out"][0]),
        "b_a_out": f(inputs["b_a_out"][0]), "ln_v_g": f(inputs["ln_v_g"][0]),
        "ln_v_b": f(inputs["ln_v_b"][0]), "w_s": f(inputs["w_s"][0]), "b_s": f(inputs["b_s"][0]),
        "w_b_out": f(inputs["w_b_out"][0]), "w_o": f(inputs["w_o"][0]), "g_ffn": f(inputs["g_ffn"][0]),
        "w_ffn_in": f(inputs["w_ffn_in"][0]), "w_ffn_out": f(inputs["w_ffn_out"][0]),
        "g_final": f(inputs["g_final"]),
    }
    x_prompt = np.asarray(inputs["x_prompt"], dtype=np.float32)
    x_sample = np.asarray(inputs["x_sample"], dtype=np.float32)
    cache_conv = np.asarray(inputs["cache_conv"], dtype=np.float32)
    maps = []
    for c in range(NCORE):
        m = dict(shared)
        m["xp"] = np.ascontiguousarray(x_prompt[c])
        m["xs"] = np.ascontiguousarray(x_sample[4 * c:4 * c + 4].reshape(128, D))
        m["cch"] = np.ascontiguousarray(cache_conv[0, 4 * c:4 * c + 4].reshape(120, CC))
        maps.append(m)
    return maps


def kernel(**inputs):
    nc = _get_program()
    maps = make_in_maps(inputs)
    res = run_bass_kernel_spmd(nc, maps, core_ids=list(range(NCORE)))
    r = res.results
    y_prompt = np.stack([np.asarray(r[c]["yp"], dtype=np.float32) for c in range(NCORE)], axis=0)
    y_sample = np.concatenate([np.asarray(r[c]["ys"], dtype=np.float32).reshape(4, 32, D)
                               for c in range(NCORE)], axis=0)
    scp_ = np.stack([np.asarray(r[c]["scp"], dtype=np.float32) for c in range(NCORE)], axis=0)[None]
    scs_ = np.concatenate([np.asarray(r[c]["scs"], dtype=np.float32).reshape(4, 30, CC)
                           for c in range(NCORE)], axis=0)[None]
    sv_ = np.concatenate([np.asarray(r[c]["svo"], dtype=np.float32).reshape(4, 32, CS)
                          for c in range(NCORE)], axis=0)[None]
    return (y_prompt, y_sample, scp_, scs_, sv_)
```
